# Optimizing a Trainium2 kernel written in Bass

```python
import math
import jax, jax.numpy as jnp
from jax import lax
import numpy as np

D_MODEL = 1024
BATCH = 8
SEQ = 4096
DEPTH = 1

CTX_LEN = 256
GRID_W = 64
D_MIX = D_MODEL
RET_WIDTH = D_MIX // 2
RET_HEADS = 4
RET_DK = RET_WIDTH // RET_HEADS
RET_DV = RET_DK
RET_CHUNK = 128
LRU_WIDTH = D_MIX - RET_WIDTH
LRU_BLOCKS = 8
LRU_BLOCK = LRU_WIDTH // LRU_BLOCKS
LRU_C = 8.0
CONV_W = 4
CONV_PAD_LO = 2
D_FF = 2816
N_SUB = 3
MACARON = 0.5
ALPHA = (2.0 * DEPTH) ** 0.25
BETA = (8.0 * DEPTH) ** -0.25
ROPE_BASE = 10000.0
LN_EPS = 1e-5

K_OFF = 0
V_OFF = K_OFF + RET_WIDTH
X_OFF = V_OFF + RET_WIDTH
CTX_COLS = X_OFF + LRU_WIDTH
Q_OFF = CTX_COLS
G_OFF = Q_OFF + RET_WIDTH
GATE_OFF = G_OFF + RET_WIDTH
IN_COLS = GATE_OFF + LRU_WIDTH

kernel_name = "hymba_retention_rglru_macaron_deepnorm_dit"


def layer_norm(h, g, b):
    hf = h.astype(jnp.float32)
    mu = jnp.mean(hf, -1, keepdims=True)
    var = jnp.mean(jnp.square(hf - mu), -1, keepdims=True)
    return ((hf - mu) * lax.rsqrt(var + LN_EPS) * g + b).astype(h.dtype)


def modulate(h, shift, scale):
    return h * (1 + scale) + shift


def swiglu(u, wg, wu, wd):
    return (jax.nn.silu(u @ wg) * (u @ wu)) @ wd


def ffn_sublayer(h, shift, scale, gate, wg, wu, wd, g, b):
    f = swiglu(modulate(h, shift, scale), wg, wu, wd)
    return layer_norm(ALPHA * h + MACARON * gate * f, g, b)


def _heads(t):
    B, T, _ = t.shape
    return t.reshape(B, T, RET_HEADS, -1).transpose(0, 2, 1, 3)


def _rotate(t, ang):
    cos = jnp.cos(ang).astype(t.dtype)
    sin = jnp.sin(ang).astype(t.dtype)
    t1, t2 = jnp.split(t, 2, axis=-1)
    return jnp.concatenate([t1 * cos - t2 * sin, t1 * sin + t2 * cos], -1)


def rope_2d(t, rows, cols):
    n = t.shape[-1] // 4
    inv = ROPE_BASE ** (-jnp.arange(n, dtype=jnp.float32) / n)
    a_r = rows.astype(jnp.float32)[:, None] * inv
    a_c = cols.astype(jnp.float32)[:, None] * inv
    tr, tc = jnp.split(t, 2, axis=-1)
    return jnp.concatenate([_rotate(tr, a_r), _rotate(tc, a_c)], -1)


def retention_chunkwise(q, k, v, log_g, s0, strict):
    B, H, T, dk = q.shape
    dv = v.shape[-1]
    C = RET_CHUNK
    N = T // C
    dt = q.dtype
    qc = q.reshape(B, H, N, C, dk)
    kc = k.reshape(B, H, N, C, dk)
    vc = v.reshape(B, H, N, C, dv)
    idx = jnp.arange(C)
    diff = idx[:, None] - idx[None, :]
    mask = (diff > 0) if strict else (diff >= 0)
    decay = jnp.where(mask[None], jnp.exp(log_g[:, None, None] * jnp.maximum(diff, 0)[None]), 0.0).astype(dt)
    scores = jnp.einsum('bhnid,bhnjd->bhnij', qc, kc) * decay[None, :, None]
    o = jnp.einsum('bhnij,bhnjv->bhniv', scores, vc)
    k_dec = kc * jnp.exp(log_g[:, None] * (C - 1 - idx)).astype(dt)[None, :, None, :, None]
    chunk_kv = jnp.einsum('bhnjd,bhnjv->bhndv', k_dec, vc)
    g_c = jnp.exp(log_g * C).astype(dt)[None, :, None, None]

    def step(s, kv):
        return g_c * s + kv, s

    _, s_prev = lax.scan(step, s0.astype(dt), jnp.moveaxis(chunk_kv, 2, 0))
    s_prev = jnp.moveaxis(s_prev, 0, 2)
    q_dec = qc * jnp.exp(log_g[:, None] * (idx + 1)).astype(dt)[None, :, None, :, None]
    o = o + jnp.einsum('bhnid,bhndv->bhniv', q_dec, s_prev)
    return o.reshape(B, H, T, dv)


def retention_final_state(k, v, log_g, reverse):
    T = k.shape[2]
    pos = jnp.arange(T)
    dist = pos if reverse else (T - 1 - pos)
    w = jnp.exp(log_g[:, None] * dist).astype(k.dtype)
    return jnp.einsum('bhtd,bhtv->bhdv', k * w[None, :, :, None], v)


def bidir_retention(q, k, v, log_g, s_f, s_b):
    o_f = retention_chunkwise(q, k, v, log_g[0], s_f, False)
    o_b = retention_chunkwise(jnp.flip(q, 2), jnp.flip(k, 2), jnp.flip(v, 2), log_g[1], s_b, True)
    return o_f + jnp.flip(o_b, 2)


def depthwise_conv(t, w, b):
    out = lax.conv_general_dilated(
        t, w[:, None, :], window_strides=(1,),
        padding=[(CONV_PAD_LO, CONV_W - 1 - CONV_PAD_LO)],
        dimension_numbers=('NWC', 'WIO', 'NWC'),
        feature_group_count=t.shape[-1])
    return out + b


def _lin_combine(left, right):
    return (left[0] * right[0], right[0] * left[1] + right[1])


def rglru_direction(xc, wa, ba, wi, bi, lam, h0):
    B, T, W = xc.shape
    xb = xc.reshape(B, T, LRU_BLOCKS, LRU_BLOCK)
    r = jax.nn.sigmoid(jnp.einsum('btnc,ncd->btnd', xb, wa).reshape(B, T, W) + ba).astype(jnp.float32)
    i = jax.nn.sigmoid(jnp.einsum('btnc,ncd->btnd', xb, wi).reshape(B, T, W) + bi).astype(jnp.float32)
    log_a = -LRU_C * r * jax.nn.softplus(-lam.astype(jnp.float32))
    a = jnp.exp(log_a)
    bx = jnp.sqrt(-jnp.expm1(2.0 * log_a)) * i * xc.astype(jnp.float32)
    a_cum, b_cum = lax.associative_scan(_lin_combine, (a, bx), axis=1)
    return a_cum * h0[:, None, :] + b_cum


def rglru_bidir(xc, wa, ba, wi, bi, lam, h0_f, h0_b):
    h_f = rglru_direction(xc, wa[0], ba[0], wi[0], bi[0], lam[0], h0_f)
    h_b = jnp.flip(rglru_direction(jnp.flip(xc, 1), wa[1], ba[1], wi[1], bi[1], lam[1], h0_b), 1)
    return h_f, h_b


def mixer_output(proj, o_ret, h_sum, ret_ng, ret_nb, w_out):
    B, H, T, dv = o_ret.shape
    of = o_ret.astype(jnp.float32)
    mu = jnp.mean(of, -1, keepdims=True)
    var = jnp.mean(jnp.square(of - mu), -1, keepdims=True)
    on = ((of - mu) * lax.rsqrt(var + LN_EPS)).transpose(0, 2, 1, 3).reshape(B, T, H * dv)
    ret_out = (on * ret_ng + ret_nb).astype(proj.dtype) * jax.nn.silu(proj[..., G_OFF:G_OFF + RET_WIDTH])
    lru_out = h_sum.astype(proj.dtype) * jax.nn.gelu(proj[..., GATE_OFF:GATE_OFF + LRU_WIDTH])
    return jnp.concatenate([ret_out, lru_out], -1) @ w_out


def hybrid_mixer(u_lat, u_ctx, rows, cols, need_ctx_out, w_in, w_out, ret_logit, ret_ng, ret_nb,
                 conv_w, conv_b, wa, ba, wi, bi, lam):
    log_g = jax.nn.log_sigmoid(ret_logit.astype(jnp.float32))
    k_scale = RET_DK ** -0.5
    pc = u_ctx @ (w_in if need_ctx_out else w_in[:, :CTX_COLS])
    kc = _heads(pc[..., K_OFF:K_OFF + RET_WIDTH]) * k_scale
    vc = _heads(pc[..., V_OFF:V_OFF + RET_WIDTH])
    s_cf = retention_final_state(kc, vc, log_g[0], False)
    s_cb = retention_final_state(kc, vc, log_g[1], True)
    xcc = depthwise_conv(pc[..., X_OFF:X_OFF + LRU_WIDTH], conv_w, conv_b)
    z = jnp.zeros((xcc.shape[0], LRU_WIDTH), jnp.float32)
    hcf, hcb = rglru_bidir(xcc, wa, ba, wi, bi, lam, z, z)
    pl = u_lat @ w_in
    q = rope_2d(_heads(pl[..., Q_OFF:Q_OFF + RET_WIDTH]), rows, cols)
    k = rope_2d(_heads(pl[..., K_OFF:K_OFF + RET_WIDTH]), rows, cols) * k_scale
    v = _heads(pl[..., V_OFF:V_OFF + RET_WIDTH])
    o = bidir_retention(q, k, v, log_g, s_cf, s_cb)
    xl = depthwise_conv(pl[..., X_OFF:X_OFF + LRU_WIDTH], conv_w, conv_b)
    hf, hb = rglru_bidir(xl, wa, ba, wi, bi, lam, hcf[:, -1], hcb[:, 0])
    y_lat = mixer_output(pl, o, hf + hb, ret_ng, ret_nb, w_out)
    y_ctx = None
    if need_ctx_out:
        qc = _heads(pc[..., Q_OFF:Q_OFF + RET_WIDTH])
        zs = jnp.zeros_like(s_cf)
        oc = bidir_retention(qc, kc, vc, log_g, zs, zs)
        y_ctx = mixer_output(pc, oc, hcf + hcb, ret_ng, ret_nb, w_out)
    return y_lat, y_ctx


def setup_inputs(seed: int = 0) -> dict:
    key = jax.random.key(seed)
    ks = jax.random.split(key, 32)
    f32 = jnp.float32
    L = DEPTH

    def nrm(k, shape, s):
        return jax.random.normal(k, shape, f32) * s

    gamma0 = 1.0 - 2.0 ** (-5.0 - jnp.arange(RET_HEADS, dtype=f32))
    logit0 = jnp.log(gamma0) - jnp.log1p(-gamma0)
    u = jax.random.uniform(ks[25], (L, 2, LRU_WIDTH), f32, 0.9, 0.999)
    a0 = u ** (1.0 / LRU_C)
    return {
        "x": nrm(ks[0], (BATCH, SEQ, D_MODEL), 1.0),
        "c": nrm(ks[1], (BATCH, D_MODEL), 1.0),
        "ctx": nrm(ks[2], (BATCH, CTX_LEN, D_MODEL), 1.0),
        "c_ctx": nrm(ks[3], (D_MODEL,), 1.0),
        "w_ada": nrm(ks[4], (L, D_MODEL, 3 * N_SUB * D_MODEL), D_MODEL ** -0.5),
        "b_ada": nrm(ks[5], (L, 3 * N_SUB * D_MODEL), 0.02),
        "ffn1_w_gate": nrm(ks[6], (L, D_MODEL, D_FF), D_MODEL ** -0.5),
        "ffn1_w_up": nrm(ks[7], (L, D_MODEL, D_FF), D_MODEL ** -0.5),
        "ffn1_w_down": nrm(ks[8], (L, D_FF, D_MODEL), BETA * D_FF ** -0.5),
        "ffn2_w_gate": nrm(ks[9], (L, D_MODEL, D_FF), D_MODEL ** -0.5),
        "ffn2_w_up": nrm(ks[10], (L, D_MODEL, D_FF), D_MODEL ** -0.5),
        "ffn2_w_down": nrm(ks[11], (L, D_FF, D_MODEL), BETA * D_FF ** -0.5),
        "w_in": nrm(ks[12], (L, D_MODEL, IN_COLS), D_MODEL ** -0.5),
        "w_out": nrm(ks[13], (L, D_MIX, D_MODEL), BETA * D_MIX ** -0.5),
        "ret_decay_logit": logit0 + nrm(ks[14], (L, 2, RET_HEADS), 0.1),
        "ret_norm_g": 1.0 + nrm(ks[15], (L, RET_WIDTH), 0.02),
        "ret_norm_b": nrm(ks[16], (L, RET_WIDTH), 0.02),
        "lru_conv_w": nrm(ks[17], (L, CONV_W, LRU_WIDTH), CONV_W ** -0.5),
        "lru_conv_b": nrm(ks[18], (L, LRU_WIDTH), 0.02),
        "lru_w_a": nrm(ks[19], (L, 2, LRU_BLOCKS, LRU_BLOCK, LRU_BLOCK), LRU_BLOCK ** -0.5),
        "lru_b_a": nrm(ks[20], (L, 2, LRU_WIDTH), 0.02),
        "lru_w_i": nrm(ks[21], (L, 2, LRU_BLOCKS, LRU_BLOCK, LRU_BLOCK), LRU_BLOCK ** -0.5),
        "lru_b_i": nrm(ks[22], (L, 2, LRU_WIDTH), 0.02),
        "lru_lambda": jnp.log(a0) - jnp.log1p(-a0),
        "ln_g": 1.0 + nrm(ks[23], (L, N_SUB, D_MODEL), 0.02),
        "ln_b": nrm(ks[24], (L, N_SUB, D_MODEL), 0.02),
    }


def reference(x, c, ctx, c_ctx, w_ada, b_ada, ffn1_w_gate, ffn1_w_up, ffn1_w_down,
              ffn2_w_gate, ffn2_w_up, ffn2_w_down, w_in, w_out, ret_decay_logit,
              ret_norm_g, ret_norm_b, lru_conv_w, lru_conv_b, lru_w_a, lru_b_a,
              lru_w_i, lru_b_i, lru_lambda, ln_g, ln_b):
    T = x.shape[1]
    ROWS = T // GRID_W
    rows = jnp.repeat(jnp.arange(ROWS), GRID_W)
    cols = jnp.tile(jnp.arange(GRID_W), ROWS)
    h_ctx = ctx
    for l in range(DEPTH):
        last = l == DEPTH - 1
        m_lat = jnp.split((jax.nn.silu(c) @ w_ada[l] + b_ada[l])[:, None, :], 3 * N_SUB, -1)
        m_ctx = jnp.split((jax.nn.silu(c_ctx) @ w_ada[l] + b_ada[l])[None, None, :], 3 * N_SUB, -1)
        x = ffn_sublayer(x, m_lat[0], m_lat[1], m_lat[2], ffn1_w_gate[l], ffn1_w_up[l], ffn1_w_down[l], ln_g[l, 0], ln_b[l, 0])
        h_ctx = ffn_sublayer(h_ctx, m_ctx[0], m_ctx[1], m_ctx[2], ffn1_w_gate[l], ffn1_w_up[l], ffn1_w_down[l], ln_g[l, 0], ln_b[l, 0])
        y_lat, y_ctx = hybrid_mixer(
            modulate(x, m_lat[3], m_lat[4]), modulate(h_ctx, m_ctx[3], m_ctx[4]), rows, cols, not last,
            w_in[l], w_out[l], ret_decay_logit[l], ret_norm_g[l], ret_norm_b[l],
            lru_conv_w[l], lru_conv_b[l], lru_w_a[l], lru_b_a[l], lru_w_i[l], lru_b_i[l], lru_lambda[l])
        x = layer_norm(ALPHA * x + m_lat[5] * y_lat, ln_g[l, 1], ln_b[l, 1])
        x = ffn_sublayer(x, m_lat[6], m_lat[7], m_lat[8], ffn2_w_gate[l], ffn2_w_up[l], ffn2_w_down[l], ln_g[l, 2], ln_b[l, 2])
        if not last:
            h_ctx = layer_norm(ALPHA * h_ctx + m_ctx[5] * y_ctx, ln_g[l, 1], ln_b[l, 1])
            h_ctx = ffn_sublayer(h_ctx, m_ctx[6], m_ctx[7], m_ctx[8], ffn2_w_gate[l], ffn2_w_up[l], ffn2_w_down[l], ln_g[l, 2], ln_b[l, 2])
    return x
```

```python
import contextlib
import numpy as np
import ml_dtypes
import concourse.bass as bass
import concourse.mybir as mybir
from concourse.bass_utils import run_bass_kernel_spmd

F32 = mybir.dt.float32
BF16 = mybir.dt.bfloat16
AF = mybir.ActivationFunctionType
ALU = mybir.AluOpType

T = 4096
TC = 256
TOT = T + TC
D = 1024
FF = 2816
NFC = 22
NDC = 8
NT = 1024
ALWAYS_STRICT = True
MULENG = "dve"
ALPHA = 2.0 ** 0.25
LN_EPS = 1e-5
K_SCALE = 128.0 ** -0.5
ENGS = ("pe", "act", "dve", "pool", "sp")
SB_BASE = 16640
SB_END = 229376


class Buf:
    __slots__ = ("name", "last_w", "readers")

    def __init__(self, name=""):
        self.name = name
        self.last_w = None
        self.readers = []


class Op:
    __slots__ = ("idx", "eng", "fn", "is_dma", "deps", "need_inc", "sem", "val", "dma_prev")

    def __init__(self, idx, eng, fn, is_dma):
        self.idx = idx
        self.eng = eng
        self.fn = fn
        self.is_dma = is_dma
        self.deps = set()
        self.need_inc = False
        self.sem = None
        self.val = 0
        self.dma_prev = 0


class Sched:
    def __init__(self, nc, n_dma_sems=8):
        self.nc = nc
        self.ops = []
        self.n_dma_sems = n_dma_sems
        self.out_dma_ops = []
        self.last_compute = {}
        self.dmas_since_barrier = []
        self.pending_barrier = {}
        self.strict = False

    def barrier(self):
        deps = set(self.last_compute.values()) | set(self.dmas_since_barrier)
        for e in ENGS:
            self.pending_barrier[e] = set(deps) | self.pending_barrier.get(e, set())
        self.dmas_since_barrier = []

    def add(self, eng, fn, reads=(), writes=(), dma=False, is_output=False):
        op = Op(len(self.ops), eng, fn, dma)
        ops = self.ops
        for b in reads:
            if b.last_w is not None:
                op.deps.add(b.last_w)
        for b in writes:
            if b.last_w is not None:
                op.deps.add(b.last_w)
            for r in b.readers:
                op.deps.add(r)
        pb = self.pending_barrier.pop(eng, None)
        if pb:
            op.deps |= pb
        pr = set()
        for d in op.deps:
            p = ops[d]
            if (not p.is_dma) and (not dma) and p.eng == eng and (eng == "pe" or not (self.strict or ALWAYS_STRICT)):
                continue
            pr.add(d)
        op.deps = pr
        for b in reads:
            if not dma:
                b.readers = [r for r in b.readers if ops[r].is_dma or ops[r].eng != eng]
            b.readers.append(op.idx)
        for b in writes:
            b.last_w = op.idx
            b.readers = []
        ops.append(op)
        if dma:
            self.dmas_since_barrier.append(op.idx)
        else:
            self.last_compute[eng] = op.idx
        if is_output:
            self.out_dma_ops.append(op.idx)
        return op

    def emit(self):
        nc = self.nc
        ops = self.ops
        for op in ops:
            for d in op.deps:
                ops[d].need_inc = True
        for d in self.out_dma_ops:
            ops[d].need_inc = True
        with contextlib.ExitStack() as st:
            eng_sem = {e: st.enter_context(nc.semaphore("s_" + e)) for e in ENGS}
            dma_sems = {e: [st.enter_context(nc.semaphore("d_%s%d" % (e, i))) for i in range(self.n_dma_sems)]
                        for e in ("sp", "pool", "act")}
            cnt = {e: 0 for e in ENGS}
            dcnt = {e: [0] * self.n_dma_sems for e in dma_sems}
            drr = {e: 0 for e in dma_sems}
            for op in ops:
                if op.is_dma:
                    k = drr[op.eng]
                    drr[op.eng] = (k + 1) % self.n_dma_sems
                    op.sem = dma_sems[op.eng][k]
                    op.dma_prev = dcnt[op.eng][k]
                    dcnt[op.eng][k] += 16
                    op.val = dcnt[op.eng][k]
                elif op.need_inc:
                    cnt[op.eng] += 1
                    op.sem = eng_sem[op.eng]
                    op.val = cnt[op.eng]
            block = st.enter_context(nc.Block())
            by_eng = {e: [op for op in ops if op.eng == e] for e in ENGS}
            final_waits = {}
            for d in self.out_dma_ops:
                key = id(ops[d].sem)
                if final_waits.get(key, (None, 0))[1] < ops[d].val:
                    final_waits[key] = (ops[d].sem, ops[d].val)

            def run(e, engine):
                waited = {}
                for op in by_eng[e]:
                    need = {}
                    for d in op.deps:
                        p = ops[d]
                        key = id(p.sem)
                        if need.get(key, (None, 0))[1] < p.val:
                            need[key] = (p.sem, p.val)
                    if op.is_dma and op.dma_prev > 0:
                        key = id(op.sem)
                        if need.get(key, (None, 0))[1] < op.dma_prev:
                            need[key] = (op.sem, op.dma_prev)
                    for key, (sem, val) in need.items():
                        if waited.get(key, 0) < val:
                            engine.wait_ge(sem, val)
                            waited[key] = val
                    ins = op.fn(engine)
                    if op.is_dma:
                        ins.then_inc(op.sem, 16)
                    elif op.need_inc:
                        ins.then_inc(op.sem, 1)
                if e == "sp":
                    for sem, val in final_waits.values():
                        if waited.get(id(sem), 0) < val:
                            engine.wait_ge(sem, val)

            @block.tensor
            def _(eng):
                run("pe", eng)

            @block.scalar
            def _(eng):
                run("act", eng)

            @block.vector
            def _(eng):
                run("dve", eng)

            @block.gpsimd
            def _(eng):
                run("pool", eng)

            @block.sync
            def _(eng):
                run("sp", eng)


class TB:
    __slots__ = ("t", "b")

    def __init__(self, t, b):
        self.t = t
        self.b = b


class Rot:
    def __init__(self, items):
        self.items = items
        self.i = 0

    def next(self):
        it = self.items[self.i % len(self.items)]
        self.i += 1
        return it


R_BADA, R_C, R_CCTX, R_LNG = 0, 72, 80, 88
R_LNB, R_RNG, R_RNB, R_CW, R_CB, R_BA, R_BI, R_LAM = 128, 152, 156, 160, 176, 180, 188, 196
(C_SC1L, C_SH1L, C_G1L, C_SC1C, C_SH1C, C_G1C, C_GU1L, C_BU1L, C_GU1C, C_BU1C, C_G2, C_GU2, C_BU2, C_G3,
 C_TMP, C_LG, C_KDEC, C_GC, C_CP, C_CP2, C_EPS, C_TMP2, C_EPSH, C_ONE) = [8 * i for i in range(24)]


def build_nc(stop_after=None):
    nc = bass.Bass("TRN2", target_bir_lowering=False)

    def din(name, shape, dt=F32):
        return nc.dram_tensor(name, shape, dt, kind="ExternalInput").ap()

    x_d = din("x", [T, D])
    ctx_d = din("ctx", [TC, D])
    small_d = din("small", [256, 128])
    wada_d = din("w_ada", [D, 9 * D])
    wg_d = [din("wg1", [D, FF]), din("wg2", [D, FF])]
    wu_d = [din("wu1", [D, FF]), din("wu2", [D, FF])]
    wd_d = [din("wd1", [FF, D]), din("wd2", [FF, D])]
    win_d = din("w_in", [D, 3 * D])
    wout_d = din("w_out", [D, D])
    lwa_d = din("lru_wa", [16, 64, 64])
    lwi_d = din("lru_wi", [16, 64, 64])
    logit_d = din("logit", [1, 8])
    cst_d = din("cst", [128, 648])
    ropec_d = din("rope_c", [128, T], BF16)
    ropes_d = din("rope_s", [128, T], BF16)
    out_d = nc.dram_tensor("out", [T, D], F32, kind="ExternalOutput").ap()
    okind = "ExternalOutput" if stop_after else "Internal"
    X1 = nc.dram_tensor("X1", [D, T], F32, kind=okind).ap()
    KT = nc.dram_tensor("KT", [4, 128, TOT], BF16, kind=okind).ap()
    QT = nc.dram_tensor("QT", [4, 128, T], BF16, kind=okind).ap()
    VTM = nc.dram_tensor("VTM", [4, 128, TOT // 128, 128], BF16, kind=okind).ap()
    XS = nc.dram_tensor("XS", [4, 128, TOT], F32, kind=okind).ap()
    GS = nc.dram_tensor("GS", [4, 128, T], BF16, kind=okind).ap()
    GG = nc.dram_tensor("GG", [4, 128, T], BF16, kind=okind).ap()
    CAT = nc.dram_tensor("CAT", [8, 128, T], BF16, kind=okind).ap()

    S = Sched(nc)
    off = [SB_BASE]

    def sb(name, shape, dt):
        n = int(np.prod(shape[1:])) * (4 if dt == F32 else 2)
        n = (n + 63) // 64 * 64
        assert off[0] + n <= SB_END, ("SBUF overflow", name, off[0] + n)
        t = nc.alloc_sbuf_tensor_at(name, list(shape), dt, offset=off[0])
        off[0] += n
        return t

    with contextlib.ExitStack() as st:
        PS = [TB(st.enter_context(nc.psum_tensor("ps%d" % i, [128, 512], F32)), Buf("ps%d" % i)) for i in range(8)]
        banks = Rot(PS)

        def mm(out, lhsT, rhs, start, stop, reads, writes):
            S.add("pe", lambda e: e.matmul(out, lhsT=lhsT, rhs=rhs, start=start, stop=stop), reads, writes)

        def tr(out, in_, ident, reads, writes):
            S.add("pe", lambda e: e.transpose(out, in_, ident), reads, writes)

        def act(out, in_, func, reads, writes, scale=None, bias=None):
            kw = {}
            if scale is not None:
                kw["scale"] = scale
            if bias is not None:
                kw["bias"] = bias
            S.add("act", lambda e: e.activation(out=out, in_=in_, func=func, **kw), reads, writes)

        def tt(eng, out, in0, in1, op, reads, writes):
            S.add(eng, lambda e: e.tensor_tensor(out=out, in0=in0, in1=in1, op=op), reads, writes)

        def ts(eng, out, in0, s1, s2, op0, op1, reads, writes):
            if s2 is None:
                S.add(eng, lambda e: e.tensor_scalar(out=out, in0=in0, scalar1=s1, scalar2=None, op0=op0), reads, writes)
            else:
                S.add(eng, lambda e: e.tensor_scalar(out=out, in0=in0, scalar1=s1, scalar2=s2, op0=op0, op1=op1), reads, writes)

        def stt(out, in0, scalar, in1, op0, op1, reads, writes):
            S.add("dve", lambda e: e.scalar_tensor_tensor(out=out, in0=in0, scalar=scalar, in1=in1, op0=op0, op1=op1), reads, writes)

        def cp(eng, out, in_, reads, writes):
            if eng == "act":
                act(out, in_, AF.Identity, reads, writes)
            else:
                S.add(eng, lambda e: e.tensor_copy(out=out, in_=in_), reads, writes)

        def dma(eng, out, in_, reads=(), writes=(), is_output=False):
            S.add(eng, lambda e: e.dma_start(out=out, in_=in_), reads, writes, dma=True, is_output=is_output)

        dbg_outs = {}

        def dbg(name, ap, shape, dt, reads):
            if not stop_after:
                return
            o = nc.dram_tensor("DBG_" + name, list(shape), dt, kind="ExternalOutput").ap()
            dma("sp", o[:], ap, reads=reads)

        VEC = sb("VEC", [128, 256], F32)
        MOD = sb("MOD", [128, 72, 2], F32)
        COLS = sb("COLS", [128, 192], F32)
        CST = sb("CST", [128, 648], F32)
        IDF = CST[:, 0:128]
        DIFFPOS = CST[:, 128:256]
        DIFFNEG = CST[:, 256:384]
        IROW = CST[:, 384:640]
        PCOLX = CST[:, 640:648]
        IDB = sb("IDB", [128, 128], BF16)
        ONES_D = sb("ONES_D", [128, 128], BF16)
        ONES_H = sb("ONES_H", [128, 128], BF16)
        DM = sb("DM", [128, 4, 128], F32)
        QDEC = sb("QDEC", [128, 8, 128], F32)
        WBD = sb("WBD", [128, 16, 128], BF16)
        SC = sb("SC", [128, 16], BF16)
        PERM = sb("PERM", [128, 128], BF16)
        bDG = Buf("DG")
        bVEC, bMOD, bCOLS, bCST, bIDB, bONES, bDM, bQDEC, bWBD, bSC = [Buf(n) for n in
            ["VEC", "MOD", "COLS", "CST", "IDB", "ONES", "DM", "QDEC", "WBD", "SC"]]
        phase_base = off[0]

        def col(c, i=0, w=1):
            return COLS[:, c + i:c + i + w]

        XT = sb("XT", [128, 8, NT], F32)
        UT = sb("UT", [128, 8, NT], BF16)
        HT = sb("HT", [128, NFC, NT], BF16)
        WD = sb("WD", [128, NFC, D], BF16)
        WS = [TB(sb("WS%d" % i, [128, 8, 256], BF16), Buf("WS%d" % i)) for i in range(4)]
        XIN = [TB(sb("XIN%d" % i, [128, D], F32), Buf("XIN%d" % i)) for i in range(2)]
        TMPF = [TB(sb("TMPF%d" % i, [128, 512], F32), Buf("TMPF%d" % i)) for i in range(4)]
        MEANS = [TB(sb("MEAN%d" % i, [128, 512], F32), Buf("MEAN%d" % i)) for i in range(2)]
        RSTDS = [TB(sb("RSTD%d" % i, [128, 512], F32), Buf("RSTD%d" % i)) for i in range(2)]
        STG = [TB(sb("STG%d" % i, [128, 512], BF16), Buf("STG%d" % i)) for i in range(4)]
        PB16 = Rot([TB(sb("PB16_%d" % i, [128, 512], BF16), Buf("PB16_%d" % i)) for i in range(3)])
        RC = TB(sb("RC", [128, NT], BF16), Buf("RC"))
        RS = TB(sb("RS", [128, NT], BF16), Buf("RS"))
        ac_end = off[0]
        ws_rot, xin_rot, tmpf_rot, stg_rot = Rot(WS), Rot(XIN), Rot(TMPF), Rot(STG)
        mean_rot, rstd_rot = Rot(MEANS), Rot(RSTDS)
        bXT = [[Buf("XT%d_%d" % (dc, t)) for t in range(8)] for dc in range(8)]
        bUT = [[Buf("UT%d_%d" % (dc, h)) for h in range(2)] for dc in range(8)]
        bHT = [[Buf("HT%d_%d" % (fc, h)) for h in range(2)] for fc in range(NFC)]
        bWD = [Buf("WD%d" % s) for s in range(11)]

        HSTG = []
        for i in range(6):
            fc = 8 + 2 * i
            t_ = HT[:, fc:fc + 2, :].rearrange("p a n -> p (a n)").bitcast(F32)
            HSTG.append((t_, [bHT[fc][0], bHT[fc][1], bHT[fc + 1][0], bHT[fc + 1][1]]))

        def xtb(dc, h0, hw):
            return [bXT[dc][t] for t in range(h0 // 128, (h0 + hw + 127) // 128)]

        def halves(n):
            return [(h, h * 512, min(512, n - h * 512)) for h in range((n + 511) // 512)]

        S.strict = True
        dma("sp", CST[:], cst_d[:], writes=[bCST])
        dma("sp", COLS[:, C_LG:C_LG + 8], logit_d.partition_broadcast(128), writes=[bCOLS])
        sm0, sm1 = XIN[0], XIN[1]
        dma("sp", sm0.t[:, 0:128], small_d[0:128, :], writes=[sm0.b])
        dma("sp", sm1.t[:, 0:128], small_d[128:256, :], writes=[sm1.b])
        for i, smx in enumerate((sm0, sm1)):
            pb = banks.next()
            tr(pb.t[:, 0:128], smx.t[:, 0:128], IDF, [smx.b, bCST], [pb.b])
            cp("dve", VEC[:, i * 128:(i + 1) * 128], pb.t[:, 0:128], [pb.b], [bVEC])
        cp("dve", IDB[:], IDF, [bCST], [bIDB])
        for (d0, s0) in ((0, 32), (32, 0), (64, 96), (96, 64)):
            cp("dve", PERM[:, d0:d0 + 32], IDF[:, s0:s0 + 32], [bCST], [bIDB])
        S.add("dve", lambda e: e.memset(ONES_D[:], 1.0 / D), (), [bONES])
        S.add("dve", lambda e: e.memset(ONES_H[:], 1.0 / 128.0), (), [bONES])
        S.add("dve", lambda e: e.memset(COLS[:, C_EPS:C_EPS + 8], LN_EPS / (ALPHA * ALPHA)), (), [bCOLS])
        S.add("dve", lambda e: e.memset(COLS[:, C_EPSH:C_EPSH + 8], LN_EPS), (), [bCOLS])
        S.add("dve", lambda e: e.memset(COLS[:, C_ONE:C_ONE + 8], 1.0), (), [bCOLS])
        act(SC[:], VEC[:, R_C:R_C + 16], AF.Silu, [bVEC], [bSC])
        wada_v = wada_d.rearrange("(kh kl) c -> kl kh c", kl=128)
        pmod = banks.next()
        WA = [TB(HT[:, 0:8, :], bHT[0][0]), TB(HT[:, 8:16, :], bHT[8][0])]
        for i in range(9):
            wa = WA[i % 2]
            dma("pool", wa.t, wada_v[:, :, i * 1024:(i + 1) * 1024], writes=[wa.b])
            for c in range(8):
                for kh in range(8):
                    o = (i * 8 + c) * 2
                    mm(pmod.t[:, o:o + 2], wa.t[:, kh, c * 128:(c + 1) * 128], SC[:, kh:16:8],
                       kh == 0, kh == 7, [wa.b, bSC], [pmod.b])
        tt("dve", MOD[:], pmod.t[:, 0:144].rearrange("p (r j) -> p r j", j=2),
           VEC[:, 0:72, None].broadcast_to([128, 72, 2]), ALU.add, [pmod.b, bVEC], [bMOD])

        def modv(i, j):
            return MOD[:, i * 8:(i + 1) * 8, j]

        def lng(l):
            return VEC[:, R_LNG + l * 8:R_LNG + l * 8 + 8]

        def lnb(l):
            return VEC[:, R_LNB + l * 8:R_LNB + l * 8 + 8]

        RW = [bMOD, bVEC, bCOLS]
        for j, (csc, csh, cg, cgu, cbu) in enumerate([(C_SC1L, C_SH1L, C_G1L, C_GU1L, C_BU1L),
                                                       (C_SC1C, C_SH1C, C_G1C, C_GU1C, C_BU1C)]):
            ts("dve", col(csc, 0, 8), modv(1, j), 1.0, None, ALU.add, None, RW, [bCOLS])
            cp("dve", col(csh, 0, 8), modv(0, j), RW, [bCOLS])
            ts("dve", col(cg, 0, 8), modv(2, j), 0.5 / ALPHA, None, ALU.mult, None, RW, [bCOLS])
            ts("dve", col(C_TMP, 0, 8), modv(4, j), 1.0, None, ALU.add, None, RW, [bCOLS])
            tt("dve", col(cgu, 0, 8), col(C_TMP, 0, 8), lng(0), ALU.mult, RW, [bCOLS])
            tt("dve", col(C_TMP2, 0, 8), col(C_TMP, 0, 8), lnb(0), ALU.mult, RW, [bCOLS])
            tt("dve", col(cbu, 0, 8), col(C_TMP2, 0, 8), modv(3, j), ALU.add, RW, [bCOLS])
        ts("dve", col(C_G2, 0, 8), modv(5, 0), 1.0 / ALPHA, None, ALU.mult, None, RW, [bCOLS])
        ts("dve", col(C_TMP, 0, 8), modv(7, 0), 1.0, None, ALU.add, None, RW, [bCOLS])
        tt("dve", col(C_GU2, 0, 8), col(C_TMP, 0, 8), lng(1), ALU.mult, RW, [bCOLS])
        tt("dve", col(C_TMP2, 0, 8), col(C_TMP, 0, 8), lnb(1), ALU.mult, RW, [bCOLS])
        tt("dve", col(C_BU2, 0, 8), col(C_TMP2, 0, 8), modv(6, 0), ALU.add, RW, [bCOLS])
        ts("dve", col(C_G3, 0, 8), modv(8, 0), 0.5 / ALPHA, None, ALU.mult, None, RW, [bCOLS])

        act(col(C_LG, 0, 8), col(C_LG, 0, 8), AF.Exp, [bCOLS], [bCOLS], scale=-1.0)
        ts("dve", col(C_LG, 0, 8), col(C_LG, 0, 8), 1.0, None, ALU.add, None, [bCOLS], [bCOLS])
        act(col(C_LG, 0, 8), col(C_LG, 0, 8), AF.Ln, [bCOLS], [bCOLS])
        ts("dve", col(C_LG, 0, 8), col(C_LG, 0, 8), -1.0, None, ALU.mult, None, [bCOLS], [bCOLS])
        tt("dve", col(C_KDEC, 0, 8), col(C_LG, 0, 8), PCOLX, ALU.mult, [bCOLS, bCST], [bCOLS])
        act(col(C_KDEC, 0, 8), col(C_KDEC, 0, 8), AF.Exp, [bCOLS], [bCOLS])
        ts("dve", col(C_KDEC, 0, 8), col(C_KDEC, 0, 8), K_SCALE, None, ALU.mult, None, [bCOLS], [bCOLS])
        act(col(C_GC, 0, 8), col(C_LG, 0, 8), AF.Exp, [bCOLS], [bCOLS], scale=128.0)
        for h in range(4):
            ts("dve", DM[:, h, :], DIFFPOS, col(C_LG, h), None, ALU.mult, None, [bCOLS, bCST], [bDM])
            stt(DM[:, h, :], DIFFNEG, col(C_LG, 4 + h), DM[:, h, :], ALU.mult, ALU.add, [bCOLS, bCST, bDM], [bDM])
            act(DM[:, h, :], DM[:, h, :], AF.Exp, [bDM], [bDM])
            ts("dve", DM[:, h, :], DM[:, h, :], K_SCALE, None, ALU.mult, None, [bDM], [bDM])
            for d in range(2):
                ts("dve", QDEC[:, d * 4 + h, :], IROW[:, d * 128:(d + 1) * 128], col(C_LG, d * 4 + h), None,
                   ALU.mult, None, [bCOLS, bCST], [bQDEC])
                act(QDEC[:, d * 4 + h, :], QDEC[:, d * 4 + h, :], AF.Exp, [bQDEC], [bQDEC])
        act(col(C_CP, 0, 8), VEC[:, R_LAM:R_LAM + 8], AF.Exp, [bVEC], [bCOLS], scale=-1.0)
        ts("dve", col(C_CP, 0, 8), col(C_CP, 0, 8), 1.0, None, ALU.add, None, [bCOLS], [bCOLS])
        act(col(C_CP, 0, 8), col(C_CP, 0, 8), AF.Ln, [bCOLS], [bCOLS])
        ts("dve", col(C_CP2, 0, 8), col(C_CP, 0, 8), -16.0, None, ALU.mult, None, [bCOLS], [bCOLS])
        ts("dve", col(C_CP, 0, 8), col(C_CP, 0, 8), -8.0, None, ALU.mult, None, [bCOLS], [bCOLS])
        S.add("dve", lambda e: e.memset(WBD[:], 0.0), (), [bWBD])
        for g, wsrc in enumerate((lwa_d, lwi_d)):
            for d in range(2):
                for cb in range(4):
                    for n in range(2):
                        dma("pool", WBD[64 * n:64 * n + 64, (g * 2 + d) * 4 + cb, 64 * n:64 * n + 64],
                            wsrc[d * 8 + 2 * cb + n], writes=[bWBD])

        S.strict = False
        def mm1_load(which, s):
            wgv = wg_d[which].rearrange("(kh kl) c -> kl kh c", kl=128)
            wuv = wu_d[which].rearrange("(kh kl) c -> kl kh c", kl=128)
            wdv = wd_d[which].rearrange("(j fl) c -> fl j c", fl=128)
            bg = ws_rot.next()
            dma("pool", bg.t[:], wgv[:, :, s * 256:(s + 1) * 256], writes=[bg.b])
            bu = ws_rot.next()
            dma("pool", bu.t[:], wuv[:, :, s * 256:(s + 1) * 256], writes=[bu.b])
            dma("pool", WD[:, 2 * s:2 * s + 2, :], wdv[:, 2 * s:2 * s + 2, :], writes=[bWD[s]])
            return bg, bu

        def mm1_unit(bg, bu, s, j, h, h0, hw):
            fc = 2 * s + j
            pg = banks.next()
            for k in range(8):
                mm(pg.t[:, :hw], bg.t[:, k, j * 128:(j + 1) * 128], UT[:, k, h0:h0 + hw],
                   k == 0, k == 7, [bg.b, bUT[k][h]], [pg.b])
            pu = banks.next()
            for k in range(8):
                mm(pu.t[:, :hw], bu.t[:, k, j * 128:(j + 1) * 128], UT[:, k, h0:h0 + hw],
                   k == 0, k == 7, [bu.b, bUT[k][h]], [pu.b])
            sg = tmpf_rot.next()
            act(sg.t[:, :hw], pg.t[:, :hw], AF.Silu, [pg.b], [sg.b])
            tt("dve", HT[:, fc, h0:h0 + hw], sg.t[:, :hw], pu.t[:, :hw], ALU.mult,
               [sg.b, pu.b], [bHT[fc][h]])

        def ffn_mm1(which, n, early=None):
            hv = halves(n)
            early = early or {}
            for s in sorted(early):
                bg, bu = early[s]
                for j in range(2):
                    for (h, h0, hw) in hv[1:]:
                        mm1_unit(bg, bu, s, j, h, h0, hw)
            for s in range(11):
                if s in early:
                    continue
                bg, bu = mm1_load(which, s)
                for j in range(2):
                    for (h, h0, hw) in hv:
                        mm1_unit(bg, bu, s, j, h, h0, hw)

        def mm1_early(which, n, slabs=(0, 1)):
            (h, h0, hw) = halves(n)[0]
            early = {}
            for s in slabs:
                bg, bu = mm1_load(which, s)
                early[s] = (bg, bu)
                for j in range(2):
                    mm1_unit(bg, bu, s, j, h, h0, hw)
            return early

        def mm2_unit(h, h0, hw, dc, gcolbase):
            pf = banks.next()
            for fc in range(NFC):
                mm(pf.t[:, :hw], WD[:, fc, dc * 128:(dc + 1) * 128], HT[:, fc, h0:h0 + hw],
                   fc == 0, fc == NFC - 1, [bWD[fc // 2], bHT[fc][h]], [pf.b])
            xb = xtb(dc, h0, hw)
            stt(XT[:, dc, h0:h0 + hw], pf.t[:, :hw], col(gcolbase, dc), XT[:, dc, h0:h0 + hw],
                ALU.mult, ALU.add, [pf.b, bCOLS] + xb, xb)

        def residual_ln(n, unit_fn, mode, j=0, t0=0, filler=None, after_mm=None):
            hv = halves(n)
            if len(hv) == 1:
                (h, h0, hw) = hv[0]
                for dc in range(8):
                    unit_fn(h, h0, hw, dc)
                if after_mm is not None:
                    after_mm()
                ln_pre(h, h0, hw)
                pm, pq = ln_stats(h, h0, hw)
                ln_post(h, h0, hw, pm, pq, mode, j, t0)
                return
            (ha, a0, aw), (hb, b0, bw) = hv
            for dc in range(8):
                unit_fn(ha, a0, aw, dc)
            ln_pre(ha, a0, aw)
            for dc in range(4):
                unit_fn(hb, b0, bw, dc)
            pm, pq = ln_stats(ha, a0, aw)
            ln_post(ha, a0, aw, pm, pq, mode, j, t0)
            for dc in range(4, 8):
                unit_fn(hb, b0, bw, dc)
            if after_mm is not None:
                after_mm()
            ln_pre(hb, b0, bw)
            if filler is not None:
                filler()
            pm, pq = ln_stats(hb, b0, bw)
            ln_post(hb, b0, bw, pm, pq, mode, j, t0)

        def ln_pre(h, h0, hw):
            for dc in range(8):
                xb = xtb(dc, h0, hw)
                act(UT[:, dc, h0:h0 + hw], XT[:, dc, h0:h0 + hw], AF.Identity, xb, [bUT[dc][h]])
                act(HT[:, dc, h0:h0 + hw], XT[:, dc, h0:h0 + hw], AF.Square, xb, [bHT[dc][h]])

        def ln_stats(h, h0, hw):
            pm = banks.next()
            for dc in range(8):
                mm(pm.t[:, :hw], ONES_D[:], UT[:, dc, h0:h0 + hw], dc == 0, dc == 7, [bONES, bUT[dc][h]], [pm.b])
            pq = banks.next()
            for dc in range(8):
                mm(pq.t[:, :hw], ONES_D[:], HT[:, dc, h0:h0 + hw], dc == 0, dc == 7, [bONES, bHT[dc][h]], [pq.b])
            return pm, pq

        def ln_post(h, h0, hw, pm, pq, mode, j=0, t0=0):
            mean = mean_rot.next()
            rstd = rstd_rot.next()
            act(mean.t[:, :hw], pm.t[:, :hw], AF.Identity, [pm.b], [mean.b])
            act(rstd.t[:, :hw], pm.t[:, :hw], AF.Square, [pm.b], [rstd.b])
            tt("dve", rstd.t[:, :hw], pq.t[:, :hw], rstd.t[:, :hw], ALU.subtract, [pq.b, rstd.b], [rstd.b])
            act(rstd.t[:, :hw], rstd.t[:, :hw], AF.Ln, [rstd.b, bCOLS], [rstd.b], bias=col(C_EPS))
            act(rstd.t[:, :hw], rstd.t[:, :hw], AF.Exp, [rstd.b], [rstd.b], scale=-0.5)
            for dc in range(8):
                xb = xtb(dc, h0, hw)
                t1 = tmpf_rot.next()
                tt("dve", t1.t[:, :hw], XT[:, dc, h0:h0 + hw], mean.t[:, :hw], ALU.subtract, xb + [mean.b], [t1.b])
                tt("dve", t1.t[:, :hw], t1.t[:, :hw], rstd.t[:, :hw], ALU.mult, [t1.b, rstd.b], [t1.b])
                if mode == "ln1":
                    xs = tmpf_rot.next()
                    act(xs.t[:, :hw], t1.t[:, :hw], AF.Identity, [t1.b, bVEC], [xs.b],
                        scale=VEC[:, R_LNG + dc:R_LNG + dc + 1], bias=VEC[:, R_LNB + dc:R_LNB + dc + 1])
                    dma("sp", X1[dc * 128:(dc + 1) * 128, t0 + h0:t0 + h0 + hw], xs.t[:, :hw], reads=[xs.b])
                if mode in ("ln1", "ln1c"):
                    cgu, cbu = (C_GU1L, C_BU1L) if j == 0 else (C_GU1C, C_BU1C)
                    act(UT[:, dc, h0:h0 + hw], t1.t[:, :hw], AF.Identity, [t1.b, bCOLS], [bUT[dc][h]],
                        scale=col(cgu, dc), bias=col(cbu, dc))
                elif mode == "ln2":
                    act(XT[:, dc, h0:h0 + hw], t1.t[:, :hw], AF.Identity, [t1.b, bVEC], xb,
                        scale=VEC[:, R_LNG + 8 + dc:R_LNG + 9 + dc], bias=VEC[:, R_LNB + 8 + dc:R_LNB + 9 + dc])
                    act(UT[:, dc, h0:h0 + hw], t1.t[:, :hw], AF.Identity, [t1.b, bCOLS], [bUT[dc][h]],
                        scale=col(C_GU2, dc), bias=col(C_BU2, dc))
                elif mode == "ln3":
                    act(XT[:, dc, h0:h0 + hw], t1.t[:, :hw], AF.Identity, [t1.b, bVEC], xb,
                        scale=VEC[:, R_LNG + 16 + dc:R_LNG + 17 + dc], bias=VEC[:, R_LNB + 16 + dc:R_LNB + 17 + dc])

        def rope_evac(p, hw, h0, stg):
            pb = PB16.next()
            cp("act", pb.t[:, :hw], p.t[:, :hw], [p.b], [pb.b])
            q = banks.next()
            mm(q.t[:, :hw], PERM[:], pb.t[:, :hw], True, True, [bIDB, pb.b], [q.b])
            t1 = tmpf_rot.next()
            tt("dve", t1.t[:, :hw], pb.t[:, :hw], RC.t[:, h0:h0 + hw], ALU.mult, [pb.b, RC.b], [t1.b])
            t2 = tmpf_rot.next()
            tt("dve", t2.t[:, :hw], q.t[:, :hw], RS.t[:, h0:h0 + hw], ALU.mult, [q.b, RS.b], [t2.b])
            tt("dve", stg.t[:, :hw], t1.t[:, :hw], t2.t[:, :hw], ALU.add, [t1.b, t2.b], [stg.b])

        win_v = win_d.rearrange("(kh kl) c -> kl kh c", kl=128)
        groups = [("ctx", 0, TC)] + [("lat", g * NT, NT) for g in range(T // NT)]

        def stage_in_loads(kind, t0, n):
            src = x_d if kind == "lat" else ctx_d
            tiles = []
            for tix in range(n // 128):
                if tix < 2:
                    xb = XIN[tix]
                    t_, bl = xb.t[:], [xb.b]
                else:
                    t_, bl = HSTG[tix - 2]
                dma("sp", t_, src[t0 + tix * 128:t0 + (tix + 1) * 128, :], writes=bl)
                tiles.append((t_, bl))
            return tiles

        def stage_in(kind, t0, n, tiles):
            for tix, (t_, bl) in enumerate(tiles):
                for hb in range(2):
                    pb = banks.next()
                    for q in range(4):
                        tr(pb.t[:, q * 128:(q + 1) * 128], t_[:, (hb * 4 + q) * 128:(hb * 4 + q + 1) * 128], IDF,
                           bl + [bCST], [pb.b])
                    cp("act" if hb == 0 else "dve", XT[:, hb * 4:(hb + 1) * 4, tix * 128:(tix + 1) * 128],
                       pb.t[:].rearrange("p (q i) -> p q i", q=4), [pb.b], [bXT[hb * 4 + q][tix] for q in range(4)])

        def modulate_in(kind, n):
            csc, csh = (C_SC1L, C_SH1L) if kind == "lat" else (C_SC1C, C_SH1C)
            for dc in range(8):
                for (h, h0, hw) in halves(n):
                    act(UT[:, dc, h0:h0 + hw], XT[:, dc, h0:h0 + hw], AF.Identity, xtb(dc, h0, hw) + [bCOLS],
                        [bUT[dc][h]], scale=col(csc, dc), bias=col(csh, dc))

        def proj_unit(wb, s, jj, h, h0, hw, lat, col0, t0):
            pk = s // 2
            c = (s % 2) * 2 + jj
            p = banks.next()
            for k in range(8):
                mm(p.t[:, :hw], wb.t[:, k, jj * 128:(jj + 1) * 128], UT[:, k, h0:h0 + hw],
                   k == 0, k == 7, [wb.b, bUT[k][h]], [p.b])
            if pk == 2:
                xs = tmpf_rot.next()
                cp("act", xs.t[:, :hw], p.t[:, :hw], [p.b], [xs.b])
                dma("sp", XS[c, :, col0 + h0:col0 + h0 + hw], xs.t[:, :hw], reads=[xs.b])
                return
            sg = stg_rot.next()
            if pk == 0:
                if lat:
                    rope_evac(p, hw, h0, sg)
                else:
                    cp("act", sg.t[:, :hw], p.t[:, :hw], [p.b], [sg.b])
                dma("sp", KT[c, :, col0 + h0:col0 + h0 + hw], sg.t[:, :hw], reads=[sg.b])
            elif pk == 3:
                rope_evac(p, hw, h0, sg)
                dma("sp", QT[c, :, t0 + h0:t0 + h0 + hw], sg.t[:, :hw], reads=[sg.b])
            elif pk == 4:
                act(sg.t[:, :hw], p.t[:, :hw], AF.Silu, [p.b], [sg.b])
                dma("sp", GS[c, :, t0 + h0:t0 + h0 + hw], sg.t[:, :hw], reads=[sg.b])
            else:
                act(sg.t[:, :hw], p.t[:, :hw], AF.Gelu, [p.b], [sg.b])
                dma("sp", GG[c, :, t0 + h0:t0 + h0 + hw], sg.t[:, :hw], reads=[sg.b])

        def proj_load_pair(s0):
            wbp = []
            for s in (s0, s0 + 1):
                wb = ws_rot.next()
                dma("pool", wb.t[:], win_v[:, :, s * 256:(s + 1) * 256], writes=[wb.b])
                wbp.append(wb)
            return wbp

        def proj_pair_units(kind, t0, n, s0, wbp, hsel):
            lat = kind == "lat"
            col0 = TC + t0 if lat else 0
            if s0 // 2 == 1:
                for tix in range(n // 128):
                    h = tix // 4
                    if h not in hsel:
                        continue
                    for si, s in enumerate((s0, s0 + 1)):
                        wb = wbp[si]
                        pv = banks.next()
                        for k in range(8):
                            mm(pv.t[:, 0:256], UT[:, k, tix * 128:(tix + 1) * 128], wb.t[:, k, :],
                               k == 0, k == 7, [wb.b, bUT[k][h]], [pv.b])
                        sg = stg_rot.next()
                        cp("act" if (tix + si) % 2 == 0 else "dve", sg.t[:, 0:256], pv.t[:, 0:256], [pv.b], [sg.b])
                        for hh in range(2):
                            dma("sp", VTM[(s - 2) * 2 + hh, :, col0 // 128 + tix, :], sg.t[:, hh * 128:(hh + 1) * 128],
                                reads=[sg.b])
                return
            for (h, h0, hw) in halves(n):
                if h not in hsel:
                    continue
                for si, s in enumerate((s0, s0 + 1)):
                    for jj in range(2):
                        proj_unit(wbp[si], s, jj, h, h0, hw, lat, col0, t0)

        stage_in(*groups[0], stage_in_loads(*groups[0]))
        modulate_in(groups[0][0], groups[0][2])
        for gi, (kind, t0, n) in enumerate(groups):
            lat = kind == "lat"
            if lat:
                dma("sp", RC.t[:], ropec_d[:, t0:t0 + n], writes=[RC.b])
                dma("sp", RS.t[:], ropes_d[:, t0:t0 + n], writes=[RS.b])
            gcol = C_G1L if lat else C_G1C
            nxt = {}

            def prefetch_next(gi=gi, nxt=nxt):
                if gi + 1 < len(groups):
                    nxt["tiles"] = stage_in_loads(*groups[gi + 1])

            ffn_mm1(0, n)
            pair0 = {}

            def filler(kind=kind, t0=t0, n=n, pair0=pair0):
                pair0["wbp"] = proj_load_pair(0)
                proj_pair_units(kind, t0, n, 0, pair0["wbp"], (0,))

            residual_ln(n, lambda h, h0, hw, dc, gcol=gcol: mm2_unit(h, h0, hw, dc, gcol),
                        "ln1" if lat else "ln1c", j=0 if lat else 1, t0=t0, filler=filler if lat else None,
                        after_mm=prefetch_next)
            if lat and t0 == 0:
                dbg("UT", UT[:], [128, 8, NT], BF16, [b for r in bUT for b in r])
                dbg("COLS", COLS[:], [128, 192], F32, [bCOLS])
                dbg("MOD", MOD[:].rearrange("p r j -> p (r j)"), [128, 144], F32, [bMOD])
            if lat:
                proj_pair_units(kind, t0, n, 0, pair0["wbp"], (1,))
            else:
                proj_pair_units(kind, t0, n, 0, proj_load_pair(0), (0,))
            if gi + 1 < len(groups):
                stage_in(*groups[gi + 1], nxt["tiles"])
            for s0 in range(2, 12 if lat else 6, 2):
                proj_pair_units(kind, t0, n, s0, proj_load_pair(s0), (0, 1))
            if gi + 1 < len(groups):
                modulate_in(groups[gi + 1][0], groups[gi + 1][2])
        S.barrier()

        def finish_debug(src_ap):
            z = XIN[0]
            S.add("dve", lambda e: e.memset(z.t[:], 0.0), (), [z.b])
            for r in range(T // 128):
                dma("sp", out_d[r * 128:(r + 1) * 128, :], z.t[:], reads=[z.b], is_output=True)
            S.emit()

        if stop_after == "A":
            finish_debug(None)
            return nc

        off[0] = phase_base
        NCH = TOT // 128
        LOADS = []
        for i in range(2):
            LOADS.append(dict(
                K=TB(sb("KTh%d" % i, [128, TOT], BF16), Buf("KTh%d" % i)),
                Q=TB(sb("QTh%d" % i, [128, T], BF16), Buf("QTh%d" % i)),
                V=TB(sb("Vh%d" % i, [128, NCH, 128], BF16), Buf("Vh%d" % i)),
                G=TB(sb("GSh%d" % i, [128, T], BF16), Buf("GSh%d" % i))))
        KDF = TB(sb("KDF", [128, NCH, 128], BF16), Buf("KDF"))
        KDB = TB(sb("KDB", [128, NCH, 128], BF16), Buf("KDB"))
        VDF = TB(sb("VDF", [128, NCH, 128], BF16), Buf("VDF"))
        NSL = 4
        SFF = sb("SFF", [128, NSL, 128], F32)
        SBF = sb("SBF", [128, NSL, 128], F32)
        bSFF = [Buf("SFF%d" % i) for i in range(NSL)]
        bSBF = [Buf("SBF%d" % i) for i in range(NSL)]
        SFb = TB(sb("SFb", [128, 32, 128], BF16), Buf("SFb"))
        SBb = TB(sb("SBb", [128, 32, 128], BF16), Buf("SBb"))
        QF = TB(sb("QF", [128, T], BF16), Buf("QF"))
        QB = TB(sb("QB", [128, T], BF16), Buf("QB"))

        def mkrot(name, n, shape, dt):
            return Rot([TB(sb("%s%d" % (name, i), shape, dt), Buf("%s%d" % (name, i))) for i in range(n)])

        PT = mkrot("PT", 3, [128, 512], BF16)
        OB = mkrot("OB", 3, [128, 512], BF16)
        OSQ = mkrot("OSQ", 3, [128, 512], BF16)
        OF = mkrot("OF", 9, [128, 512], F32)
        BMEAN = mkrot("BMEAN", 3, [128, 512], F32)
        BRSTD = mkrot("BRSTD", 5, [128, 512], F32)
        BEX2 = mkrot("BEX2", 3, [128, 512], F32)
        RETO = mkrot("RETO", 1, [128, T], BF16)

        def ret_load(h):
            L = LOADS[h % 2]
            dma("sp", L["K"].t[:], KT[h], writes=[L["K"].b])
            dma("sp", L["V"].t[:], VTM[h], writes=[L["V"].b])
            dma("sp", L["Q"].t[:], QT[h], writes=[L["Q"].b])
            dma("sp", L["G"].t[:], GS[h], writes=[L["G"].b])

        ret_load(0)
        for h in range(4):
            if h + 1 < 4:
                ret_load(h + 1)
            L = LOADS[h % 2]
            Kt, Qt, Vt, Gt = L["K"], L["Q"], L["V"], L["G"]
            act(VDF.t[:].rearrange("p n i -> p (n i)"), Vt.t[:].rearrange("p n i -> p (n i)"), AF.Identity,
                [Vt.b, bCOLS], [VDF.b], scale=col(C_KDEC, h))
            act(KDB.t[:].rearrange("p n i -> p (n i)"), Vt.t[:].rearrange("p n i -> p (n i)"), AF.Identity,
                [Vt.b, bCOLS], [KDB.b], scale=col(C_KDEC, 4 + h))
            for n0 in range(0, NCH, 4):
                nn = min(4, NCH - n0)
                pb = banks.next()
                for q in range(nn):
                    mm(pb.t[:, q * 128:(q + 1) * 128], Kt.t[:, (n0 + q) * 128:(n0 + q + 1) * 128], IDB[:],
                       True, True, [Kt.b, bIDB], [pb.b])
                cp("act", KDF.t[:].rearrange("p n i -> p (n i)")[:, n0 * 128:(n0 + nn) * 128],
                   pb.t[:, 0:nn * 128], [pb.b], [KDF.b])
            tt("dve", QF.t[:].rearrange("p (n i) -> p n i", i=128), Qt.t[:].rearrange("p (n i) -> p n i", i=128),
               QDEC[:, h:h + 1, :].broadcast_to([128, 32, 128]), ALU.mult, [Qt.b, bQDEC], [QF.b])
            tt("dve", QB.t[:].rearrange("p (n i) -> p n i", i=128), Qt.t[:].rearrange("p (n i) -> p n i", i=128),
               QDEC[:, 4 + h:5 + h, :].broadcast_to([128, 32, 128]), ALU.mult, [Qt.b, bQDEC], [QB.b])
            border = [1, 0] + [2 + m for m in range(31, 0, -1)]
            for si in range(33):
                for (d, n, ST, bST, VD, Sb) in ((0, si, SFF, bSFF, VDF, SFb), (1, border[si], SBF, bSBF, KDB, SBb)):
                    pc = banks.next()
                    mm(pc.t[:, 0:128], KDF.t[:, n, :], VD.t[:, n, :], True, True, [KDF.b, VD.b], [pc.b])
                    sl, pl_ = si % NSL, (si - 1) % NSL
                    if si == 0:
                        cp("dve", ST[:, sl, :], pc.t[:, 0:128], [pc.b], [bST[sl]])
                    else:
                        stt(ST[:, sl, :], ST[:, pl_, :], col(C_GC, d * 4 + h), pc.t[:, 0:128], ALU.mult, ALU.add,
                            [pc.b, bST[pl_], bCOLS], [bST[sl]])
                        act(Sb.t[:, si - 1, :], ST[:, sl, :], AF.Identity, [bST[sl]], [Sb.b])
            ro = RETO.next()
            NBT = 8
            st_ = [dict() for _ in range(NBT)]

            def T0(bt, X):
                X["ps"] = banks.next()
                for c in range(4):
                    m = bt * 4 + c
                    kc = (2 + m) * 128
                    mm(X["ps"].t[:, c * 128:(c + 1) * 128], Kt.t[:, kc:kc + 128], Qt.t[:, m * 128:(m + 1) * 128],
                       True, True, [Kt.b, Qt.b], [X["ps"].b])

            def T1(bt, X):
                X["pt"] = PT.next()
                tt("dve", X["pt"].t[:].rearrange("p (c i) -> p c i", c=4), X["ps"].t[:].rearrange("p (c i) -> p c i", c=4),
                   DM[:, h:h + 1, :].broadcast_to([128, 4, 128]), ALU.mult, [X["ps"].b, bDM], [X["pt"].b])

            def T2(bt, X):
                X["po"] = banks.next()
                po, pt = X["po"], X["pt"]
                for c in range(4):
                    m = bt * 4 + c
                    oc = po.t[:, c * 128:(c + 1) * 128]
                    mm(oc, Vt.t[:, 2 + m, :], pt.t[:, c * 128:(c + 1) * 128], True, False, [Vt.b, pt.b], [po.b])
                    mm(oc, SFb.t[:, m, :], QF.t[:, m * 128:(m + 1) * 128], False, False, [SFb.b, QF.b], [po.b])
                    mm(oc, SBb.t[:, 31 - m, :], QB.t[:, m * 128:(m + 1) * 128], False, True, [SBb.b, QB.b], [po.b])

            def T3(bt, X):
                X["ob"], X["osq"], X["of"] = OB.next(), OSQ.next(), OF.next()
                po = X["po"]
                cp("act", X["of"].t[:], po.t[:], [po.b], [X["of"].b])
                act(X["osq"].t[:], po.t[:], AF.Square, [po.b], [X["osq"].b])
                cp("pool", X["ob"].t[:], X["of"].t[:], [X["of"].b], [X["ob"].b])

            def T4(bt, X):
                X["pm"], X["pq"] = banks.next(), banks.next()
                mm(X["pm"].t[:], ONES_H[:], X["ob"].t[:], True, True, [bONES, X["ob"].b], [X["pm"].b])
                mm(X["pq"].t[:], ONES_H[:], X["osq"].t[:], True, True, [bONES, X["osq"].b], [X["pq"].b])

            def T5(bt, X):
                X["mean"], X["rstd"], X["ex2"] = BMEAN.next(), BRSTD.next(), BEX2.next()
                act(X["mean"].t[:], X["pm"].t[:], AF.Identity, [X["pm"].b], [X["mean"].b])
                act(X["rstd"].t[:], X["pm"].t[:], AF.Square, [X["pm"].b], [X["rstd"].b])
                act(X["ex2"].t[:], X["pq"].t[:], AF.Identity, [X["pq"].b], [X["ex2"].b])

            def T6(bt, X):
                r = X["rstd"]
                tt("dve", r.t[:], X["ex2"].t[:], r.t[:], ALU.subtract, [X["ex2"].b, r.b], [r.b])
                tt("pool", X["of"].t[:], X["of"].t[:], X["mean"].t[:], ALU.subtract, [X["of"].b, X["mean"].b], [X["of"].b])

            def T7(bt, X):
                r = X["rstd"]
                act(r.t[:], r.t[:], AF.Ln, [r.b, bCOLS], [r.b], bias=col(C_EPSH))
                act(r.t[:], r.t[:], AF.Exp, [r.b], [r.b], scale=-0.5)

            def T8(bt, X):
                r = X["rstd"]
                tt("dve", X["of"].t[:], X["of"].t[:], r.t[:], ALU.mult, [X["of"].b, r.b], [X["of"].b])

            def T9(bt, X):
                o_ = X["of"]
                ts("dve", o_.t[:], o_.t[:], VEC[:, R_RNG + h:R_RNG + h + 1], VEC[:, R_RNB + h:R_RNB + h + 1],
                   ALU.mult, ALU.add, [o_.b, bVEC], [o_.b])

            def T10(bt, X):
                tt("pool", ro.t[:, bt * 512:(bt + 1) * 512], X["of"].t[:], Gt.t[:, bt * 512:(bt + 1) * 512], ALU.mult,
                   [X["of"].b, Gt.b], [ro.b])

            stages = [T0, T1, T2, T3, T4, T5, T6, T7, T8, T9, T10]
            for k in range(NBT + len(stages) - 1):
                for s_i, fn in enumerate(stages):
                    bt = k - s_i
                    if 0 <= bt < NBT:
                        fn(bt, st_[bt])
            dma("sp", CAT[h], ro.t[:], reads=[ro.b])
        S.barrier()

        if stop_after == "B1":
            finish_debug(None)
            return nc
        off[0] = phase_base
        LX = TC + 3 + T
        LP = LX + 5
        XP = TB(sb("XP", [128, LP], F32), Buf("XP"))
        XC = TB(sb("XC", [128, LP], F32), Buf("XC"))
        XCB = TB(sb("XCB", [128, LP], BF16), Buf("XCB"))
        RAd = [TB(sb("RA%d" % d, [128, LP], F32), Buf("RA%d" % d)) for d in range(2)]
        IBd = [TB(sb("IB%d" % d, [128, LP], F32), Buf("IB%d" % d)) for d in range(2)]
        TMd = [TB(sb("TM%d" % d, [128, LP], F32), Buf("TM%d" % d)) for d in range(2)]
        GGc = TB(sb("GGc", [128, T], BF16), Buf("GGc"))
        OUTc = TB(sb("OUTc", [128, T], BF16), Buf("OUTc"))
        DG = sb("DG", [128, 16, 128], F32)
        S.strict = True
        for k in range(16):
            ts("dve", DG[:, k, :], IDF, VEC[:, R_CW + k:R_CW + k + 1], None, ALU.mult, None, [bCST, bVEC], [bDG])
        S.strict = False
        C0, L0 = 0, TC + 3
        S.add("pool", lambda e: e.memset(XP.t[:], 0.0), (), [XP.b])
        def lru_load_x(cb):
            dma("sp", XP.t[:, 2:2 + TC], XS[cb, :, 0:TC], writes=[XP.b])
            dma("sp", XP.t[:, 5 + TC:5 + TC + T], XS[cb, :, TC:TOT], writes=[XP.b])

        lru_load_x(0)
        dma("sp", GGc.t[:], GG[0], writes=[GGc.b])
        for cb in range(4):
            for c0 in range(0, LX, 512):
                w = min(512, LX - c0)
                pcv = banks.next()
                for jt in range(4):
                    mm(pcv.t[:, :w], DG[:, jt * 4 + cb, :], XP.t[:, c0 + jt:c0 + jt + w], jt == 0, jt == 3,
                       [bDG, XP.b], [pcv.b])
                act(XC.t[:, c0:c0 + w], pcv.t[:, :w], AF.Identity, [pcv.b, bVEC], [XC.b], bias=VEC[:, R_CB + cb:R_CB + cb + 1])
                act(XCB.t[:, c0:c0 + w], pcv.t[:, :w], AF.Identity, [pcv.b, bVEC], [XCB.b], bias=VEC[:, R_CB + cb:R_CB + cb + 1])
            if cb + 1 < 4:
                lru_load_x(cb + 1)
            for c0 in range(0, LX, 512):
                w = min(512, LX - c0)
                for d in range(2):
                    pr = banks.next()
                    mm(pr.t[:, :w], WBD[:, (0 * 2 + d) * 4 + cb, :], XCB.t[:, c0:c0 + w], True, True, [bWBD, XCB.b], [pr.b])
                    pi = banks.next()
                    mm(pi.t[:, :w], WBD[:, (1 * 2 + d) * 4 + cb, :], XCB.t[:, c0:c0 + w], True, True, [bWBD, XCB.b], [pi.b])
                    ci = d * 4 + cb
                    act(RAd[d].t[:, c0:c0 + w], pr.t[:, :w], AF.Sigmoid, [pr.b, bVEC], [RAd[d].b], bias=VEC[:, R_BA + ci:R_BA + ci + 1])
                    act(IBd[d].t[:, c0:c0 + w], pi.t[:, :w], AF.Sigmoid, [pi.b, bVEC], [IBd[d].b], bias=VEC[:, R_BI + ci:R_BI + ci + 1])
            for d in range(2):
                ci = d * 4 + cb
                act(RAd[d].t[:, 0:LX], RAd[d].t[:, 0:LX], AF.Exp, [RAd[d].b, bCOLS], [RAd[d].b], scale=col(C_CP, ci))
                tt("dve", TMd[d].t[:, 0:LX], RAd[d].t[:, 0:LX], RAd[d].t[:, 0:LX], ALU.mult, [RAd[d].b], [TMd[d].b])
            for d in range(2):
                act(TMd[d].t[:, 0:LX], TMd[d].t[:, 0:LX], AF.Sqrt, [TMd[d].b, bCOLS], [TMd[d].b], scale=-1.0, bias=col(C_ONE))
            for d in range(2):
                tt(MULENG, IBd[d].t[:, 0:LX], IBd[d].t[:, 0:LX], TMd[d].t[:, 0:LX], ALU.mult, [IBd[d].b, TMd[d].b], [IBd[d].b])
                tt(MULENG, IBd[d].t[:, 0:LX], IBd[d].t[:, 0:LX], XC.t[:, 0:LX], ALU.mult, [IBd[d].b, XC.b], [IBd[d].b])
            HF, HB = TMd[0], TMd[1]
            S.strict = True
            S.add("dve", lambda e: e.tensor_tensor_scan(out=HF.t[:, C0:C0 + TC], data0=RAd[0].t[:, C0:C0 + TC],
                  data1=IBd[0].t[:, C0:C0 + TC], initial=0.0, op0=ALU.mult, op1=ALU.add), [RAd[0].b, IBd[0].b], [HF.b])
            S.add("dve", lambda e: e.tensor_tensor_scan(out=HB.t[:, C0:C0 + TC][:, ::-1],
                  data0=RAd[1].t[:, C0:C0 + TC][:, ::-1], data1=IBd[1].t[:, C0:C0 + TC][:, ::-1], initial=0.0,
                  op0=ALU.mult, op1=ALU.add), [RAd[1].b, IBd[1].b], [HB.b])
            S.add("dve", lambda e: e.tensor_tensor_scan(out=HF.t[:, L0:L0 + T], data0=RAd[0].t[:, L0:L0 + T],
                  data1=IBd[0].t[:, L0:L0 + T], initial=HF.t[:, C0 + TC - 1:C0 + TC], op0=ALU.mult, op1=ALU.add),
                  [RAd[0].b, IBd[0].b, HF.b], [HF.b])
            S.add("dve", lambda e: e.tensor_tensor_scan(out=HB.t[:, L0:L0 + T][:, ::-1],
                  data0=RAd[1].t[:, L0:L0 + T][:, ::-1], data1=IBd[1].t[:, L0:L0 + T][:, ::-1],
                  initial=HB.t[:, C0:C0 + 1], op0=ALU.mult, op1=ALU.add), [RAd[1].b, IBd[1].b, HB.b], [HB.b])
            tt("dve", HF.t[:, L0:L0 + T], HF.t[:, L0:L0 + T], HB.t[:, L0:L0 + T], ALU.add, [HF.b, HB.b], [HF.b])
            S.strict = False
            tt(MULENG, OUTc.t[:], HF.t[:, L0:L0 + T], GGc.t[:], ALU.mult, [HF.b, GGc.b], [OUTc.b])
            if cb + 1 < 4:
                dma("sp", GGc.t[:], GG[cb + 1], writes=[GGc.b])
            dma("sp", CAT[4 + cb], OUTc.t[:], reads=[OUTc.b])
        S.barrier()
        if stop_after == "B":
            finish_debug(None)
            return nc

        wout_v = wout_d.rearrange("(kh kl) c -> kl kh c", kl=128)
        for g in range(T // NT):
            t0 = g * NT
            n = NT
            hv = halves(n)
            for kc in range(8):
                for (h, h0, hw) in hv:
                    dma("sp", HT[:, kc, h0:h0 + hw], CAT[kc, :, t0 + h0:t0 + h0 + hw], writes=[bHT[kc][h]])
            for dc in range(8):
                dma("sp", XT[:, dc, :], X1[dc * 128:(dc + 1) * 128, t0:t0 + n], writes=bXT[dc])
            wbs = []
            for s in range(4):
                wb = ws_rot.next()
                dma("pool", wb.t[:], wout_v[:, :, s * 256:(s + 1) * 256], writes=[wb.b])
                wbs.append(wb)
            def wout_unit(h, h0, hw, dc, wbs=wbs):
                wb, jj = wbs[dc // 2], dc % 2
                p = banks.next()
                for kc in range(8):
                    mm(p.t[:, :hw], wb.t[:, kc, jj * 128:(jj + 1) * 128], HT[:, kc, h0:h0 + hw],
                       kc == 0, kc == 7, [wb.b, bHT[kc][h]], [p.b])
                xb = xtb(dc, h0, hw)
                stt(XT[:, dc, h0:h0 + hw], p.t[:, :hw], col(C_G2, dc), XT[:, dc, h0:h0 + hw],
                    ALU.mult, ALU.add, [p.b, bCOLS] + xb, xb)

            early = {}

            def filler2(n=n, early=early):
                early.update(mm1_early(1, n, slabs=(0, 1)))

            residual_ln(n, wout_unit, "ln2", filler=filler2)
            ffn_mm1(1, n, early=early)

            def out_tiles(tixs, t0=t0):
                for tix in tixs:
                    if tix < 2:
                        ob_t, ob_b = XIN[tix].t[:], [XIN[tix].b]
                    else:
                        ob_t, ob_b = HSTG[tix - 2]
                    for hb in range(2):
                        pb = banks.next()
                        for q in range(4):
                            dc = hb * 4 + q
                            tr(pb.t[:, q * 128:(q + 1) * 128], XT[:, dc, tix * 128:(tix + 1) * 128], IDF,
                               [bXT[dc][tix], bCST], [pb.b])
                        cp("act" if hb == 0 else "dve", ob_t[:, hb * 512:(hb + 1) * 512], pb.t[:], [pb.b], ob_b)
                    dma("sp", out_d[t0 + tix * 128:t0 + (tix + 1) * 128, :], ob_t, reads=ob_b, is_output=True)

            residual_ln(n, lambda h, h0, hw, dc: mm2_unit(h, h0, hw, dc, C_G3), "ln3",
                        filler=lambda: out_tiles(range(0, 4)))
            out_tiles(range(4, 8))
        S.emit()
    return nc


def _consts():
    p = np.arange(128, dtype=np.float32)
    ident = np.eye(128, dtype=np.float32)
    diff = p[None, :] - p[:, None]
    diffpos = np.maximum(diff, 0.0)
    diffneg = np.maximum(-diff, 0.0)
    irow = np.concatenate([np.tile((p + 1.0)[None, :], (128, 1)), np.tile((128.0 - p)[None, :], (128, 1))], 1)
    pcolx = np.concatenate([np.tile((127.0 - p)[:, None], (1, 4)), np.tile(p[:, None], (1, 4))], 1)
    cst = np.concatenate([ident, diffpos, diffneg, irow, pcolx], 1).astype(np.float32)
    n = 32
    inv = (np.float32(10000.0) ** (-np.arange(n, dtype=np.float32) / np.float32(n))).astype(np.float32)
    t = np.arange(T)
    rows = (t // 64).astype(np.float32)
    cols = (t % 64).astype(np.float32)
    d = np.arange(128)
    f = d % 32
    pos = np.where((d // 64)[:, None] == 0, rows[None, :], cols[None, :]).astype(np.float32)
    ang = (pos * inv[f][:, None]).astype(np.float32)
    c = np.cos(ang).astype(np.float32)
    s = np.sin(ang).astype(np.float32)
    half = (d % 64) // 32
    ss = np.where(half[:, None] == 0, -s, s).astype(np.float32)
    return cst, c.astype(ml_dtypes.bfloat16), ss.astype(ml_dtypes.bfloat16)


def _small(b, c, c_ctx, b_ada, ln_g, ln_b, ret_norm_g, ret_norm_b, lru_conv_w, lru_conv_b, lru_b_a, lru_b_i, lru_lambda):
    sm = np.zeros((256, 128), np.float32)
    sm[R_BADA:R_BADA + 72] = b_ada[0].reshape(72, 128)
    sm[R_C:R_C + 8] = c[b].reshape(8, 128)
    sm[R_CCTX:R_CCTX + 8] = c_ctx.reshape(8, 128)
    sm[R_LNG:R_LNG + 24] = ln_g[0].reshape(24, 128)
    sm[R_LNB:R_LNB + 24] = ln_b[0].reshape(24, 128)
    sm[R_RNG:R_RNG + 4] = ret_norm_g[0].reshape(4, 128)
    sm[R_RNB:R_RNB + 4] = ret_norm_b[0].reshape(4, 128)
    sm[R_CW:R_CW + 16] = lru_conv_w[0].reshape(16, 128)
    sm[R_CB:R_CB + 4] = lru_conv_b[0].reshape(4, 128)
    sm[R_BA:R_BA + 8] = lru_b_a[0].reshape(8, 128)
    sm[R_BI:R_BI + 8] = lru_b_i[0].reshape(8, 128)
    sm[R_LAM:R_LAM + 8] = lru_lambda[0].reshape(8, 128)
    return sm


def make_in_maps(x, c, ctx, c_ctx, w_ada, b_ada, ffn1_w_gate, ffn1_w_up, ffn1_w_down,
                 ffn2_w_gate, ffn2_w_up, ffn2_w_down, w_in, w_out, ret_decay_logit,
                 ret_norm_g, ret_norm_b, lru_conv_w, lru_conv_b, lru_w_a, lru_b_a,
                 lru_w_i, lru_b_i, lru_lambda, ln_g, ln_b):
    f = lambda a: np.ascontiguousarray(np.asarray(a, dtype=np.float32))
    cst, rc, rs = _consts()
    shared = {
        "w_ada": f(w_ada[0]), "wg1": f(ffn1_w_gate[0]), "wu1": f(ffn1_w_up[0]), "wd1": f(ffn1_w_down[0]),
        "wg2": f(ffn2_w_gate[0]), "wu2": f(ffn2_w_up[0]), "wd2": f(ffn2_w_down[0]),
        "w_in": f(w_in[0]), "w_out": f(w_out[0]),
        "lru_wa": f(np.asarray(lru_w_a[0]).reshape(16, 64, 64)), "lru_wi": f(np.asarray(lru_w_i[0]).reshape(16, 64, 64)),
        "logit": f(np.asarray(ret_decay_logit[0]).reshape(1, 8)),
        "cst": cst, "rope_c": rc, "rope_s": rs,
    }
    args = [np.asarray(a) for a in (c, c_ctx, b_ada, ln_g, ln_b, ret_norm_g, ret_norm_b, lru_conv_w, lru_conv_b,
                                    lru_b_a, lru_b_i, lru_lambda)]
    in_maps = []
    for b in range(8):
        m = dict(shared)
        m["x"] = f(x[b])
        m["ctx"] = f(ctx[b])
        m["small"] = _small(b, args[0], args[1], *args[2:])
        in_maps.append(m)
    return in_maps


def kernel(**inputs):
    in_maps = make_in_maps(**inputs)
    nc = build_nc()
    res = run_bass_kernel_spmd(nc, in_maps, core_ids=list(range(8)))
    return np.stack([np.asarray(r["out"], dtype=np.float32) for r in res.results], 0)
```

```python
import contextlib
import numpy as np
import ml_dtypes
import concourse.bass as bass
import concourse.mybir as mybir
from concourse.bass_utils import run_bass_kernel_spmd

F32 = mybir.dt.float32
BF16 = mybir.dt.bfloat16
AF = mybir.ActivationFunctionType
ALU = mybir.AluOpType

T = 4096
TC = 256
TOT = T + TC
D = 1024
FF = 2816
NFC = 22
NDC = 8
NT = 1024
MULENG = "dve"
ALPHA = 2.0 ** 0.25
LN_EPS = 1e-5
K_SCALE = 128.0 ** -0.5
ENGS = ("pe", "act", "dve", "pool", "sp")
SB_BASE = 16640
SB_END = 229376


class Buf:
    __slots__ = ("name", "last_w", "readers")

    def __init__(self, name=""):
        self.name = name
        self.last_w = None
        self.readers = []


class Op:
    __slots__ = ("idx", "eng", "fn", "is_dma", "deps", "need_inc", "sem", "val", "dma_prev")

    def __init__(self, idx, eng, fn, is_dma):
        self.idx = idx
        self.eng = eng
        self.fn = fn
        self.is_dma = is_dma
        self.deps = set()
        self.need_inc = False
        self.sem = None
        self.val = 0
        self.dma_prev = 0


class Sched:
    def __init__(self, nc, n_dma_sems=8):
        self.nc = nc
        self.ops = []
        self.n_dma_sems = n_dma_sems
        self.out_dma_ops = []
        self.last_compute = {}
        self.dmas_since_barrier = []
        self.pending_barrier = {}
        self.strict = False

    def barrier(self):
        deps = set(self.last_compute.values()) | set(self.dmas_since_barrier)
        for e in ENGS:
            self.pending_barrier[e] = set(deps) | self.pending_barrier.get(e, set())
        self.dmas_since_barrier = []

    def add(self, eng, fn, reads=(), writes=(), dma=False, is_output=False):
        op = Op(len(self.ops), eng, fn, dma)
        ops = self.ops
        for b in reads:
            if b.last_w is not None:
                op.deps.add(b.last_w)
        for b in writes:
            if b.last_w is not None:
                op.deps.add(b.last_w)
            for r in b.readers:
                op.deps.add(r)
        pb = self.pending_barrier.pop(eng, None)
        if pb:
            op.deps |= pb
        pr = set()
        for d in op.deps:
            p = ops[d]
            if (not p.is_dma) and (not dma) and p.eng == eng and (eng == "pe" or not self.strict):
                continue
            pr.add(d)
        op.deps = pr
        for b in reads:
            if not dma:
                b.readers = [r for r in b.readers if ops[r].is_dma or ops[r].eng != eng]
            b.readers.append(op.idx)
        for b in writes:
            b.last_w = op.idx
            b.readers = []
        ops.append(op)
        if dma:
            self.dmas_since_barrier.append(op.idx)
        else:
            self.last_compute[eng] = op.idx
        if is_output:
            self.out_dma_ops.append(op.idx)
        return op

    def emit(self):
        nc = self.nc
        ops = self.ops
        for op in ops:
            for d in op.deps:
                ops[d].need_inc = True
        for d in self.out_dma_ops:
            ops[d].need_inc = True
        with contextlib.ExitStack() as st:
            eng_sem = {e: st.enter_context(nc.semaphore("s_" + e)) for e in ENGS}
            dma_sems = {e: [st.enter_context(nc.semaphore("d_%s%d" % (e, i))) for i in range(self.n_dma_sems)]
                        for e in ("sp", "pool", "act")}
            cnt = {e: 0 for e in ENGS}
            dcnt = {e: [0] * self.n_dma_sems for e in dma_sems}
            drr = {e: 0 for e in dma_sems}
            for op in ops:
                if op.is_dma:
                    k = drr[op.eng]
                    drr[op.eng] = (k + 1) % self.n_dma_sems
                    op.sem = dma_sems[op.eng][k]
                    op.dma_prev = dcnt[op.eng][k]
                    dcnt[op.eng][k] += 16
                    op.val = dcnt[op.eng][k]
                elif op.need_inc:
                    cnt[op.eng] += 1
                    op.sem = eng_sem[op.eng]
                    op.val = cnt[op.eng]
            block = st.enter_context(nc.Block())
            by_eng = {e: [op for op in ops if op.eng == e] for e in ENGS}
            final_waits = {}
            for d in self.out_dma_ops:
                key = id(ops[d].sem)
                if final_waits.get(key, (None, 0))[1] < ops[d].val:
                    final_waits[key] = (ops[d].sem, ops[d].val)

            def run(e, engine):
                waited = {}
                for op in by_eng[e]:
                    need = {}
                    for d in op.deps:
                        p = ops[d]
                        key = id(p.sem)
                        if need.get(key, (None, 0))[1] < p.val:
                            need[key] = (p.sem, p.val)
                    if op.is_dma and op.dma_prev > 0:
                        key = id(op.sem)
                        if need.get(key, (None, 0))[1] < op.dma_prev:
                            need[key] = (op.sem, op.dma_prev)
                    for key, (sem, val) in need.items():
                        if waited.get(key, 0) < val:
                            engine.wait_ge(sem, val)
                            waited[key] = val
                    ins = op.fn(engine)
                    if op.is_dma:
                        ins.then_inc(op.sem, 16)
                    elif op.need_inc:
                        ins.then_inc(op.sem, 1)
                if e == "sp":
                    for sem, val in final_waits.values():
                        if waited.get(id(sem), 0) < val:
                            engine.wait_ge(sem, val)

            @block.tensor
            def _(eng):
                run("pe", eng)

            @block.scalar
            def _(eng):
                run("act", eng)

            @block.vector
            def _(eng):
                run("dve", eng)

            @block.gpsimd
            def _(eng):
                run("pool", eng)

            @block.sync
            def _(eng):
                run("sp", eng)


class TB:
    __slots__ = ("t", "b")

    def __init__(self, t, b):
        self.t = t
        self.b = b


class Rot:
    def __init__(self, items):
        self.items = items
        self.i = 0

    def next(self):
        it = self.items[self.i % len(self.items)]
        self.i += 1
        return it


R_BADA, R_C, R_CCTX, R_LNG = 0, 72, 80, 88
R_LNB, R_RNG, R_RNB, R_CW, R_CB, R_BA, R_BI, R_LAM = 128, 152, 156, 160, 176, 180, 188, 196
(C_SC1L, C_SH1L, C_G1L, C_SC1C, C_SH1C, C_G1C, C_GU1L, C_BU1L, C_GU1C, C_BU1C, C_G2, C_GU2, C_BU2, C_G3,
 C_TMP, C_LG, C_KDEC, C_GC, C_CP, C_CP2, C_EPS, C_TMP2, C_EPSH, C_ONE) = [8 * i for i in range(24)]


def build_nc(stop_after=None):
    nc = bass.Bass("TRN2", target_bir_lowering=False)

    def din(name, shape, dt=F32):
        return nc.dram_tensor(name, shape, dt, kind="ExternalInput").ap()

    x_d = din("x", [T, D])
    ctx_d = din("ctx", [TC, D])
    small_d = din("small", [256, 128])
    wada_d = din("w_ada", [D, 9 * D])
    wg_d = [din("wg1", [D, FF]), din("wg2", [D, FF])]
    wu_d = [din("wu1", [D, FF]), din("wu2", [D, FF])]
    wd_d = [din("wd1", [FF, D]), din("wd2", [FF, D])]
    win_d = din("w_in", [D, 3 * D])
    wout_d = din("w_out", [D, D])
    lwa_d = din("lru_wa", [16, 64, 64])
    lwi_d = din("lru_wi", [16, 64, 64])
    logit_d = din("logit", [1, 8])
    cst_d = din("cst", [128, 648])
    ropec_d = din("rope_c", [128, T], BF16)
    ropes_d = din("rope_s", [128, T], BF16)
    out_d = nc.dram_tensor("out", [T, D], F32, kind="ExternalOutput").ap()
    okind = "ExternalOutput" if stop_after else "Internal"
    X1 = nc.dram_tensor("X1", [D, T], F32, kind=okind).ap()
    KT = nc.dram_tensor("KT", [4, 128, TOT], BF16, kind=okind).ap()
    QT = nc.dram_tensor("QT", [4, 128, T], BF16, kind=okind).ap()
    VTM = nc.dram_tensor("VTM", [4, 128, TOT // 128, 128], BF16, kind=okind).ap()
    XS = nc.dram_tensor("XS", [4, 128, TOT], F32, kind=okind).ap()
    GS = nc.dram_tensor("GS", [4, 128, T], BF16, kind=okind).ap()
    GG = nc.dram_tensor("GG", [4, 128, T], BF16, kind=okind).ap()
    CAT = nc.dram_tensor("CAT", [8, 128, T], BF16, kind=okind).ap()

    S = Sched(nc)
    off = [SB_BASE]

    def sb(name, shape, dt):
        n = int(np.prod(shape[1:])) * (4 if dt == F32 else 2)
        n = (n + 63) // 64 * 64
        assert off[0] + n <= SB_END, ("SBUF overflow", name, off[0] + n)
        t = nc.alloc_sbuf_tensor_at(name, list(shape), dt, offset=off[0])
        off[0] += n
        return t

    with contextlib.ExitStack() as st:
        PS = [TB(st.enter_context(nc.psum_tensor("ps%d" % i, [128, 512], F32)), Buf("ps%d" % i)) for i in range(8)]
        banks = Rot(PS)

        def mm(out, lhsT, rhs, start, stop, reads, writes):
            S.add("pe", lambda e: e.matmul(out, lhsT=lhsT, rhs=rhs, start=start, stop=stop), reads, writes)

        def tr(out, in_, ident, reads, writes):
            S.add("pe", lambda e: e.transpose(out, in_, ident), reads, writes)

        def act(out, in_, func, reads, writes, scale=None, bias=None):
            kw = {}
            if scale is not None:
                kw["scale"] = scale
            if bias is not None:
                kw["bias"] = bias
            S.add("act", lambda e: e.activation(out=out, in_=in_, func=func, **kw), reads, writes)

        def tt(eng, out, in0, in1, op, reads, writes):
            S.add(eng, lambda e: e.tensor_tensor(out=out, in0=in0, in1=in1, op=op), reads, writes)

        def ts(eng, out, in0, s1, s2, op0, op1, reads, writes):
            if s2 is None:
                S.add(eng, lambda e: e.tensor_scalar(out=out, in0=in0, scalar1=s1, scalar2=None, op0=op0), reads, writes)
            else:
                S.add(eng, lambda e: e.tensor_scalar(out=out, in0=in0, scalar1=s1, scalar2=s2, op0=op0, op1=op1), reads, writes)

        def stt(out, in0, scalar, in1, op0, op1, reads, writes):
            S.add("dve", lambda e: e.scalar_tensor_tensor(out=out, in0=in0, scalar=scalar, in1=in1, op0=op0, op1=op1), reads, writes)

        def cp(eng, out, in_, reads, writes):
            if eng == "act":
                act(out, in_, AF.Identity, reads, writes)
            else:
                S.add(eng, lambda e: e.tensor_copy(out=out, in_=in_), reads, writes)

        def dma(eng, out, in_, reads=(), writes=(), is_output=False):
            S.add(eng, lambda e: e.dma_start(out=out, in_=in_), reads, writes, dma=True, is_output=is_output)

        dbg_outs = {}

        def dbg(name, ap, shape, dt, reads):
            if not stop_after:
                return
            o = nc.dram_tensor("DBG_" + name, list(shape), dt, kind="ExternalOutput").ap()
            dma("sp", o[:], ap, reads=reads)

        VEC = sb("VEC", [128, 256], F32)
        MOD = sb("MOD", [128, 72, 2], F32)
        COLS = sb("COLS", [128, 192], F32)
        CST = sb("CST", [128, 648], F32)
        IDF = CST[:, 0:128]
        DIFFPOS = CST[:, 128:256]
        DIFFNEG = CST[:, 256:384]
        IROW = CST[:, 384:640]
        PCOLX = CST[:, 640:648]
        IDB = sb("IDB", [128, 128], BF16)
        ONES_D = sb("ONES_D", [128, 128], BF16)
        ONES_H = sb("ONES_H", [128, 128], BF16)
        DM = sb("DM", [128, 4, 128], F32)
        QDEC = sb("QDEC", [128, 8, 128], F32)
        WBD = sb("WBD", [128, 16, 128], BF16)
        SC = sb("SC", [128, 16], BF16)
        PERM = sb("PERM", [128, 128], BF16)
        bDG = Buf("DG")
        bVEC, bMOD, bCOLS, bCST, bIDB, bONES, bDM, bQDEC, bWBD, bSC = [Buf(n) for n in
            ["VEC", "MOD", "COLS", "CST", "IDB", "ONES", "DM", "QDEC", "WBD", "SC"]]
        phase_base = off[0]

        def col(c, i=0, w=1):
            return COLS[:, c + i:c + i + w]

        XT = sb("XT", [128, 8, NT], F32)
        UT = sb("UT", [128, 8, NT], BF16)
        HT = sb("HT", [128, NFC, NT], BF16)
        WD = sb("WD", [128, NFC, D], BF16)
        WS = [TB(sb("WS%d" % i, [128, 8, 256], BF16), Buf("WS%d" % i)) for i in range(4)]
        XIN = [TB(sb("XIN%d" % i, [128, D], F32), Buf("XIN%d" % i)) for i in range(2)]
        TMPF = [TB(sb("TMPF%d" % i, [128, 512], F32), Buf("TMPF%d" % i)) for i in range(4)]
        MEANS = [TB(sb("MEAN%d" % i, [128, 512], F32), Buf("MEAN%d" % i)) for i in range(2)]
        RSTDS = [TB(sb("RSTD%d" % i, [128, 512], F32), Buf("RSTD%d" % i)) for i in range(2)]
        STG = [TB(sb("STG%d" % i, [128, 512], BF16), Buf("STG%d" % i)) for i in range(4)]
        PB16 = Rot([TB(sb("PB16_%d" % i, [128, 512], BF16), Buf("PB16_%d" % i)) for i in range(3)])
        RC = TB(sb("RC", [128, NT], BF16), Buf("RC"))
        RS = TB(sb("RS", [128, NT], BF16), Buf("RS"))
        ac_end = off[0]
        ws_rot, xin_rot, tmpf_rot, stg_rot = Rot(WS), Rot(XIN), Rot(TMPF), Rot(STG)
        mean_rot, rstd_rot = Rot(MEANS), Rot(RSTDS)
        bXT = [[Buf("XT%d_%d" % (dc, t)) for t in range(8)] for dc in range(8)]
        bUT = [[Buf("UT%d_%d" % (dc, h)) for h in range(2)] for dc in range(8)]
        bHT = [[Buf("HT%d_%d" % (fc, h)) for h in range(2)] for fc in range(NFC)]
        bWD = [Buf("WD%d" % s) for s in range(11)]

        HSTG = []
        for i in range(6):
            fc = 8 + 2 * i
            t_ = HT[:, fc:fc + 2, :].rearrange("p a n -> p (a n)").bitcast(F32)
            HSTG.append((t_, [bHT[fc][0], bHT[fc][1], bHT[fc + 1][0], bHT[fc + 1][1]]))

        def xtb(dc, h0, hw):
            return [bXT[dc][t] for t in range(h0 // 128, (h0 + hw + 127) // 128)]

        def halves(n):
            return [(h, h * 512, min(512, n - h * 512)) for h in range((n + 511) // 512)]

        S.strict = True
        dma("sp", CST[:], cst_d[:], writes=[bCST])
        dma("sp", COLS[:, C_LG:C_LG + 8], logit_d.partition_broadcast(128), writes=[bCOLS])
        sm0, sm1 = XIN[0], XIN[1]
        dma("sp", sm0.t[:, 0:128], small_d[0:128, :], writes=[sm0.b])
        dma("sp", sm1.t[:, 0:128], small_d[128:256, :], writes=[sm1.b])
        for i, smx in enumerate((sm0, sm1)):
            pb = banks.next()
            tr(pb.t[:, 0:128], smx.t[:, 0:128], IDF, [smx.b, bCST], [pb.b])
            cp("dve", VEC[:, i * 128:(i + 1) * 128], pb.t[:, 0:128], [pb.b], [bVEC])
        cp("dve", IDB[:], IDF, [bCST], [bIDB])
        for (d0, s0) in ((0, 32), (32, 0), (64, 96), (96, 64)):
            cp("dve", PERM[:, d0:d0 + 32], IDF[:, s0:s0 + 32], [bCST], [bIDB])
        S.add("dve", lambda e: e.memset(ONES_D[:], 1.0 / D), (), [bONES])
        S.add("dve", lambda e: e.memset(ONES_H[:], 1.0 / 128.0), (), [bONES])
        S.add("dve", lambda e: e.memset(COLS[:, C_EPS:C_EPS + 8], LN_EPS / (ALPHA * ALPHA)), (), [bCOLS])
        S.add("dve", lambda e: e.memset(COLS[:, C_EPSH:C_EPSH + 8], LN_EPS), (), [bCOLS])
        S.add("dve", lambda e: e.memset(COLS[:, C_ONE:C_ONE + 8], 1.0), (), [bCOLS])
        act(SC[:], VEC[:, R_C:R_C + 16], AF.Silu, [bVEC], [bSC])
        wada_v = wada_d.rearrange("(kh kl) c -> kl kh c", kl=128)
        pmod = banks.next()
        WA = [TB(HT[:, 0:8, :], bHT[0][0]), TB(HT[:, 8:16, :], bHT[8][0])]
        for i in range(9):
            wa = WA[i % 2]
            dma("pool", wa.t, wada_v[:, :, i * 1024:(i + 1) * 1024], writes=[wa.b])
            for c in range(8):
                for kh in range(8):
                    o = (i * 8 + c) * 2
                    mm(pmod.t[:, o:o + 2], wa.t[:, kh, c * 128:(c + 1) * 128], SC[:, kh:16:8],
                       kh == 0, kh == 7, [wa.b, bSC], [pmod.b])
        tt("dve", MOD[:], pmod.t[:, 0:144].rearrange("p (r j) -> p r j", j=2),
           VEC[:, 0:72, None].broadcast_to([128, 72, 2]), ALU.add, [pmod.b, bVEC], [bMOD])

        def modv(i, j):
            return MOD[:, i * 8:(i + 1) * 8, j]

        def lng(l):
            return VEC[:, R_LNG + l * 8:R_LNG + l * 8 + 8]

        def lnb(l):
            return VEC[:, R_LNB + l * 8:R_LNB + l * 8 + 8]

        RW = [bMOD, bVEC, bCOLS]
        for j, (csc, csh, cg, cgu, cbu) in enumerate([(C_SC1L, C_SH1L, C_G1L, C_GU1L, C_BU1L),
                                                       (C_SC1C, C_SH1C, C_G1C, C_GU1C, C_BU1C)]):
            ts("dve", col(csc, 0, 8), modv(1, j), 1.0, None, ALU.add, None, RW, [bCOLS])
            cp("dve", col(csh, 0, 8), modv(0, j), RW, [bCOLS])
            ts("dve", col(cg, 0, 8), modv(2, j), 0.5 / ALPHA, None, ALU.mult, None, RW, [bCOLS])
            ts("dve", col(C_TMP, 0, 8), modv(4, j), 1.0, None, ALU.add, None, RW, [bCOLS])
            tt("dve", col(cgu, 0, 8), col(C_TMP, 0, 8), lng(0), ALU.mult, RW, [bCOLS])
            tt("dve", col(C_TMP2, 0, 8), col(C_TMP, 0, 8), lnb(0), ALU.mult, RW, [bCOLS])
            tt("dve", col(cbu, 0, 8), col(C_TMP2, 0, 8), modv(3, j), ALU.add, RW, [bCOLS])
        ts("dve", col(C_G2, 0, 8), modv(5, 0), 1.0 / ALPHA, None, ALU.mult, None, RW, [bCOLS])
        ts("dve", col(C_TMP, 0, 8), modv(7, 0), 1.0, None, ALU.add, None, RW, [bCOLS])
        tt("dve", col(C_GU2, 0, 8), col(C_TMP, 0, 8), lng(1), ALU.mult, RW, [bCOLS])
        tt("dve", col(C_TMP2, 0, 8), col(C_TMP, 0, 8), lnb(1), ALU.mult, RW, [bCOLS])
        tt("dve", col(C_BU2, 0, 8), col(C_TMP2, 0, 8), modv(6, 0), ALU.add, RW, [bCOLS])
        ts("dve", col(C_G3, 0, 8), modv(8, 0), 0.5 / ALPHA, None, ALU.mult, None, RW, [bCOLS])

        act(col(C_LG, 0, 8), col(C_LG, 0, 8), AF.Exp, [bCOLS], [bCOLS], scale=-1.0)
        ts("dve", col(C_LG, 0, 8), col(C_LG, 0, 8), 1.0, None, ALU.add, None, [bCOLS], [bCOLS])
        act(col(C_LG, 0, 8), col(C_LG, 0, 8), AF.Ln, [bCOLS], [bCOLS])
        ts("dve", col(C_LG, 0, 8), col(C_LG, 0, 8), -1.0, None, ALU.mult, None, [bCOLS], [bCOLS])
        tt("dve", col(C_KDEC, 0, 8), col(C_LG, 0, 8), PCOLX, ALU.mult, [bCOLS, bCST], [bCOLS])
        act(col(C_KDEC, 0, 8), col(C_KDEC, 0, 8), AF.Exp, [bCOLS], [bCOLS])
        ts("dve", col(C_KDEC, 0, 8), col(C_KDEC, 0, 8), K_SCALE, None, ALU.mult, None, [bCOLS], [bCOLS])
        act(col(C_GC, 0, 8), col(C_LG, 0, 8), AF.Exp, [bCOLS], [bCOLS], scale=128.0)
        for h in range(4):
            ts("dve", DM[:, h, :], DIFFPOS, col(C_LG, h), None, ALU.mult, None, [bCOLS, bCST], [bDM])
            stt(DM[:, h, :], DIFFNEG, col(C_LG, 4 + h), DM[:, h, :], ALU.mult, ALU.add, [bCOLS, bCST, bDM], [bDM])
            act(DM[:, h, :], DM[:, h, :], AF.Exp, [bDM], [bDM])
            ts("dve", DM[:, h, :], DM[:, h, :], K_SCALE, None, ALU.mult, None, [bDM], [bDM])
            for d in range(2):
                ts("dve", QDEC[:, d * 4 + h, :], IROW[:, d * 128:(d + 1) * 128], col(C_LG, d * 4 + h), None,
                   ALU.mult, None, [bCOLS, bCST], [bQDEC])
                act(QDEC[:, d * 4 + h, :], QDEC[:, d * 4 + h, :], AF.Exp, [bQDEC], [bQDEC])
        act(col(C_CP, 0, 8), VEC[:, R_LAM:R_LAM + 8], AF.Exp, [bVEC], [bCOLS], scale=-1.0)
        ts("dve", col(C_CP, 0, 8), col(C_CP, 0, 8), 1.0, None, ALU.add, None, [bCOLS], [bCOLS])
        act(col(C_CP, 0, 8), col(C_CP, 0, 8), AF.Ln, [bCOLS], [bCOLS])
        ts("dve", col(C_CP2, 0, 8), col(C_CP, 0, 8), -16.0, None, ALU.mult, None, [bCOLS], [bCOLS])
        ts("dve", col(C_CP, 0, 8), col(C_CP, 0, 8), -8.0, None, ALU.mult, None, [bCOLS], [bCOLS])
        S.add("dve", lambda e: e.memset(WBD[:], 0.0), (), [bWBD])
        for g, wsrc in enumerate((lwa_d, lwi_d)):
            for d in range(2):
                for cb in range(4):
                    for n in range(2):
                        dma("pool", WBD[64 * n:64 * n + 64, (g * 2 + d) * 4 + cb, 64 * n:64 * n + 64],
                            wsrc[d * 8 + 2 * cb + n], writes=[bWBD])

        S.strict = False
        def mm1_load(which, s):
            wgv = wg_d[which].rearrange("(kh kl) c -> kl kh c", kl=128)
            wuv = wu_d[which].rearrange("(kh kl) c -> kl kh c", kl=128)
            wdv = wd_d[which].rearrange("(j fl) c -> fl j c", fl=128)
            bg = ws_rot.next()
            dma("pool", bg.t[:], wgv[:, :, s * 256:(s + 1) * 256], writes=[bg.b])
            bu = ws_rot.next()
            dma("pool", bu.t[:], wuv[:, :, s * 256:(s + 1) * 256], writes=[bu.b])
            dma("pool", WD[:, 2 * s:2 * s + 2, :], wdv[:, 2 * s:2 * s + 2, :], writes=[bWD[s]])
            return bg, bu

        def mm1_unit(bg, bu, s, j, h, h0, hw):
            fc = 2 * s + j
            pg = banks.next()
            for k in range(8):
                mm(pg.t[:, :hw], bg.t[:, k, j * 128:(j + 1) * 128], UT[:, k, h0:h0 + hw],
                   k == 0, k == 7, [bg.b, bUT[k][h]], [pg.b])
            pu = banks.next()
            for k in range(8):
                mm(pu.t[:, :hw], bu.t[:, k, j * 128:(j + 1) * 128], UT[:, k, h0:h0 + hw],
                   k == 0, k == 7, [bu.b, bUT[k][h]], [pu.b])
            sg = tmpf_rot.next()
            act(sg.t[:, :hw], pg.t[:, :hw], AF.Silu, [pg.b], [sg.b])
            tt("dve", HT[:, fc, h0:h0 + hw], sg.t[:, :hw], pu.t[:, :hw], ALU.mult,
               [sg.b, pu.b], [bHT[fc][h]])

        def ffn_mm1(which, n, early=None):
            hv = halves(n)
            early = early or {}
            for s in sorted(early):
                bg, bu = early[s]
                for j in range(2):
                    for (h, h0, hw) in hv[1:]:
                        mm1_unit(bg, bu, s, j, h, h0, hw)
            for s in range(11):
                if s in early:
                    continue
                bg, bu = mm1_load(which, s)
                for j in range(2):
                    for (h, h0, hw) in hv:
                        mm1_unit(bg, bu, s, j, h, h0, hw)

        def mm1_early(which, n, slabs=(0, 1)):
            (h, h0, hw) = halves(n)[0]
            early = {}
            for s in slabs:
                bg, bu = mm1_load(which, s)
                early[s] = (bg, bu)
                for j in range(2):
                    mm1_unit(bg, bu, s, j, h, h0, hw)
            return early

        def mm2_unit(h, h0, hw, dc, gcolbase):
            pf = banks.next()
            for fc in range(NFC):
                mm(pf.t[:, :hw], WD[:, fc, dc * 128:(dc + 1) * 128], HT[:, fc, h0:h0 + hw],
                   fc == 0, fc == NFC - 1, [bWD[fc // 2], bHT[fc][h]], [pf.b])
            xb = xtb(dc, h0, hw)
            stt(XT[:, dc, h0:h0 + hw], pf.t[:, :hw], col(gcolbase, dc), XT[:, dc, h0:h0 + hw],
                ALU.mult, ALU.add, [pf.b, bCOLS] + xb, xb)

        def residual_ln(n, unit_fn, mode, j=0, t0=0, filler=None, after_mm=None):
            hv = halves(n)
            if len(hv) == 1:
                (h, h0, hw) = hv[0]
                for dc in range(8):
                    unit_fn(h, h0, hw, dc)
                if after_mm is not None:
                    after_mm()
                ln_pre(h, h0, hw)
                pm, pq = ln_stats(h, h0, hw)
                ln_post(h, h0, hw, pm, pq, mode, j, t0)
                return
            (ha, a0, aw), (hb, b0, bw) = hv
            for dc in range(8):
                unit_fn(ha, a0, aw, dc)
            ln_pre(ha, a0, aw)
            for dc in range(4):
                unit_fn(hb, b0, bw, dc)
            pm, pq = ln_stats(ha, a0, aw)
            ln_post(ha, a0, aw, pm, pq, mode, j, t0)
            for dc in range(4, 8):
                unit_fn(hb, b0, bw, dc)
            if after_mm is not None:
                after_mm()
            ln_pre(hb, b0, bw)
            if filler is not None:
                filler()
            pm, pq = ln_stats(hb, b0, bw)
            ln_post(hb, b0, bw, pm, pq, mode, j, t0)

        def ln_pre(h, h0, hw):
            for dc in range(8):
                xb = xtb(dc, h0, hw)
                act(UT[:, dc, h0:h0 + hw], XT[:, dc, h0:h0 + hw], AF.Identity, xb, [bUT[dc][h]])
                act(HT[:, dc, h0:h0 + hw], XT[:, dc, h0:h0 + hw], AF.Square, xb, [bHT[dc][h]])

        def ln_stats(h, h0, hw):
            pm = banks.next()
            for dc in range(8):
                mm(pm.t[:, :hw], ONES_D[:], UT[:, dc, h0:h0 + hw], dc == 0, dc == 7, [bONES, bUT[dc][h]], [pm.b])
            pq = banks.next()
            for dc in range(8):
                mm(pq.t[:, :hw], ONES_D[:], HT[:, dc, h0:h0 + hw], dc == 0, dc == 7, [bONES, bHT[dc][h]], [pq.b])
            return pm, pq

        def ln_post(h, h0, hw, pm, pq, mode, j=0, t0=0):
            mean = mean_rot.next()
            rstd = rstd_rot.next()
            act(mean.t[:, :hw], pm.t[:, :hw], AF.Identity, [pm.b], [mean.b])
            act(rstd.t[:, :hw], pm.t[:, :hw], AF.Square, [pm.b], [rstd.b])
            tt("dve", rstd.t[:, :hw], pq.t[:, :hw], rstd.t[:, :hw], ALU.subtract, [pq.b, rstd.b], [rstd.b])
            act(rstd.t[:, :hw], rstd.t[:, :hw], AF.Ln, [rstd.b, bCOLS], [rstd.b], bias=col(C_EPS))
            act(rstd.t[:, :hw], rstd.t[:, :hw], AF.Exp, [rstd.b], [rstd.b], scale=-0.5)
            for dc in range(8):
                xb = xtb(dc, h0, hw)
                t1 = tmpf_rot.next()
                tt("dve", t1.t[:, :hw], XT[:, dc, h0:h0 + hw], mean.t[:, :hw], ALU.subtract, xb + [mean.b], [t1.b])
                tt("dve", t1.t[:, :hw], t1.t[:, :hw], rstd.t[:, :hw], ALU.mult, [t1.b, rstd.b], [t1.b])
                if mode == "ln1":
                    xs = tmpf_rot.next()
                    act(xs.t[:, :hw], t1.t[:, :hw], AF.Identity, [t1.b, bVEC], [xs.b],
                        scale=VEC[:, R_LNG + dc:R_LNG + dc + 1], bias=VEC[:, R_LNB + dc:R_LNB + dc + 1])
                    dma("sp", X1[dc * 128:(dc + 1) * 128, t0 + h0:t0 + h0 + hw], xs.t[:, :hw], reads=[xs.b])
                if mode in ("ln1", "ln1c"):
                    cgu, cbu = (C_GU1L, C_BU1L) if j == 0 else (C_GU1C, C_BU1C)
                    act(UT[:, dc, h0:h0 + hw], t1.t[:, :hw], AF.Identity, [t1.b, bCOLS], [bUT[dc][h]],
                        scale=col(cgu, dc), bias=col(cbu, dc))
                elif mode == "ln2":
                    act(XT[:, dc, h0:h0 + hw], t1.t[:, :hw], AF.Identity, [t1.b, bVEC], xb,
                        scale=VEC[:, R_LNG + 8 + dc:R_LNG + 9 + dc], bias=VEC[:, R_LNB + 8 + dc:R_LNB + 9 + dc])
                    act(UT[:, dc, h0:h0 + hw], t1.t[:, :hw], AF.Identity, [t1.b, bCOLS], [bUT[dc][h]],
                        scale=col(C_GU2, dc), bias=col(C_BU2, dc))
                elif mode == "ln3":
                    act(XT[:, dc, h0:h0 + hw], t1.t[:, :hw], AF.Identity, [t1.b, bVEC], xb,
                        scale=VEC[:, R_LNG + 16 + dc:R_LNG + 17 + dc], bias=VEC[:, R_LNB + 16 + dc:R_LNB + 17 + dc])

        def rope_evac(p, hw, h0, stg):
            pb = PB16.next()
            cp("act", pb.t[:, :hw], p.t[:, :hw], [p.b], [pb.b])
            q = banks.next()
            mm(q.t[:, :hw], PERM[:], pb.t[:, :hw], True, True, [bIDB, pb.b], [q.b])
            t1 = tmpf_rot.next()
            tt("dve", t1.t[:, :hw], pb.t[:, :hw], RC.t[:, h0:h0 + hw], ALU.mult, [pb.b, RC.b], [t1.b])
            t2 = tmpf_rot.next()
            tt("dve", t2.t[:, :hw], q.t[:, :hw], RS.t[:, h0:h0 + hw], ALU.mult, [q.b, RS.b], [t2.b])
            tt("dve", stg.t[:, :hw], t1.t[:, :hw], t2.t[:, :hw], ALU.add, [t1.b, t2.b], [stg.b])

        win_v = win_d.rearrange("(kh kl) c -> kl kh c", kl=128)
        groups = [("ctx", 0, TC)] + [("lat", g * NT, NT) for g in range(T // NT)]

        def stage_in_loads(kind, t0, n):
            src = x_d if kind == "lat" else ctx_d
            tiles = []
            for tix in range(n // 128):
                if tix < 2:
                    xb = XIN[tix]
                    t_, bl = xb.t[:], [xb.b]
                else:
                    t_, bl = HSTG[tix - 2]
                dma("sp", t_, src[t0 + tix * 128:t0 + (tix + 1) * 128, :], writes=bl)
                tiles.append((t_, bl))
            return tiles

        def stage_in(kind, t0, n, tiles):
            for tix, (t_, bl) in enumerate(tiles):
                for hb in range(2):
                    pb = banks.next()
                    for q in range(4):
                        tr(pb.t[:, q * 128:(q + 1) * 128], t_[:, (hb * 4 + q) * 128:(hb * 4 + q + 1) * 128], IDF,
                           bl + [bCST], [pb.b])
                    cp("act" if hb == 0 else "dve", XT[:, hb * 4:(hb + 1) * 4, tix * 128:(tix + 1) * 128],
                       pb.t[:].rearrange("p (q i) -> p q i", q=4), [pb.b], [bXT[hb * 4 + q][tix] for q in range(4)])

        def modulate_in(kind, n):
            csc, csh = (C_SC1L, C_SH1L) if kind == "lat" else (C_SC1C, C_SH1C)
            for dc in range(8):
                for (h, h0, hw) in halves(n):
                    act(UT[:, dc, h0:h0 + hw], XT[:, dc, h0:h0 + hw], AF.Identity, xtb(dc, h0, hw) + [bCOLS],
                        [bUT[dc][h]], scale=col(csc, dc), bias=col(csh, dc))

        def proj_unit(wb, s, jj, h, h0, hw, lat, col0, t0):
            pk = s // 2
            c = (s % 2) * 2 + jj
            p = banks.next()
            for k in range(8):
                mm(p.t[:, :hw], wb.t[:, k, jj * 128:(jj + 1) * 128], UT[:, k, h0:h0 + hw],
                   k == 0, k == 7, [wb.b, bUT[k][h]], [p.b])
            if pk == 2:
                xs = tmpf_rot.next()
                cp("act", xs.t[:, :hw], p.t[:, :hw], [p.b], [xs.b])
                dma("sp", XS[c, :, col0 + h0:col0 + h0 + hw], xs.t[:, :hw], reads=[xs.b])
                return
            sg = stg_rot.next()
            if pk == 0:
                if lat:
                    rope_evac(p, hw, h0, sg)
                else:
                    cp("act", sg.t[:, :hw], p.t[:, :hw], [p.b], [sg.b])
                dma("sp", KT[c, :, col0 + h0:col0 + h0 + hw], sg.t[:, :hw], reads=[sg.b])
            elif pk == 3:
                rope_evac(p, hw, h0, sg)
                dma("sp", QT[c, :, t0 + h0:t0 + h0 + hw], sg.t[:, :hw], reads=[sg.b])
            elif pk == 4:
                act(sg.t[:, :hw], p.t[:, :hw], AF.Silu, [p.b], [sg.b])
                dma("sp", GS[c, :, t0 + h0:t0 + h0 + hw], sg.t[:, :hw], reads=[sg.b])
            else:
                act(sg.t[:, :hw], p.t[:, :hw], AF.Gelu, [p.b], [sg.b])
                dma("sp", GG[c, :, t0 + h0:t0 + h0 + hw], sg.t[:, :hw], reads=[sg.b])

        def proj_load_pair(s0):
            wbp = []
            for s in (s0, s0 + 1):
                wb = ws_rot.next()
                dma("pool", wb.t[:], win_v[:, :, s * 256:(s + 1) * 256], writes=[wb.b])
                wbp.append(wb)
            return wbp

        def proj_pair_units(kind, t0, n, s0, wbp, hsel):
            lat = kind == "lat"
            col0 = TC + t0 if lat else 0
            if s0 // 2 == 1:
                for tix in range(n // 128):
                    h = tix // 4
                    if h not in hsel:
                        continue
                    for si, s in enumerate((s0, s0 + 1)):
                        wb = wbp[si]
                        pv = banks.next()
                        for k in range(8):
                            mm(pv.t[:, 0:256], UT[:, k, tix * 128:(tix + 1) * 128], wb.t[:, k, :],
                               k == 0, k == 7, [wb.b, bUT[k][h]], [pv.b])
                        sg = stg_rot.next()
                        cp("act" if (tix + si) % 2 == 0 else "dve", sg.t[:, 0:256], pv.t[:, 0:256], [pv.b], [sg.b])
                        for hh in range(2):
                            dma("sp", VTM[(s - 2) * 2 + hh, :, col0 // 128 + tix, :], sg.t[:, hh * 128:(hh + 1) * 128],
                                reads=[sg.b])
                return
            for (h, h0, hw) in halves(n):
                if h not in hsel:
                    continue
                for si, s in enumerate((s0, s0 + 1)):
                    for jj in range(2):
                        proj_unit(wbp[si], s, jj, h, h0, hw, lat, col0, t0)

        stage_in(*groups[0], stage_in_loads(*groups[0]))
        modulate_in(groups[0][0], groups[0][2])
        for gi, (kind, t0, n) in enumerate(groups):
            lat = kind == "lat"
            if lat:
                dma("sp", RC.t[:], ropec_d[:, t0:t0 + n], writes=[RC.b])
                dma("sp", RS.t[:], ropes_d[:, t0:t0 + n], writes=[RS.b])
            gcol = C_G1L if lat else C_G1C
            nxt = {}

            def prefetch_next(gi=gi, nxt=nxt):
                if gi + 1 < len(groups):
                    nxt["tiles"] = stage_in_loads(*groups[gi + 1])

            ffn_mm1(0, n)
            pair0 = {}

            def filler(kind=kind, t0=t0, n=n, pair0=pair0):
                pair0["wbp"] = proj_load_pair(0)
                proj_pair_units(kind, t0, n, 0, pair0["wbp"], (0,))

            residual_ln(n, lambda h, h0, hw, dc, gcol=gcol: mm2_unit(h, h0, hw, dc, gcol),
                        "ln1" if lat else "ln1c", j=0 if lat else 1, t0=t0, filler=filler if lat else None,
                        after_mm=prefetch_next)
            if lat and t0 == 0:
                dbg("UT", UT[:], [128, 8, NT], BF16, [b for r in bUT for b in r])
                dbg("COLS", COLS[:], [128, 192], F32, [bCOLS])
                dbg("MOD", MOD[:].rearrange("p r j -> p (r j)"), [128, 144], F32, [bMOD])
            if lat:
                proj_pair_units(kind, t0, n, 0, pair0["wbp"], (1,))
            else:
                proj_pair_units(kind, t0, n, 0, proj_load_pair(0), (0,))
            if gi + 1 < len(groups):
                stage_in(*groups[gi + 1], nxt["tiles"])
            for s0 in range(2, 12 if lat else 6, 2):
                proj_pair_units(kind, t0, n, s0, proj_load_pair(s0), (0, 1))
            if gi + 1 < len(groups):
                modulate_in(groups[gi + 1][0], groups[gi + 1][2])
        S.barrier()

        def finish_debug(src_ap):
            z = XIN[0]
            S.add("dve", lambda e: e.memset(z.t[:], 0.0), (), [z.b])
            for r in range(T // 128):
                dma("sp", out_d[r * 128:(r + 1) * 128, :], z.t[:], reads=[z.b], is_output=True)
            S.emit()

        if stop_after == "A":
            finish_debug(None)
            return nc

        off[0] = phase_base
        NCH = TOT // 128
        LOADS = []
        for i in range(2):
            LOADS.append(dict(
                K=TB(sb("KTh%d" % i, [128, TOT], BF16), Buf("KTh%d" % i)),
                Q=TB(sb("QTh%d" % i, [128, T], BF16), Buf("QTh%d" % i)),
                V=TB(sb("Vh%d" % i, [128, NCH, 128], BF16), Buf("Vh%d" % i)),
                G=TB(sb("GSh%d" % i, [128, T], BF16), Buf("GSh%d" % i))))
        KDF = TB(sb("KDF", [128, NCH, 128], BF16), Buf("KDF"))
        KDB = TB(sb("KDB", [128, NCH, 128], BF16), Buf("KDB"))
        VDF = TB(sb("VDF", [128, NCH, 128], BF16), Buf("VDF"))
        NSL = 4
        SFF = sb("SFF", [128, NSL, 128], F32)
        SBF = sb("SBF", [128, NSL, 128], F32)
        bSFF = [Buf("SFF%d" % i) for i in range(NSL)]
        bSBF = [Buf("SBF%d" % i) for i in range(NSL)]
        SFb = TB(sb("SFb", [128, 32, 128], BF16), Buf("SFb"))
        SBb = TB(sb("SBb", [128, 32, 128], BF16), Buf("SBb"))
        QF = TB(sb("QF", [128, T], BF16), Buf("QF"))
        QB = TB(sb("QB", [128, T], BF16), Buf("QB"))

        def mkrot(name, n, shape, dt):
            return Rot([TB(sb("%s%d" % (name, i), shape, dt), Buf("%s%d" % (name, i))) for i in range(n)])

        PT = mkrot("PT", 3, [128, 512], BF16)
        OB = mkrot("OB", 3, [128, 512], BF16)
        OSQ = mkrot("OSQ", 3, [128, 512], BF16)
        OF = mkrot("OF", 9, [128, 512], F32)
        BMEAN = mkrot("BMEAN", 3, [128, 512], F32)
        BRSTD = mkrot("BRSTD", 5, [128, 512], F32)
        BEX2 = mkrot("BEX2", 3, [128, 512], F32)
        RETO = mkrot("RETO", 1, [128, T], BF16)

        def ret_load(h):
            L = LOADS[h % 2]
            dma("sp", L["K"].t[:], KT[h], writes=[L["K"].b])
            dma("sp", L["V"].t[:], VTM[h], writes=[L["V"].b])
            dma("sp", L["Q"].t[:], QT[h], writes=[L["Q"].b])
            dma("sp", L["G"].t[:], GS[h], writes=[L["G"].b])

        ret_load(0)
        for h in range(4):
            if h + 1 < 4:
                ret_load(h + 1)
            L = LOADS[h % 2]
            Kt, Qt, Vt, Gt = L["K"], L["Q"], L["V"], L["G"]
            act(VDF.t[:].rearrange("p n i -> p (n i)"), Vt.t[:].rearrange("p n i -> p (n i)"), AF.Identity,
                [Vt.b, bCOLS], [VDF.b], scale=col(C_KDEC, h))
            act(KDB.t[:].rearrange("p n i -> p (n i)"), Vt.t[:].rearrange("p n i -> p (n i)"), AF.Identity,
                [Vt.b, bCOLS], [KDB.b], scale=col(C_KDEC, 4 + h))
            for n0 in range(0, NCH, 4):
                nn = min(4, NCH - n0)
                pb = banks.next()
                for q in range(nn):
                    mm(pb.t[:, q * 128:(q + 1) * 128], Kt.t[:, (n0 + q) * 128:(n0 + q + 1) * 128], IDB[:],
                       True, True, [Kt.b, bIDB], [pb.b])
                cp("act", KDF.t[:].rearrange("p n i -> p (n i)")[:, n0 * 128:(n0 + nn) * 128],
                   pb.t[:, 0:nn * 128], [pb.b], [KDF.b])
            tt("pool", QF.t[:].rearrange("p (n i) -> p n i", i=128), Qt.t[:].rearrange("p (n i) -> p n i", i=128),
               QDEC[:, h:h + 1, :].broadcast_to([128, 32, 128]), ALU.mult, [Qt.b, bQDEC], [QF.b])
            tt("pool", QB.t[:].rearrange("p (n i) -> p n i", i=128), Qt.t[:].rearrange("p (n i) -> p n i", i=128),
               QDEC[:, 4 + h:5 + h, :].broadcast_to([128, 32, 128]), ALU.mult, [Qt.b, bQDEC], [QB.b])
            border = [1, 0] + [2 + m for m in range(31, 0, -1)]
            for si in range(33):
                for (d, n, ST, bST, VD, Sb) in ((0, si, SFF, bSFF, VDF, SFb), (1, border[si], SBF, bSBF, KDB, SBb)):
                    pc = banks.next()
                    mm(pc.t[:, 0:128], KDF.t[:, n, :], VD.t[:, n, :], True, True, [KDF.b, VD.b], [pc.b])
                    sl, pl_ = si % NSL, (si - 1) % NSL
                    if si == 0:
                        cp("dve", ST[:, sl, :], pc.t[:, 0:128], [pc.b], [bST[sl]])
                    else:
                        stt(ST[:, sl, :], ST[:, pl_, :], col(C_GC, d * 4 + h), pc.t[:, 0:128], ALU.mult, ALU.add,
                            [pc.b, bST[pl_], bCOLS], [bST[sl]])
                        act(Sb.t[:, si - 1, :], ST[:, sl, :], AF.Identity, [bST[sl]], [Sb.b])
            ro = RETO.next()
            NBT = 8
            st_ = [dict() for _ in range(NBT)]

            def T0(bt, X):
                X["ps"] = banks.next()
                for c in range(4):
                    m = bt * 4 + c
                    kc = (2 + m) * 128
                    mm(X["ps"].t[:, c * 128:(c + 1) * 128], Kt.t[:, kc:kc + 128], Qt.t[:, m * 128:(m + 1) * 128],
                       True, True, [Kt.b, Qt.b], [X["ps"].b])

            def T1(bt, X):
                X["pt"] = PT.next()
                tt("dve", X["pt"].t[:].rearrange("p (c i) -> p c i", c=4), X["ps"].t[:].rearrange("p (c i) -> p c i", c=4),
                   DM[:, h:h + 1, :].broadcast_to([128, 4, 128]), ALU.mult, [X["ps"].b, bDM], [X["pt"].b])

            def T2(bt, X):
                X["po"] = banks.next()
                po, pt = X["po"], X["pt"]
                for c in range(4):
                    m = bt * 4 + c
                    oc = po.t[:, c * 128:(c + 1) * 128]
                    mm(oc, Vt.t[:, 2 + m, :], pt.t[:, c * 128:(c + 1) * 128], True, False, [Vt.b, pt.b], [po.b])
                    mm(oc, SFb.t[:, m, :], QF.t[:, m * 128:(m + 1) * 128], False, False, [SFb.b, QF.b], [po.b])
                    mm(oc, SBb.t[:, 31 - m, :], QB.t[:, m * 128:(m + 1) * 128], False, True, [SBb.b, QB.b], [po.b])

            def T3(bt, X):
                X["ob"], X["osq"], X["of"] = OB.next(), OSQ.next(), OF.next()
                po = X["po"]
                cp("act", X["of"].t[:], po.t[:], [po.b], [X["of"].b])
                act(X["osq"].t[:], po.t[:], AF.Square, [po.b], [X["osq"].b])
                cp("pool", X["ob"].t[:], X["of"].t[:], [X["of"].b], [X["ob"].b])

            def T4(bt, X):
                X["pm"], X["pq"] = banks.next(), banks.next()
                mm(X["pm"].t[:], ONES_H[:], X["ob"].t[:], True, True, [bONES, X["ob"].b], [X["pm"].b])
                mm(X["pq"].t[:], ONES_H[:], X["osq"].t[:], True, True, [bONES, X["osq"].b], [X["pq"].b])

            def T5(bt, X):
                X["mean"], X["rstd"], X["ex2"] = BMEAN.next(), BRSTD.next(), BEX2.next()
                act(X["mean"].t[:], X["pm"].t[:], AF.Identity, [X["pm"].b], [X["mean"].b])
                act(X["rstd"].t[:], X["pm"].t[:], AF.Square, [X["pm"].b], [X["rstd"].b])
                act(X["ex2"].t[:], X["pq"].t[:], AF.Identity, [X["pq"].b], [X["ex2"].b])

            def T6(bt, X):
                r = X["rstd"]
                tt("dve", r.t[:], X["ex2"].t[:], r.t[:], ALU.subtract, [X["ex2"].b, r.b], [r.b])
                tt("pool", X["of"].t[:], X["of"].t[:], X["mean"].t[:], ALU.subtract, [X["of"].b, X["mean"].b], [X["of"].b])

            def T7(bt, X):
                r = X["rstd"]
                act(r.t[:], r.t[:], AF.Ln, [r.b, bCOLS], [r.b], bias=col(C_EPSH))
                act(r.t[:], r.t[:], AF.Exp, [r.b], [r.b], scale=-0.5)

            def T8(bt, X):
                r = X["rstd"]
                tt("dve", X["of"].t[:], X["of"].t[:], r.t[:], ALU.mult, [X["of"].b, r.b], [X["of"].b])

            def T9(bt, X):
                o_ = X["of"]
                ts("dve", o_.t[:], o_.t[:], VEC[:, R_RNG + h:R_RNG + h + 1], VEC[:, R_RNB + h:R_RNB + h + 1],
                   ALU.mult, ALU.add, [o_.b, bVEC], [o_.b])

            def T10(bt, X):
                tt("pool", ro.t[:, bt * 512:(bt + 1) * 512], X["of"].t[:], Gt.t[:, bt * 512:(bt + 1) * 512], ALU.mult,
                   [X["of"].b, Gt.b], [ro.b])

            stages = [T0, T1, T2, T3, T4, T5, T6, T7, T8, T9, T10]
            for k in range(NBT + len(stages) - 1):
                for s_i, fn in enumerate(stages):
                    bt = k - s_i
                    if 0 <= bt < NBT:
                        fn(bt, st_[bt])
            dma("sp", CAT[h], ro.t[:], reads=[ro.b])
        S.barrier()

        if stop_after == "B1":
            finish_debug(None)
            return nc
        off[0] = phase_base
        LX = TC + 3 + T
        LP = LX + 5
        XP = TB(sb("XP", [128, LP], F32), Buf("XP"))
        XC = TB(sb("XC", [128, LP], F32), Buf("XC"))
        XCB = TB(sb("XCB", [128, LP], BF16), Buf("XCB"))
        RAd = [TB(sb("RA%d" % d, [128, LP], F32), Buf("RA%d" % d)) for d in range(2)]
        IBd = [TB(sb("IB%d" % d, [128, LP], F32), Buf("IB%d" % d)) for d in range(2)]
        TMd = [TB(sb("TM%d" % d, [128, LP], F32), Buf("TM%d" % d)) for d in range(2)]
        GGc = TB(sb("GGc", [128, T], BF16), Buf("GGc"))
        OUTc = TB(sb("OUTc", [128, T], BF16), Buf("OUTc"))
        DG = sb("DG", [128, 16, 128], F32)
        S.strict = True
        for k in range(16):
            ts("dve", DG[:, k, :], IDF, VEC[:, R_CW + k:R_CW + k + 1], None, ALU.mult, None, [bCST, bVEC], [bDG])
        S.strict = False
        C0, L0 = 0, TC + 3
        S.add("pool", lambda e: e.memset(XP.t[:], 0.0), (), [XP.b])
        def lru_load_x(cb):
            dma("sp", XP.t[:, 2:2 + TC], XS[cb, :, 0:TC], writes=[XP.b])
            dma("sp", XP.t[:, 5 + TC:5 + TC + T], XS[cb, :, TC:TOT], writes=[XP.b])

        lru_load_x(0)
        dma("sp", GGc.t[:], GG[0], writes=[GGc.b])
        for cb in range(4):
            for c0 in range(0, LX, 512):
                w = min(512, LX - c0)
                pcv = banks.next()
                for jt in range(4):
                    mm(pcv.t[:, :w], DG[:, jt * 4 + cb, :], XP.t[:, c0 + jt:c0 + jt + w], jt == 0, jt == 3,
                       [bDG, XP.b], [pcv.b])
                act(XC.t[:, c0:c0 + w], pcv.t[:, :w], AF.Identity, [pcv.b, bVEC], [XC.b], bias=VEC[:, R_CB + cb:R_CB + cb + 1])
                act(XCB.t[:, c0:c0 + w], pcv.t[:, :w], AF.Identity, [pcv.b, bVEC], [XCB.b], bias=VEC[:, R_CB + cb:R_CB + cb + 1])
            if cb + 1 < 4:
                lru_load_x(cb + 1)
            for c0 in range(0, LX, 512):
                w = min(512, LX - c0)
                for d in range(2):
                    pr = banks.next()
                    mm(pr.t[:, :w], WBD[:, (0 * 2 + d) * 4 + cb, :], XCB.t[:, c0:c0 + w], True, True, [bWBD, XCB.b], [pr.b])
                    pi = banks.next()
                    mm(pi.t[:, :w], WBD[:, (1 * 2 + d) * 4 + cb, :], XCB.t[:, c0:c0 + w], True, True, [bWBD, XCB.b], [pi.b])
                    ci = d * 4 + cb
                    act(RAd[d].t[:, c0:c0 + w], pr.t[:, :w], AF.Sigmoid, [pr.b, bVEC], [RAd[d].b], bias=VEC[:, R_BA + ci:R_BA + ci + 1])
                    act(IBd[d].t[:, c0:c0 + w], pi.t[:, :w], AF.Sigmoid, [pi.b, bVEC], [IBd[d].b], bias=VEC[:, R_BI + ci:R_BI + ci + 1])
            for d in range(2):
                ci = d * 4 + cb
                act(RAd[d].t[:, 0:LX], RAd[d].t[:, 0:LX], AF.Exp, [RAd[d].b, bCOLS], [RAd[d].b], scale=col(C_CP, ci))
                tt("dve", TMd[d].t[:, 0:LX], RAd[d].t[:, 0:LX], RAd[d].t[:, 0:LX], ALU.mult, [RAd[d].b], [TMd[d].b])
            for d in range(2):
                act(TMd[d].t[:, 0:LX], TMd[d].t[:, 0:LX], AF.Sqrt, [TMd[d].b, bCOLS], [TMd[d].b], scale=-1.0, bias=col(C_ONE))
            for d in range(2):
                tt(MULENG, IBd[d].t[:, 0:LX], IBd[d].t[:, 0:LX], TMd[d].t[:, 0:LX], ALU.mult, [IBd[d].b, TMd[d].b], [IBd[d].b])
                tt(MULENG, IBd[d].t[:, 0:LX], IBd[d].t[:, 0:LX], XC.t[:, 0:LX], ALU.mult, [IBd[d].b, XC.b], [IBd[d].b])
            HF, HB = TMd[0], TMd[1]
            S.strict = True
            S.add("dve", lambda e: e.tensor_tensor_scan(out=HF.t[:, C0:C0 + TC], data0=RAd[0].t[:, C0:C0 + TC],
                  data1=IBd[0].t[:, C0:C0 + TC], initial=0.0, op0=ALU.mult, op1=ALU.add), [RAd[0].b, IBd[0].b], [HF.b])
            S.add("dve", lambda e: e.tensor_tensor_scan(out=HB.t[:, C0:C0 + TC][:, ::-1],
                  data0=RAd[1].t[:, C0:C0 + TC][:, ::-1], data1=IBd[1].t[:, C0:C0 + TC][:, ::-1], initial=0.0,
                  op0=ALU.mult, op1=ALU.add), [RAd[1].b, IBd[1].b], [HB.b])
            S.add("dve", lambda e: e.tensor_tensor_scan(out=HF.t[:, L0:L0 + T], data0=RAd[0].t[:, L0:L0 + T],
                  data1=IBd[0].t[:, L0:L0 + T], initial=HF.t[:, C0 + TC - 1:C0 + TC], op0=ALU.mult, op1=ALU.add),
                  [RAd[0].b, IBd[0].b, HF.b], [HF.b])
            S.add("dve", lambda e: e.tensor_tensor_scan(out=HB.t[:, L0:L0 + T][:, ::-1],
                  data0=RAd[1].t[:, L0:L0 + T][:, ::-1], data1=IBd[1].t[:, L0:L0 + T][:, ::-1],
                  initial=HB.t[:, C0:C0 + 1], op0=ALU.mult, op1=ALU.add), [RAd[1].b, IBd[1].b, HB.b], [HB.b])
            tt("dve", HF.t[:, L0:L0 + T], HF.t[:, L0:L0 + T], HB.t[:, L0:L0 + T], ALU.add, [HF.b, HB.b], [HF.b])
            S.strict = False
            tt(MULENG, OUTc.t[:], HF.t[:, L0:L0 + T], GGc.t[:], ALU.mult, [HF.b, GGc.b], [OUTc.b])
            if cb + 1 < 4:
                dma("sp", GGc.t[:], GG[cb + 1], writes=[GGc.b])
            dma("sp", CAT[4 + cb], OUTc.t[:], reads=[OUTc.b])
        S.barrier()
        if stop_after == "B":
            finish_debug(None)
            return nc

        wout_v = wout_d.rearrange("(kh kl) c -> kl kh c", kl=128)
        for g in range(T // NT):
            t0 = g * NT
            n = NT
            hv = halves(n)
            for kc in range(8):
                for (h, h0, hw) in hv:
                    dma("sp", HT[:, kc, h0:h0 + hw], CAT[kc, :, t0 + h0:t0 + h0 + hw], writes=[bHT[kc][h]])
            for dc in range(8):
                dma("sp", XT[:, dc, :], X1[dc * 128:(dc + 1) * 128, t0:t0 + n], writes=bXT[dc])
            wbs = []
            for s in range(4):
                wb = ws_rot.next()
                dma("pool", wb.t[:], wout_v[:, :, s * 256:(s + 1) * 256], writes=[wb.b])
                wbs.append(wb)
            def wout_unit(h, h0, hw, dc, wbs=wbs):
                wb, jj = wbs[dc // 2], dc % 2
                p = banks.next()
                for kc in range(8):
                    mm(p.t[:, :hw], wb.t[:, kc, jj * 128:(jj + 1) * 128], HT[:, kc, h0:h0 + hw],
                       kc == 0, kc == 7, [wb.b, bHT[kc][h]], [p.b])
                xb = xtb(dc, h0, hw)
                stt(XT[:, dc, h0:h0 + hw], p.t[:, :hw], col(C_G2, dc), XT[:, dc, h0:h0 + hw],
                    ALU.mult, ALU.add, [p.b, bCOLS] + xb, xb)

            early = {}

            def filler2(n=n, early=early):
                early.update(mm1_early(1, n, slabs=(0, 1)))

            residual_ln(n, wout_unit, "ln2", filler=filler2)
            ffn_mm1(1, n, early=early)

            def out_tiles(tixs, t0=t0):
                for tix in tixs:
                    if tix < 2:
                        ob_t, ob_b = XIN[tix].t[:], [XIN[tix].b]
                    else:
                        ob_t, ob_b = HSTG[tix - 2]
                    for hb in range(2):
                        pb = banks.next()
                        for q in range(4):
                            dc = hb * 4 + q
                            tr(pb.t[:, q * 128:(q + 1) * 128], XT[:, dc, tix * 128:(tix + 1) * 128], IDF,
                               [bXT[dc][tix], bCST], [pb.b])
                        cp("act" if hb == 0 else "dve", ob_t[:, hb * 512:(hb + 1) * 512], pb.t[:], [pb.b], ob_b)
                    dma("sp", out_d[t0 + tix * 128:t0 + (tix + 1) * 128, :], ob_t, reads=ob_b, is_output=True)

            residual_ln(n, lambda h, h0, hw, dc: mm2_unit(h, h0, hw, dc, C_G3), "ln3",
                        filler=lambda: out_tiles(range(0, 4)))
            out_tiles(range(4, 8))
        S.emit()
    return nc


def _consts():
    p = np.arange(128, dtype=np.float32)
    ident = np.eye(128, dtype=np.float32)
    diff = p[None, :] - p[:, None]
    diffpos = np.maximum(diff, 0.0)
    diffneg = np.maximum(-diff, 0.0)
    irow = np.concatenate([np.tile((p + 1.0)[None, :], (128, 1)), np.tile((128.0 - p)[None, :], (128, 1))], 1)
    pcolx = np.concatenate([np.tile((127.0 - p)[:, None], (1, 4)), np.tile(p[:, None], (1, 4))], 1)
    cst = np.concatenate([ident, diffpos, diffneg, irow, pcolx], 1).astype(np.float32)
    n = 32
    inv = (np.float32(10000.0) ** (-np.arange(n, dtype=np.float32) / np.float32(n))).astype(np.float32)
    t = np.arange(T)
    rows = (t // 64).astype(np.float32)
    cols = (t % 64).astype(np.float32)
    d = np.arange(128)
    f = d % 32
    pos = np.where((d // 64)[:, None] == 0, rows[None, :], cols[None, :]).astype(np.float32)
    ang = (pos * inv[f][:, None]).astype(np.float32)
    c = np.cos(ang).astype(np.float32)
    s = np.sin(ang).astype(np.float32)
    half = (d % 64) // 32
    ss = np.where(half[:, None] == 0, -s, s).astype(np.float32)
    return cst, c.astype(ml_dtypes.bfloat16), ss.astype(ml_dtypes.bfloat16)


def _small(b, c, c_ctx, b_ada, ln_g, ln_b, ret_norm_g, ret_norm_b, lru_conv_w, lru_conv_b, lru_b_a, lru_b_i, lru_lambda):
    sm = np.zeros((256, 128), np.float32)
    sm[R_BADA:R_BADA + 72] = b_ada[0].reshape(72, 128)
    sm[R_C:R_C + 8] = c[b].reshape(8, 128)
    sm[R_CCTX:R_CCTX + 8] = c_ctx.reshape(8, 128)
    sm[R_LNG:R_LNG + 24] = ln_g[0].reshape(24, 128)
    sm[R_LNB:R_LNB + 24] = ln_b[0].reshape(24, 128)
    sm[R_RNG:R_RNG + 4] = ret_norm_g[0].reshape(4, 128)
    sm[R_RNB:R_RNB + 4] = ret_norm_b[0].reshape(4, 128)
    sm[R_CW:R_CW + 16] = lru_conv_w[0].reshape(16, 128)
    sm[R_CB:R_CB + 4] = lru_conv_b[0].reshape(4, 128)
    sm[R_BA:R_BA + 8] = lru_b_a[0].reshape(8, 128)
    sm[R_BI:R_BI + 8] = lru_b_i[0].reshape(8, 128)
    sm[R_LAM:R_LAM + 8] = lru_lambda[0].reshape(8, 128)
    return sm


def make_in_maps(x, c, ctx, c_ctx, w_ada, b_ada, ffn1_w_gate, ffn1_w_up, ffn1_w_down,
                 ffn2_w_gate, ffn2_w_up, ffn2_w_down, w_in, w_out, ret_decay_logit,
                 ret_norm_g, ret_norm_b, lru_conv_w, lru_conv_b, lru_w_a, lru_b_a,
                 lru_w_i, lru_b_i, lru_lambda, ln_g, ln_b):
    f = lambda a: np.ascontiguousarray(np.asarray(a, dtype=np.float32))
    cst, rc, rs = _consts()
    shared = {
        "w_ada": f(w_ada[0]), "wg1": f(ffn1_w_gate[0]), "wu1": f(ffn1_w_up[0]), "wd1": f(ffn1_w_down[0]),
        "wg2": f(ffn2_w_gate[0]), "wu2": f(ffn2_w_up[0]), "wd2": f(ffn2_w_down[0]),
        "w_in": f(w_in[0]), "w_out": f(w_out[0]),
        "lru_wa": f(np.asarray(lru_w_a[0]).reshape(16, 64, 64)), "lru_wi": f(np.asarray(lru_w_i[0]).reshape(16, 64, 64)),
        "logit": f(np.asarray(ret_decay_logit[0]).reshape(1, 8)),
        "cst": cst, "rope_c": rc, "rope_s": rs,
    }
    args = [np.asarray(a) for a in (c, c_ctx, b_ada, ln_g, ln_b, ret_norm_g, ret_norm_b, lru_conv_w, lru_conv_b,
                                    lru_b_a, lru_b_i, lru_lambda)]
    in_maps = []
    for b in range(8):
        m = dict(shared)
        m["x"] = f(x[b])
        m["ctx"] = f(ctx[b])
        m["small"] = _small(b, args[0], args[1], *args[2:])
        in_maps.append(m)
    return in_maps


def kernel(**inputs):
    in_maps = make_in_maps(**inputs)
    nc = build_nc()
    res = run_bass_kernel_spmd(nc, in_maps, core_ids=list(range(8)))
    return np.stack([np.asarray(r["out"], dtype=np.float32) for r in res.results], 0)
```

```python
import contextlib
import numpy as np
import ml_dtypes
import concourse.bass as bass
import concourse.mybir as mybir
from concourse.bass_utils import run_bass_kernel_spmd

F32 = mybir.dt.float32
BF16 = mybir.dt.bfloat16
AF = mybir.ActivationFunctionType
ALU = mybir.AluOpType

T = 4096
TC = 256
TOT = T + TC
D = 1024
FF = 2816
NFC = 22
NDC = 8
NT = 1024
MULENG = "dve"
ALPHA = 2.0 ** 0.25
LN_EPS = 1e-5
K_SCALE = 128.0 ** -0.5
ENGS = ("pe", "act", "dve", "pool", "sp")
SB_BASE = 16640
SB_END = 229376


class Buf:
    __slots__ = ("name", "last_w", "readers")

    def __init__(self, name=""):
        self.name = name
        self.last_w = None
        self.readers = []


class Op:
    __slots__ = ("idx", "eng", "fn", "is_dma", "deps", "need_inc", "sem", "val", "dma_prev")

    def __init__(self, idx, eng, fn, is_dma):
        self.idx = idx
        self.eng = eng
        self.fn = fn
        self.is_dma = is_dma
        self.deps = set()
        self.need_inc = False
        self.sem = None
        self.val = 0
        self.dma_prev = 0


class Sched:
    def __init__(self, nc, n_dma_sems=8):
        self.nc = nc
        self.ops = []
        self.n_dma_sems = n_dma_sems
        self.out_dma_ops = []
        self.last_compute = {}
        self.dmas_since_barrier = []
        self.pending_barrier = {}
        self.strict = False

    def barrier(self):
        deps = set(self.last_compute.values()) | set(self.dmas_since_barrier)
        for e in ENGS:
            self.pending_barrier[e] = set(deps) | self.pending_barrier.get(e, set())
        self.dmas_since_barrier = []

    def add(self, eng, fn, reads=(), writes=(), dma=False, is_output=False):
        op = Op(len(self.ops), eng, fn, dma)
        ops = self.ops
        for b in reads:
            if b.last_w is not None:
                op.deps.add(b.last_w)
        for b in writes:
            if b.last_w is not None:
                op.deps.add(b.last_w)
            for r in b.readers:
                op.deps.add(r)
        pb = self.pending_barrier.pop(eng, None)
        if pb:
            op.deps |= pb
        pr = set()
        for d in op.deps:
            p = ops[d]
            if (not p.is_dma) and (not dma) and p.eng == eng and (eng == "pe" or not self.strict):
                continue
            pr.add(d)
        op.deps = pr
        for b in reads:
            if not dma:
                b.readers = [r for r in b.readers if ops[r].is_dma or ops[r].eng != eng]
            b.readers.append(op.idx)
        for b in writes:
            b.last_w = op.idx
            b.readers = []
        ops.append(op)
        if dma:
            self.dmas_since_barrier.append(op.idx)
        else:
            self.last_compute[eng] = op.idx
        if is_output:
            self.out_dma_ops.append(op.idx)
        return op

    def emit(self):
        nc = self.nc
        ops = self.ops
        for op in ops:
            for d in op.deps:
                ops[d].need_inc = True
        for d in self.out_dma_ops:
            ops[d].need_inc = True
        with contextlib.ExitStack() as st:
            eng_sem = {e: st.enter_context(nc.semaphore("s_" + e)) for e in ENGS}
            dma_sems = {e: [st.enter_context(nc.semaphore("d_%s%d" % (e, i))) for i in range(self.n_dma_sems)]
                        for e in ("sp", "pool", "act")}
            cnt = {e: 0 for e in ENGS}
            dcnt = {e: [0] * self.n_dma_sems for e in dma_sems}
            drr = {e: 0 for e in dma_sems}
            for op in ops:
                if op.is_dma:
                    k = drr[op.eng]
                    drr[op.eng] = (k + 1) % self.n_dma_sems
                    op.sem = dma_sems[op.eng][k]
                    op.dma_prev = dcnt[op.eng][k]
                    dcnt[op.eng][k] += 16
                    op.val = dcnt[op.eng][k]
                elif op.need_inc:
                    cnt[op.eng] += 1
                    op.sem = eng_sem[op.eng]
                    op.val = cnt[op.eng]
            block = st.enter_context(nc.Block())
            by_eng = {e: [op for op in ops if op.eng == e] for e in ENGS}
            final_waits = {}
            for d in self.out_dma_ops:
                key = id(ops[d].sem)
                if final_waits.get(key, (None, 0))[1] < ops[d].val:
                    final_waits[key] = (ops[d].sem, ops[d].val)

            def run(e, engine):
                waited = {}
                for op in by_eng[e]:
                    need = {}
                    for d in op.deps:
                        p = ops[d]
                        key = id(p.sem)
                        if need.get(key, (None, 0))[1] < p.val:
                            need[key] = (p.sem, p.val)
                    if op.is_dma and op.dma_prev > 0:
                        key = id(op.sem)
                        if need.get(key, (None, 0))[1] < op.dma_prev:
                            need[key] = (op.sem, op.dma_prev)
                    for key, (sem, val) in need.items():
                        if waited.get(key, 0) < val:
                            engine.wait_ge(sem, val)
                            waited[key] = val
                    ins = op.fn(engine)
                    if op.is_dma:
                        ins.then_inc(op.sem, 16)
                    elif op.need_inc:
                        ins.then_inc(op.sem, 1)
                if e == "sp":
                    for sem, val in final_waits.values():
                        if waited.get(id(sem), 0) < val:
                            engine.wait_ge(sem, val)

            @block.tensor
            def _(eng):
                run("pe", eng)

            @block.scalar
            def _(eng):
                run("act", eng)

            @block.vector
            def _(eng):
                run("dve", eng)

            @block.gpsimd
            def _(eng):
                run("pool", eng)

            @block.sync
            def _(eng):
                run("sp", eng)


class TB:
    __slots__ = ("t", "b")

    def __init__(self, t, b):
        self.t = t
        self.b = b


class Rot:
    def __init__(self, items):
        self.items = items
        self.i = 0

    def next(self):
        it = self.items[self.i % len(self.items)]
        self.i += 1
        return it


R_BADA, R_C, R_CCTX, R_LNG = 0, 72, 80, 88
R_LNB, R_RNG, R_RNB, R_CW, R_CB, R_BA, R_BI, R_LAM = 128, 152, 156, 160, 176, 180, 188, 196
(C_SC1L, C_SH1L, C_G1L, C_SC1C, C_SH1C, C_G1C, C_GU1L, C_BU1L, C_GU1C, C_BU1C, C_G2, C_GU2, C_BU2, C_G3,
 C_TMP, C_LG, C_KDEC, C_GC, C_CP, C_CP2, C_EPS, C_TMP2, C_EPSH, C_ONE) = [8 * i for i in range(24)]


def build_nc(stop_after=None):
    nc = bass.Bass("TRN2", target_bir_lowering=False)

    def din(name, shape, dt=F32):
        return nc.dram_tensor(name, shape, dt, kind="ExternalInput").ap()

    x_d = din("x", [T, D])
    ctx_d = din("ctx", [TC, D])
    small_d = din("small", [256, 128])
    wada_d = din("w_ada", [D, 9 * D])
    wg_d = [din("wg1", [D, FF]), din("wg2", [D, FF])]
    wu_d = [din("wu1", [D, FF]), din("wu2", [D, FF])]
    wd_d = [din("wd1", [FF, D]), din("wd2", [FF, D])]
    win_d = din("w_in", [D, 3 * D])
    wout_d = din("w_out", [D, D])
    lwa_d = din("lru_wa", [16, 64, 64])
    lwi_d = din("lru_wi", [16, 64, 64])
    logit_d = din("logit", [1, 8])
    cst_d = din("cst", [128, 648])
    ropec_d = din("rope_c", [128, T], BF16)
    ropes_d = din("rope_s", [128, T], BF16)
    out_d = nc.dram_tensor("out", [T, D], F32, kind="ExternalOutput").ap()
    okind = "ExternalOutput" if stop_after else "Internal"
    X1 = nc.dram_tensor("X1", [D, T], F32, kind=okind).ap()
    KT = nc.dram_tensor("KT", [4, 128, TOT], BF16, kind=okind).ap()
    QT = nc.dram_tensor("QT", [4, 128, T], BF16, kind=okind).ap()
    VTM = nc.dram_tensor("VTM", [4, 128, TOT // 128, 128], BF16, kind=okind).ap()
    XS = nc.dram_tensor("XS", [4, 128, TOT], F32, kind=okind).ap()
    GS = nc.dram_tensor("GS", [4, 128, T], BF16, kind=okind).ap()
    GG = nc.dram_tensor("GG", [4, 128, T], BF16, kind=okind).ap()
    CAT = nc.dram_tensor("CAT", [8, 128, T], BF16, kind=okind).ap()

    S = Sched(nc)
    off = [SB_BASE]

    def sb(name, shape, dt):
        n = int(np.prod(shape[1:])) * (4 if dt == F32 else 2)
        n = (n + 63) // 64 * 64
        assert off[0] + n <= SB_END, ("SBUF overflow", name, off[0] + n)
        t = nc.alloc_sbuf_tensor_at(name, list(shape), dt, offset=off[0])
        off[0] += n
        return t

    with contextlib.ExitStack() as st:
        PS = [TB(st.enter_context(nc.psum_tensor("ps%d" % i, [128, 512], F32)), Buf("ps%d" % i)) for i in range(8)]
        banks = Rot(PS)

        def mm(out, lhsT, rhs, start, stop, reads, writes):
            S.add("pe", lambda e: e.matmul(out, lhsT=lhsT, rhs=rhs, start=start, stop=stop), reads, writes)

        def tr(out, in_, ident, reads, writes):
            S.add("pe", lambda e: e.transpose(out, in_, ident), reads, writes)

        def act(out, in_, func, reads, writes, scale=None, bias=None):
            kw = {}
            if scale is not None:
                kw["scale"] = scale
            if bias is not None:
                kw["bias"] = bias
            S.add("act", lambda e: e.activation(out=out, in_=in_, func=func, **kw), reads, writes)

        def tt(eng, out, in0, in1, op, reads, writes):
            S.add(eng, lambda e: e.tensor_tensor(out=out, in0=in0, in1=in1, op=op), reads, writes)

        def ts(eng, out, in0, s1, s2, op0, op1, reads, writes):
            if s2 is None:
                S.add(eng, lambda e: e.tensor_scalar(out=out, in0=in0, scalar1=s1, scalar2=None, op0=op0), reads, writes)
            else:
                S.add(eng, lambda e: e.tensor_scalar(out=out, in0=in0, scalar1=s1, scalar2=s2, op0=op0, op1=op1), reads, writes)

        def stt(out, in0, scalar, in1, op0, op1, reads, writes):
            S.add("dve", lambda e: e.scalar_tensor_tensor(out=out, in0=in0, scalar=scalar, in1=in1, op0=op0, op1=op1), reads, writes)

        def cp(eng, out, in_, reads, writes):
            if eng == "act":
                act(out, in_, AF.Identity, reads, writes)
            else:
                S.add(eng, lambda e: e.tensor_copy(out=out, in_=in_), reads, writes)

        def dma(eng, out, in_, reads=(), writes=(), is_output=False):
            S.add(eng, lambda e: e.dma_start(out=out, in_=in_), reads, writes, dma=True, is_output=is_output)

        dbg_outs = {}

        def dbg(name, ap, shape, dt, reads):
            if not stop_after:
                return
            o = nc.dram_tensor("DBG_" + name, list(shape), dt, kind="ExternalOutput").ap()
            dma("sp", o[:], ap, reads=reads)

        VEC = sb("VEC", [128, 256], F32)
        MOD = sb("MOD", [128, 72, 2], F32)
        COLS = sb("COLS", [128, 192], F32)
        CST = sb("CST", [128, 648], F32)
        IDF = CST[:, 0:128]
        DIFFPOS = CST[:, 128:256]
        DIFFNEG = CST[:, 256:384]
        IROW = CST[:, 384:640]
        PCOLX = CST[:, 640:648]
        IDB = sb("IDB", [128, 128], BF16)
        ONES_D = sb("ONES_D", [128, 128], BF16)
        ONES_H = sb("ONES_H", [128, 128], BF16)
        DM = sb("DM", [128, 4, 128], F32)
        QDEC = sb("QDEC", [128, 8, 128], F32)
        WBD = sb("WBD", [128, 16, 128], BF16)
        SC = sb("SC", [128, 16], BF16)
        PERM = sb("PERM", [128, 128], BF16)
        bDG = Buf("DG")
        bVEC, bMOD, bCOLS, bCST, bIDB, bONES, bDM, bQDEC, bWBD, bSC = [Buf(n) for n in
            ["VEC", "MOD", "COLS", "CST", "IDB", "ONES", "DM", "QDEC", "WBD", "SC"]]
        phase_base = off[0]

        def col(c, i=0, w=1):
            return COLS[:, c + i:c + i + w]

        XT = sb("XT", [128, 8, NT], F32)
        UT = sb("UT", [128, 8, NT], BF16)
        HT = sb("HT", [128, NFC, NT], BF16)
        WD = sb("WD", [128, NFC, D], BF16)
        WS = [TB(sb("WS%d" % i, [128, 8, 256], BF16), Buf("WS%d" % i)) for i in range(4)]
        XIN = [TB(sb("XIN%d" % i, [128, D], F32), Buf("XIN%d" % i)) for i in range(2)]
        TMPF = [TB(sb("TMPF%d" % i, [128, 512], F32), Buf("TMPF%d" % i)) for i in range(4)]
        MEANS = [TB(sb("MEAN%d" % i, [128, 512], F32), Buf("MEAN%d" % i)) for i in range(2)]
        RSTDS = [TB(sb("RSTD%d" % i, [128, 512], F32), Buf("RSTD%d" % i)) for i in range(2)]
        STG = [TB(sb("STG%d" % i, [128, 512], BF16), Buf("STG%d" % i)) for i in range(4)]
        PB16 = Rot([TB(sb("PB16_%d" % i, [128, 512], BF16), Buf("PB16_%d" % i)) for i in range(3)])
        RC = TB(sb("RC", [128, NT], BF16), Buf("RC"))
        RS = TB(sb("RS", [128, NT], BF16), Buf("RS"))
        ac_end = off[0]
        ws_rot, xin_rot, tmpf_rot, stg_rot = Rot(WS), Rot(XIN), Rot(TMPF), Rot(STG)
        mean_rot, rstd_rot = Rot(MEANS), Rot(RSTDS)
        bXT = [[Buf("XT%d_%d" % (dc, t)) for t in range(8)] for dc in range(8)]
        bUT = [[Buf("UT%d_%d" % (dc, h)) for h in range(2)] for dc in range(8)]
        bHT = [[Buf("HT%d_%d" % (fc, h)) for h in range(2)] for fc in range(NFC)]
        bWD = [Buf("WD%d" % s) for s in range(11)]

        HSTG = []
        for i in range(6):
            fc = 8 + 2 * i
            t_ = HT[:, fc:fc + 2, :].rearrange("p a n -> p (a n)").bitcast(F32)
            HSTG.append((t_, [bHT[fc][0], bHT[fc][1], bHT[fc + 1][0], bHT[fc + 1][1]]))

        def xtb(dc, h0, hw):
            return [bXT[dc][t] for t in range(h0 // 128, (h0 + hw + 127) // 128)]

        def halves(n):
            return [(h, h * 512, min(512, n - h * 512)) for h in range((n + 511) // 512)]

        S.strict = True
        dma("sp", CST[:], cst_d[:], writes=[bCST])
        dma("sp", COLS[:, C_LG:C_LG + 8], logit_d.partition_broadcast(128), writes=[bCOLS])
        sm0, sm1 = XIN[0], XIN[1]
        dma("sp", sm0.t[:, 0:128], small_d[0:128, :], writes=[sm0.b])
        dma("sp", sm1.t[:, 0:128], small_d[128:256, :], writes=[sm1.b])
        for i, smx in enumerate((sm0, sm1)):
            pb = banks.next()
            tr(pb.t[:, 0:128], smx.t[:, 0:128], IDF, [smx.b, bCST], [pb.b])
            cp("dve", VEC[:, i * 128:(i + 1) * 128], pb.t[:, 0:128], [pb.b], [bVEC])
        cp("dve", IDB[:], IDF, [bCST], [bIDB])
        for (d0, s0) in ((0, 32), (32, 0), (64, 96), (96, 64)):
            cp("dve", PERM[:, d0:d0 + 32], IDF[:, s0:s0 + 32], [bCST], [bIDB])
        S.add("dve", lambda e: e.memset(ONES_D[:], 1.0 / D), (), [bONES])
        S.add("dve", lambda e: e.memset(ONES_H[:], 1.0 / 128.0), (), [bONES])
        S.add("dve", lambda e: e.memset(COLS[:, C_EPS:C_EPS + 8], LN_EPS / (ALPHA * ALPHA)), (), [bCOLS])
        S.add("dve", lambda e: e.memset(COLS[:, C_EPSH:C_EPSH + 8], LN_EPS), (), [bCOLS])
        S.add("dve", lambda e: e.memset(COLS[:, C_ONE:C_ONE + 8], 1.0), (), [bCOLS])
        act(SC[:], VEC[:, R_C:R_C + 16], AF.Silu, [bVEC], [bSC])
        wada_v = wada_d.rearrange("(kh kl) c -> kl kh c", kl=128)
        pmod = banks.next()
        WA = [TB(HT[:, 0:8, :], bHT[0][0]), TB(HT[:, 8:16, :], bHT[8][0])]
        for i in range(9):
            wa = WA[i % 2]
            dma("pool", wa.t, wada_v[:, :, i * 1024:(i + 1) * 1024], writes=[wa.b])
            for c in range(8):
                for kh in range(8):
                    o = (i * 8 + c) * 2
                    mm(pmod.t[:, o:o + 2], wa.t[:, kh, c * 128:(c + 1) * 128], SC[:, kh:16:8],
                       kh == 0, kh == 7, [wa.b, bSC], [pmod.b])
        tt("dve", MOD[:], pmod.t[:, 0:144].rearrange("p (r j) -> p r j", j=2),
           VEC[:, 0:72, None].broadcast_to([128, 72, 2]), ALU.add, [pmod.b, bVEC], [bMOD])

        def modv(i, j):
            return MOD[:, i * 8:(i + 1) * 8, j]

        def lng(l):
            return VEC[:, R_LNG + l * 8:R_LNG + l * 8 + 8]

        def lnb(l):
            return VEC[:, R_LNB + l * 8:R_LNB + l * 8 + 8]

        RW = [bMOD, bVEC, bCOLS]
        for j, (csc, csh, cg, cgu, cbu) in enumerate([(C_SC1L, C_SH1L, C_G1L, C_GU1L, C_BU1L),
                                                       (C_SC1C, C_SH1C, C_G1C, C_GU1C, C_BU1C)]):
            ts("dve", col(csc, 0, 8), modv(1, j), 1.0, None, ALU.add, None, RW, [bCOLS])
            cp("dve", col(csh, 0, 8), modv(0, j), RW, [bCOLS])
            ts("dve", col(cg, 0, 8), modv(2, j), 0.5 / ALPHA, None, ALU.mult, None, RW, [bCOLS])
            ts("dve", col(C_TMP, 0, 8), modv(4, j), 1.0, None, ALU.add, None, RW, [bCOLS])
            tt("dve", col(cgu, 0, 8), col(C_TMP, 0, 8), lng(0), ALU.mult, RW, [bCOLS])
            tt("dve", col(C_TMP2, 0, 8), col(C_TMP, 0, 8), lnb(0), ALU.mult, RW, [bCOLS])
            tt("dve", col(cbu, 0, 8), col(C_TMP2, 0, 8), modv(3, j), ALU.add, RW, [bCOLS])
        ts("dve", col(C_G2, 0, 8), modv(5, 0), 1.0 / ALPHA, None, ALU.mult, None, RW, [bCOLS])
        ts("dve", col(C_TMP, 0, 8), modv(7, 0), 1.0, None, ALU.add, None, RW, [bCOLS])
        tt("dve", col(C_GU2, 0, 8), col(C_TMP, 0, 8), lng(1), ALU.mult, RW, [bCOLS])
        tt("dve", col(C_TMP2, 0, 8), col(C_TMP, 0, 8), lnb(1), ALU.mult, RW, [bCOLS])
        tt("dve", col(C_BU2, 0, 8), col(C_TMP2, 0, 8), modv(6, 0), ALU.add, RW, [bCOLS])
        ts("dve", col(C_G3, 0, 8), modv(8, 0), 0.5 / ALPHA, None, ALU.mult, None, RW, [bCOLS])

        act(col(C_LG, 0, 8), col(C_LG, 0, 8), AF.Exp, [bCOLS], [bCOLS], scale=-1.0)
        ts("dve", col(C_LG, 0, 8), col(C_LG, 0, 8), 1.0, None, ALU.add, None, [bCOLS], [bCOLS])
        act(col(C_LG, 0, 8), col(C_LG, 0, 8), AF.Ln, [bCOLS], [bCOLS])
        ts("dve", col(C_LG, 0, 8), col(C_LG, 0, 8), -1.0, None, ALU.mult, None, [bCOLS], [bCOLS])
        tt("dve", col(C_KDEC, 0, 8), col(C_LG, 0, 8), PCOLX, ALU.mult, [bCOLS, bCST], [bCOLS])
        act(col(C_KDEC, 0, 8), col(C_KDEC, 0, 8), AF.Exp, [bCOLS], [bCOLS])
        ts("dve", col(C_KDEC, 0, 8), col(C_KDEC, 0, 8), K_SCALE, None, ALU.mult, None, [bCOLS], [bCOLS])
        act(col(C_GC, 0, 8), col(C_LG, 0, 8), AF.Exp, [bCOLS], [bCOLS], scale=128.0)
        for h in range(4):
            ts("dve", DM[:, h, :], DIFFPOS, col(C_LG, h), None, ALU.mult, None, [bCOLS, bCST], [bDM])
            stt(DM[:, h, :], DIFFNEG, col(C_LG, 4 + h), DM[:, h, :], ALU.mult, ALU.add, [bCOLS, bCST, bDM], [bDM])
            act(DM[:, h, :], DM[:, h, :], AF.Exp, [bDM], [bDM])
            ts("dve", DM[:, h, :], DM[:, h, :], K_SCALE, None, ALU.mult, None, [bDM], [bDM])
            for d in range(2):
                ts("dve", QDEC[:, d * 4 + h, :], IROW[:, d * 128:(d + 1) * 128], col(C_LG, d * 4 + h), None,
                   ALU.mult, None, [bCOLS, bCST], [bQDEC])
                act(QDEC[:, d * 4 + h, :], QDEC[:, d * 4 + h, :], AF.Exp, [bQDEC], [bQDEC])
        act(col(C_CP, 0, 8), VEC[:, R_LAM:R_LAM + 8], AF.Exp, [bVEC], [bCOLS], scale=-1.0)
        ts("dve", col(C_CP, 0, 8), col(C_CP, 0, 8), 1.0, None, ALU.add, None, [bCOLS], [bCOLS])
        act(col(C_CP, 0, 8), col(C_CP, 0, 8), AF.Ln, [bCOLS], [bCOLS])
        ts("dve", col(C_CP2, 0, 8), col(C_CP, 0, 8), -16.0, None, ALU.mult, None, [bCOLS], [bCOLS])
        ts("dve", col(C_CP, 0, 8), col(C_CP, 0, 8), -8.0, None, ALU.mult, None, [bCOLS], [bCOLS])
        S.add("dve", lambda e: e.memset(WBD[:], 0.0), (), [bWBD])
        for g, wsrc in enumerate((lwa_d, lwi_d)):
            for d in range(2):
                for cb in range(4):
                    for n in range(2):
                        dma("pool", WBD[64 * n:64 * n + 64, (g * 2 + d) * 4 + cb, 64 * n:64 * n + 64],
                            wsrc[d * 8 + 2 * cb + n], writes=[bWBD])

        S.strict = False
        def mm1_load(which, s):
            wgv = wg_d[which].rearrange("(kh kl) c -> kl kh c", kl=128)
            wuv = wu_d[which].rearrange("(kh kl) c -> kl kh c", kl=128)
            wdv = wd_d[which].rearrange("(j fl) c -> fl j c", fl=128)
            bg = ws_rot.next()
            dma("pool", bg.t[:], wgv[:, :, s * 256:(s + 1) * 256], writes=[bg.b])
            bu = ws_rot.next()
            dma("pool", bu.t[:], wuv[:, :, s * 256:(s + 1) * 256], writes=[bu.b])
            dma("pool", WD[:, 2 * s:2 * s + 2, :], wdv[:, 2 * s:2 * s + 2, :], writes=[bWD[s]])
            return bg, bu

        def mm1_unit(bg, bu, s, j, h, h0, hw):
            fc = 2 * s + j
            pg = banks.next()
            for k in range(8):
                mm(pg.t[:, :hw], bg.t[:, k, j * 128:(j + 1) * 128], UT[:, k, h0:h0 + hw],
                   k == 0, k == 7, [bg.b, bUT[k][h]], [pg.b])
            pu = banks.next()
            for k in range(8):
                mm(pu.t[:, :hw], bu.t[:, k, j * 128:(j + 1) * 128], UT[:, k, h0:h0 + hw],
                   k == 0, k == 7, [bu.b, bUT[k][h]], [pu.b])
            sg = tmpf_rot.next()
            act(sg.t[:, :hw], pg.t[:, :hw], AF.Silu, [pg.b], [sg.b])
            tt("dve", HT[:, fc, h0:h0 + hw], sg.t[:, :hw], pu.t[:, :hw], ALU.mult,
               [sg.b, pu.b], [bHT[fc][h]])

        def ffn_mm1(which, n, early=None):
            hv = halves(n)
            early = early or {}
            for s in sorted(early):
                bg, bu = early[s]
                for j in range(2):
                    for (h, h0, hw) in hv[1:]:
                        mm1_unit(bg, bu, s, j, h, h0, hw)
            for s in range(11):
                if s in early:
                    continue
                bg, bu = mm1_load(which, s)
                for j in range(2):
                    for (h, h0, hw) in hv:
                        mm1_unit(bg, bu, s, j, h, h0, hw)

        def mm1_early(which, n, slabs=(0, 1)):
            (h, h0, hw) = halves(n)[0]
            early = {}
            for s in slabs:
                bg, bu = mm1_load(which, s)
                early[s] = (bg, bu)
                for j in range(2):
                    mm1_unit(bg, bu, s, j, h, h0, hw)
            return early

        def mm2_unit(h, h0, hw, dc, gcolbase):
            pf = banks.next()
            for fc in range(NFC):
                mm(pf.t[:, :hw], WD[:, fc, dc * 128:(dc + 1) * 128], HT[:, fc, h0:h0 + hw],
                   fc == 0, fc == NFC - 1, [bWD[fc // 2], bHT[fc][h]], [pf.b])
            xb = xtb(dc, h0, hw)
            stt(XT[:, dc, h0:h0 + hw], pf.t[:, :hw], col(gcolbase, dc), XT[:, dc, h0:h0 + hw],
                ALU.mult, ALU.add, [pf.b, bCOLS] + xb, xb)

        def residual_ln(n, unit_fn, mode, j=0, t0=0, filler=None, after_mm=None):
            hv = halves(n)
            if len(hv) == 1:
                (h, h0, hw) = hv[0]
                for dc in range(8):
                    unit_fn(h, h0, hw, dc)
                if after_mm is not None:
                    after_mm()
                ln_pre(h, h0, hw)
                pm, pq = ln_stats(h, h0, hw)
                ln_post(h, h0, hw, pm, pq, mode, j, t0)
                return
            (ha, a0, aw), (hb, b0, bw) = hv
            for dc in range(8):
                unit_fn(ha, a0, aw, dc)
            ln_pre(ha, a0, aw)
            for dc in range(4):
                unit_fn(hb, b0, bw, dc)
            pm, pq = ln_stats(ha, a0, aw)
            ln_post(ha, a0, aw, pm, pq, mode, j, t0)
            for dc in range(4, 8):
                unit_fn(hb, b0, bw, dc)
            if after_mm is not None:
                after_mm()
            ln_pre(hb, b0, bw)
            if filler is not None:
                filler()
            pm, pq = ln_stats(hb, b0, bw)
            ln_post(hb, b0, bw, pm, pq, mode, j, t0)

        def ln_pre(h, h0, hw):
            for dc in range(8):
                xb = xtb(dc, h0, hw)
                act(UT[:, dc, h0:h0 + hw], XT[:, dc, h0:h0 + hw], AF.Identity, xb, [bUT[dc][h]])
                act(HT[:, dc, h0:h0 + hw], XT[:, dc, h0:h0 + hw], AF.Square, xb, [bHT[dc][h]])

        def ln_stats(h, h0, hw):
            pm = banks.next()
            for dc in range(8):
                mm(pm.t[:, :hw], ONES_D[:], UT[:, dc, h0:h0 + hw], dc == 0, dc == 7, [bONES, bUT[dc][h]], [pm.b])
            pq = banks.next()
            for dc in range(8):
                mm(pq.t[:, :hw], ONES_D[:], HT[:, dc, h0:h0 + hw], dc == 0, dc == 7, [bONES, bHT[dc][h]], [pq.b])
            return pm, pq

        def ln_post(h, h0, hw, pm, pq, mode, j=0, t0=0):
            mean = mean_rot.next()
            rstd = rstd_rot.next()
            act(mean.t[:, :hw], pm.t[:, :hw], AF.Identity, [pm.b], [mean.b])
            act(rstd.t[:, :hw], pm.t[:, :hw], AF.Square, [pm.b], [rstd.b])
            tt("dve", rstd.t[:, :hw], pq.t[:, :hw], rstd.t[:, :hw], ALU.subtract, [pq.b, rstd.b], [rstd.b])
            act(rstd.t[:, :hw], rstd.t[:, :hw], AF.Ln, [rstd.b, bCOLS], [rstd.b], bias=col(C_EPS))
            act(rstd.t[:, :hw], rstd.t[:, :hw], AF.Exp, [rstd.b], [rstd.b], scale=-0.5)
            for dc in range(8):
                xb = xtb(dc, h0, hw)
                t1 = tmpf_rot.next()
                tt("dve", t1.t[:, :hw], XT[:, dc, h0:h0 + hw], mean.t[:, :hw], ALU.subtract, xb + [mean.b], [t1.b])
                tt("dve", t1.t[:, :hw], t1.t[:, :hw], rstd.t[:, :hw], ALU.mult, [t1.b, rstd.b], [t1.b])
                if mode == "ln1":
                    xs = tmpf_rot.next()
                    act(xs.t[:, :hw], t1.t[:, :hw], AF.Identity, [t1.b, bVEC], [xs.b],
                        scale=VEC[:, R_LNG + dc:R_LNG + dc + 1], bias=VEC[:, R_LNB + dc:R_LNB + dc + 1])
                    dma("sp", X1[dc * 128:(dc + 1) * 128, t0 + h0:t0 + h0 + hw], xs.t[:, :hw], reads=[xs.b])
                if mode in ("ln1", "ln1c"):
                    cgu, cbu = (C_GU1L, C_BU1L) if j == 0 else (C_GU1C, C_BU1C)
                    act(UT[:, dc, h0:h0 + hw], t1.t[:, :hw], AF.Identity, [t1.b, bCOLS], [bUT[dc][h]],
                        scale=col(cgu, dc), bias=col(cbu, dc))
                elif mode == "ln2":
                    act(XT[:, dc, h0:h0 + hw], t1.t[:, :hw], AF.Identity, [t1.b, bVEC], xb,
                        scale=VEC[:, R_LNG + 8 + dc:R_LNG + 9 + dc], bias=VEC[:, R_LNB + 8 + dc:R_LNB + 9 + dc])
                    act(UT[:, dc, h0:h0 + hw], t1.t[:, :hw], AF.Identity, [t1.b, bCOLS], [bUT[dc][h]],
                        scale=col(C_GU2, dc), bias=col(C_BU2, dc))
                elif mode == "ln3":
                    act(XT[:, dc, h0:h0 + hw], t1.t[:, :hw], AF.Identity, [t1.b, bVEC], xb,
                        scale=VEC[:, R_LNG + 16 + dc:R_LNG + 17 + dc], bias=VEC[:, R_LNB + 16 + dc:R_LNB + 17 + dc])

        def rope_evac(p, hw, h0, stg):
            pb = PB16.next()
            cp("act", pb.t[:, :hw], p.t[:, :hw], [p.b], [pb.b])
            q = banks.next()
            mm(q.t[:, :hw], PERM[:], pb.t[:, :hw], True, True, [bIDB, pb.b], [q.b])
            t1 = tmpf_rot.next()
            tt("dve", t1.t[:, :hw], pb.t[:, :hw], RC.t[:, h0:h0 + hw], ALU.mult, [pb.b, RC.b], [t1.b])
            t2 = tmpf_rot.next()
            tt("dve", t2.t[:, :hw], q.t[:, :hw], RS.t[:, h0:h0 + hw], ALU.mult, [q.b, RS.b], [t2.b])
            tt("dve", stg.t[:, :hw], t1.t[:, :hw], t2.t[:, :hw], ALU.add, [t1.b, t2.b], [stg.b])

        win_v = win_d.rearrange("(kh kl) c -> kl kh c", kl=128)
        groups = [("ctx", 0, TC)] + [("lat", g * NT, NT) for g in range(T // NT)]

        def stage_in_loads(kind, t0, n):
            src = x_d if kind == "lat" else ctx_d
            tiles = []
            for tix in range(n // 128):
                if tix < 2:
                    xb = XIN[tix]
                    t_, bl = xb.t[:], [xb.b]
                else:
                    t_, bl = HSTG[tix - 2]
                dma("sp", t_, src[t0 + tix * 128:t0 + (tix + 1) * 128, :], writes=bl)
                tiles.append((t_, bl))
            return tiles

        def stage_in(kind, t0, n, tiles):
            for tix, (t_, bl) in enumerate(tiles):
                for hb in range(2):
                    pb = banks.next()
                    for q in range(4):
                        tr(pb.t[:, q * 128:(q + 1) * 128], t_[:, (hb * 4 + q) * 128:(hb * 4 + q + 1) * 128], IDF,
                           bl + [bCST], [pb.b])
                    cp("act" if hb == 0 else "dve", XT[:, hb * 4:(hb + 1) * 4, tix * 128:(tix + 1) * 128],
                       pb.t[:].rearrange("p (q i) -> p q i", q=4), [pb.b], [bXT[hb * 4 + q][tix] for q in range(4)])

        def modulate_in(kind, n):
            csc, csh = (C_SC1L, C_SH1L) if kind == "lat" else (C_SC1C, C_SH1C)
            for dc in range(8):
                for (h, h0, hw) in halves(n):
                    act(UT[:, dc, h0:h0 + hw], XT[:, dc, h0:h0 + hw], AF.Identity, xtb(dc, h0, hw) + [bCOLS],
                        [bUT[dc][h]], scale=col(csc, dc), bias=col(csh, dc))

        def proj_unit(wb, s, jj, h, h0, hw, lat, col0, t0):
            pk = s // 2
            c = (s % 2) * 2 + jj
            p = banks.next()
            for k in range(8):
                mm(p.t[:, :hw], wb.t[:, k, jj * 128:(jj + 1) * 128], UT[:, k, h0:h0 + hw],
                   k == 0, k == 7, [wb.b, bUT[k][h]], [p.b])
            if pk == 2:
                xs = tmpf_rot.next()
                cp("act", xs.t[:, :hw], p.t[:, :hw], [p.b], [xs.b])
                dma("sp", XS[c, :, col0 + h0:col0 + h0 + hw], xs.t[:, :hw], reads=[xs.b])
                return
            sg = stg_rot.next()
            if pk == 0:
                if lat:
                    rope_evac(p, hw, h0, sg)
                else:
                    cp("act", sg.t[:, :hw], p.t[:, :hw], [p.b], [sg.b])
                dma("sp", KT[c, :, col0 + h0:col0 + h0 + hw], sg.t[:, :hw], reads=[sg.b])
            elif pk == 3:
                rope_evac(p, hw, h0, sg)
                dma("sp", QT[c, :, t0 + h0:t0 + h0 + hw], sg.t[:, :hw], reads=[sg.b])
            elif pk == 4:
                act(sg.t[:, :hw], p.t[:, :hw], AF.Silu, [p.b], [sg.b])
                dma("sp", GS[c, :, t0 + h0:t0 + h0 + hw], sg.t[:, :hw], reads=[sg.b])
            else:
                act(sg.t[:, :hw], p.t[:, :hw], AF.Gelu, [p.b], [sg.b])
                dma("sp", GG[c, :, t0 + h0:t0 + h0 + hw], sg.t[:, :hw], reads=[sg.b])

        def proj_load_pair(s0):
            wbp = []
            for s in (s0, s0 + 1):
                wb = ws_rot.next()
                dma("pool", wb.t[:], win_v[:, :, s * 256:(s + 1) * 256], writes=[wb.b])
                wbp.append(wb)
            return wbp

        def proj_pair_units(kind, t0, n, s0, wbp, hsel):
            lat = kind == "lat"
            col0 = TC + t0 if lat else 0
            if s0 // 2 == 1:
                for tix in range(n // 128):
                    h = tix // 4
                    if h not in hsel:
                        continue
                    for si, s in enumerate((s0, s0 + 1)):
                        wb = wbp[si]
                        pv = banks.next()
                        for k in range(8):
                            mm(pv.t[:, 0:256], UT[:, k, tix * 128:(tix + 1) * 128], wb.t[:, k, :],
                               k == 0, k == 7, [wb.b, bUT[k][h]], [pv.b])
                        sg = stg_rot.next()
                        cp("act" if (tix + si) % 2 == 0 else "dve", sg.t[:, 0:256], pv.t[:, 0:256], [pv.b], [sg.b])
                        for hh in range(2):
                            dma("sp", VTM[(s - 2) * 2 + hh, :, col0 // 128 + tix, :], sg.t[:, hh * 128:(hh + 1) * 128],
                                reads=[sg.b])
                return
            for (h, h0, hw) in halves(n):
                if h not in hsel:
                    continue
                for si, s in enumerate((s0, s0 + 1)):
                    for jj in range(2):
                        proj_unit(wbp[si], s, jj, h, h0, hw, lat, col0, t0)

        stage_in(*groups[0], stage_in_loads(*groups[0]))
        modulate_in(groups[0][0], groups[0][2])
        for gi, (kind, t0, n) in enumerate(groups):
            lat = kind == "lat"
            if lat:
                dma("sp", RC.t[:], ropec_d[:, t0:t0 + n], writes=[RC.b])
                dma("sp", RS.t[:], ropes_d[:, t0:t0 + n], writes=[RS.b])
            gcol = C_G1L if lat else C_G1C
            nxt = {}

            def prefetch_next(gi=gi, nxt=nxt):
                if gi + 1 < len(groups):
                    nxt["tiles"] = stage_in_loads(*groups[gi + 1])

            ffn_mm1(0, n)
            pair0 = {}

            def filler(kind=kind, t0=t0, n=n, pair0=pair0):
                pair0["wbp"] = proj_load_pair(0)
                proj_pair_units(kind, t0, n, 0, pair0["wbp"], (0,))

            residual_ln(n, lambda h, h0, hw, dc, gcol=gcol: mm2_unit(h, h0, hw, dc, gcol),
                        "ln1" if lat else "ln1c", j=0 if lat else 1, t0=t0, filler=filler if lat else None,
                        after_mm=prefetch_next)
            if lat and t0 == 0:
                dbg("UT", UT[:], [128, 8, NT], BF16, [b for r in bUT for b in r])
                dbg("COLS", COLS[:], [128, 192], F32, [bCOLS])
                dbg("MOD", MOD[:].rearrange("p r j -> p (r j)"), [128, 144], F32, [bMOD])
            if lat:
                proj_pair_units(kind, t0, n, 0, pair0["wbp"], (1,))
            else:
                proj_pair_units(kind, t0, n, 0, proj_load_pair(0), (0,))
            if gi + 1 < len(groups):
                stage_in(*groups[gi + 1], nxt["tiles"])
            for s0 in range(2, 12 if lat else 6, 2):
                proj_pair_units(kind, t0, n, s0, proj_load_pair(s0), (0, 1))
            if gi + 1 < len(groups):
                modulate_in(groups[gi + 1][0], groups[gi + 1][2])
        S.barrier()

        def finish_debug(src_ap):
            z = XIN[0]
            S.add("dve", lambda e: e.memset(z.t[:], 0.0), (), [z.b])
            for r in range(T // 128):
                dma("sp", out_d[r * 128:(r + 1) * 128, :], z.t[:], reads=[z.b], is_output=True)
            S.emit()

        if stop_after == "A":
            finish_debug(None)
            return nc

        off[0] = phase_base
        NCH = TOT // 128
        LOADS = []
        for i in range(2):
            LOADS.append(dict(
                K=TB(sb("KTh%d" % i, [128, TOT], BF16), Buf("KTh%d" % i)),
                Q=TB(sb("QTh%d" % i, [128, T], BF16), Buf("QTh%d" % i)),
                V=TB(sb("Vh%d" % i, [128, NCH, 128], BF16), Buf("Vh%d" % i)),
                G=TB(sb("GSh%d" % i, [128, T], BF16), Buf("GSh%d" % i))))
        KDF = TB(sb("KDF", [128, NCH, 128], BF16), Buf("KDF"))
        KDB = TB(sb("KDB", [128, NCH, 128], BF16), Buf("KDB"))
        VDF = TB(sb("VDF", [128, NCH, 128], BF16), Buf("VDF"))
        NSL = 4
        SFF = sb("SFF", [128, NSL, 128], F32)
        SBF = sb("SBF", [128, NSL, 128], F32)
        bSFF = [Buf("SFF%d" % i) for i in range(NSL)]
        bSBF = [Buf("SBF%d" % i) for i in range(NSL)]
        SFb = TB(sb("SFb", [128, 32, 128], BF16), Buf("SFb"))
        SBb = TB(sb("SBb", [128, 32, 128], BF16), Buf("SBb"))
        QF = TB(sb("QF", [128, T], BF16), Buf("QF"))
        QB = TB(sb("QB", [128, T], BF16), Buf("QB"))

        def mkrot(name, n, shape, dt):
            return Rot([TB(sb("%s%d" % (name, i), shape, dt), Buf("%s%d" % (name, i))) for i in range(n)])

        PT = mkrot("PT", 3, [128, 512], BF16)
        OB = mkrot("OB", 3, [128, 512], BF16)
        OSQ = mkrot("OSQ", 3, [128, 512], BF16)
        OF = mkrot("OF", 9, [128, 512], F32)
        BMEAN = mkrot("BMEAN", 3, [128, 512], F32)
        BRSTD = mkrot("BRSTD", 5, [128, 512], F32)
        BEX2 = mkrot("BEX2", 3, [128, 512], F32)
        RETO = mkrot("RETO", 1, [128, T], BF16)

        def ret_load(h):
            L = LOADS[h % 2]
            dma("sp", L["K"].t[:], KT[h], writes=[L["K"].b])
            dma("sp", L["V"].t[:], VTM[h], writes=[L["V"].b])
            dma("sp", L["Q"].t[:], QT[h], writes=[L["Q"].b])
            dma("sp", L["G"].t[:], GS[h], writes=[L["G"].b])

        ret_load(0)
        for h in range(4):
            if h + 1 < 4:
                ret_load(h + 1)
            L = LOADS[h % 2]
            Kt, Qt, Vt, Gt = L["K"], L["Q"], L["V"], L["G"]
            act(VDF.t[:].rearrange("p n i -> p (n i)"), Vt.t[:].rearrange("p n i -> p (n i)"), AF.Identity,
                [Vt.b, bCOLS], [VDF.b], scale=col(C_KDEC, h))
            act(KDB.t[:].rearrange("p n i -> p (n i)"), Vt.t[:].rearrange("p n i -> p (n i)"), AF.Identity,
                [Vt.b, bCOLS], [KDB.b], scale=col(C_KDEC, 4 + h))
            for n0 in range(0, NCH, 4):
                nn = min(4, NCH - n0)
                pb = banks.next()
                for q in range(nn):
                    mm(pb.t[:, q * 128:(q + 1) * 128], Kt.t[:, (n0 + q) * 128:(n0 + q + 1) * 128], IDB[:],
                       True, True, [Kt.b, bIDB], [pb.b])
                cp("act", KDF.t[:].rearrange("p n i -> p (n i)")[:, n0 * 128:(n0 + nn) * 128],
                   pb.t[:, 0:nn * 128], [pb.b], [KDF.b])
            tt("dve", QF.t[:].rearrange("p (n i) -> p n i", i=128), Qt.t[:].rearrange("p (n i) -> p n i", i=128),
               QDEC[:, h:h + 1, :].broadcast_to([128, 32, 128]), ALU.mult, [Qt.b, bQDEC], [QF.b])
            tt("dve", QB.t[:].rearrange("p (n i) -> p n i", i=128), Qt.t[:].rearrange("p (n i) -> p n i", i=128),
               QDEC[:, 4 + h:5 + h, :].broadcast_to([128, 32, 128]), ALU.mult, [Qt.b, bQDEC], [QB.b])
            border = [1, 0] + [2 + m for m in range(31, 0, -1)]
            for si in range(33):
                for (d, n, ST, bST, VD, Sb) in ((0, si, SFF, bSFF, VDF, SFb), (1, border[si], SBF, bSBF, KDB, SBb)):
                    pc = banks.next()
                    mm(pc.t[:, 0:128], KDF.t[:, n, :], VD.t[:, n, :], True, True, [KDF.b, VD.b], [pc.b])
                    sl, pl_ = si % NSL, (si - 1) % NSL
                    if si == 0:
                        cp("dve", ST[:, sl, :], pc.t[:, 0:128], [pc.b], [bST[sl]])
                    else:
                        stt(ST[:, sl, :], ST[:, pl_, :], col(C_GC, d * 4 + h), pc.t[:, 0:128], ALU.mult, ALU.add,
                            [pc.b, bST[pl_], bCOLS], [bST[sl]])
                        act(Sb.t[:, si - 1, :], ST[:, sl, :], AF.Identity, [bST[sl]], [Sb.b])
            ro = RETO.next()
            NBT = 8
            st_ = [dict() for _ in range(NBT)]

            def T0(bt, X):
                X["ps"] = banks.next()
                for c in range(4):
                    m = bt * 4 + c
                    kc = (2 + m) * 128
                    mm(X["ps"].t[:, c * 128:(c + 1) * 128], Kt.t[:, kc:kc + 128], Qt.t[:, m * 128:(m + 1) * 128],
                       True, True, [Kt.b, Qt.b], [X["ps"].b])

            def T1(bt, X):
                X["pt"] = PT.next()
                tt("dve", X["pt"].t[:].rearrange("p (c i) -> p c i", c=4), X["ps"].t[:].rearrange("p (c i) -> p c i", c=4),
                   DM[:, h:h + 1, :].broadcast_to([128, 4, 128]), ALU.mult, [X["ps"].b, bDM], [X["pt"].b])

            def T2(bt, X):
                X["po"] = banks.next()
                po, pt = X["po"], X["pt"]
                for c in range(4):
                    m = bt * 4 + c
                    oc = po.t[:, c * 128:(c + 1) * 128]
                    mm(oc, Vt.t[:, 2 + m, :], pt.t[:, c * 128:(c + 1) * 128], True, False, [Vt.b, pt.b], [po.b])
                    mm(oc, SFb.t[:, m, :], QF.t[:, m * 128:(m + 1) * 128], False, False, [SFb.b, QF.b], [po.b])
                    mm(oc, SBb.t[:, 31 - m, :], QB.t[:, m * 128:(m + 1) * 128], False, True, [SBb.b, QB.b], [po.b])

            def T3(bt, X):
                X["ob"], X["osq"], X["of"] = OB.next(), OSQ.next(), OF.next()
                po = X["po"]
                cp("act", X["of"].t[:], po.t[:], [po.b], [X["of"].b])
                act(X["osq"].t[:], po.t[:], AF.Square, [po.b], [X["osq"].b])
                cp("pool", X["ob"].t[:], X["of"].t[:], [X["of"].b], [X["ob"].b])

            def T4(bt, X):
                X["pm"], X["pq"] = banks.next(), banks.next()
                mm(X["pm"].t[:], ONES_H[:], X["ob"].t[:], True, True, [bONES, X["ob"].b], [X["pm"].b])
                mm(X["pq"].t[:], ONES_H[:], X["osq"].t[:], True, True, [bONES, X["osq"].b], [X["pq"].b])

            def T5(bt, X):
                X["mean"], X["rstd"], X["ex2"] = BMEAN.next(), BRSTD.next(), BEX2.next()
                act(X["mean"].t[:], X["pm"].t[:], AF.Identity, [X["pm"].b], [X["mean"].b])
                act(X["rstd"].t[:], X["pm"].t[:], AF.Square, [X["pm"].b], [X["rstd"].b])
                act(X["ex2"].t[:], X["pq"].t[:], AF.Identity, [X["pq"].b], [X["ex2"].b])

            def T6(bt, X):
                r = X["rstd"]
                tt("dve", r.t[:], X["ex2"].t[:], r.t[:], ALU.subtract, [X["ex2"].b, r.b], [r.b])
                tt("pool", X["of"].t[:], X["of"].t[:], X["mean"].t[:], ALU.subtract, [X["of"].b, X["mean"].b], [X["of"].b])

            def T7(bt, X):
                r = X["rstd"]
                act(r.t[:], r.t[:], AF.Ln, [r.b, bCOLS], [r.b], bias=col(C_EPSH))
                act(r.t[:], r.t[:], AF.Exp, [r.b], [r.b], scale=-0.5)

            def T8(bt, X):
                r = X["rstd"]
                tt("dve", X["of"].t[:], X["of"].t[:], r.t[:], ALU.mult, [X["of"].b, r.b], [X["of"].b])

            def T9(bt, X):
                o_ = X["of"]
                ts("dve", o_.t[:], o_.t[:], VEC[:, R_RNG + h:R_RNG + h + 1], VEC[:, R_RNB + h:R_RNB + h + 1],
                   ALU.mult, ALU.add, [o_.b, bVEC], [o_.b])

            def T10(bt, X):
                tt("pool", ro.t[:, bt * 512:(bt + 1) * 512], X["of"].t[:], Gt.t[:, bt * 512:(bt + 1) * 512], ALU.mult,
                   [X["of"].b, Gt.b], [ro.b])

            stages = [T0, T1, T2, T3, T4, T5, T6, T7, T8, T9, T10]
            for k in range(NBT + len(stages) - 1):
                for s_i, fn in enumerate(stages):
                    bt = k - s_i
                    if 0 <= bt < NBT:
                        fn(bt, st_[bt])
            dma("sp", CAT[h], ro.t[:], reads=[ro.b])
        S.barrier()

        if stop_after == "B1":
            finish_debug(None)
            return nc
        off[0] = phase_base
        LX = TC + 3 + T
        LP = LX + 5
        XP = TB(sb("XP", [128, LP], F32), Buf("XP"))
        XCB = TB(sb("XCB", [128, LP], BF16), Buf("XCB"))
        RAd = [TB(sb("RA%d" % d, [128, LP], F32), Buf("RA%d" % d)) for d in range(2)]
        IBd = [TB(sb("IB%d" % d, [128, LP], BF16), Buf("IB%d" % d)) for d in range(2)]
        SQd = [TB(sb("SQ%d" % d, [128, LP], BF16), Buf("SQ%d" % d)) for d in range(2)]
        TMd = [TB(sb("TM%d" % d, [128, LP], F32), Buf("TM%d" % d)) for d in range(2)]
        GGc = TB(sb("GGc", [128, T], BF16), Buf("GGc"))
        OUTc = TB(sb("OUTc", [128, T], BF16), Buf("OUTc"))
        DG = sb("DG", [128, 16, 128], F32)
        S.strict = True
        for k in range(16):
            ts("dve", DG[:, k, :], IDF, VEC[:, R_CW + k:R_CW + k + 1], None, ALU.mult, None, [bCST, bVEC], [bDG])
        S.strict = False
        C0, L0 = 0, TC + 3
        S.add("pool", lambda e: e.memset(XP.t[:], 0.0), (), [XP.b])
        def lru_load_x(cb):
            dma("sp", XP.t[:, 2:2 + TC], XS[cb, :, 0:TC], writes=[XP.b])
            dma("sp", XP.t[:, 5 + TC:5 + TC + T], XS[cb, :, TC:TOT], writes=[XP.b])

        lru_load_x(0)
        dma("sp", GGc.t[:], GG[0], writes=[GGc.b])
        for cb in range(4):
            for c0 in range(0, LX, 512):
                w = min(512, LX - c0)
                pcv = banks.next()
                for jt in range(4):
                    mm(pcv.t[:, :w], DG[:, jt * 4 + cb, :], XP.t[:, c0 + jt:c0 + jt + w], jt == 0, jt == 3,
                       [bDG, XP.b], [pcv.b])
                act(XCB.t[:, c0:c0 + w], pcv.t[:, :w], AF.Identity, [pcv.b, bVEC], [XCB.b], bias=VEC[:, R_CB + cb:R_CB + cb + 1])
            if cb + 1 < 4:
                lru_load_x(cb + 1)
            for c0 in range(0, LX, 512):
                w = min(512, LX - c0)
                for d in range(2):
                    pr = banks.next()
                    mm(pr.t[:, :w], WBD[:, (0 * 2 + d) * 4 + cb, :], XCB.t[:, c0:c0 + w], True, True, [bWBD, XCB.b], [pr.b])
                    pi = banks.next()
                    mm(pi.t[:, :w], WBD[:, (1 * 2 + d) * 4 + cb, :], XCB.t[:, c0:c0 + w], True, True, [bWBD, XCB.b], [pi.b])
                    ci = d * 4 + cb
                    act(RAd[d].t[:, c0:c0 + w], pr.t[:, :w], AF.Sigmoid, [pr.b, bVEC], [RAd[d].b], bias=VEC[:, R_BA + ci:R_BA + ci + 1])
                    act(IBd[d].t[:, c0:c0 + w], pi.t[:, :w], AF.Sigmoid, [pi.b, bVEC], [IBd[d].b], bias=VEC[:, R_BI + ci:R_BI + ci + 1])
            for d in range(2):
                ci = d * 4 + cb
                act(RAd[d].t[:, 0:LX], RAd[d].t[:, 0:LX], AF.Exp, [RAd[d].b, bCOLS], [RAd[d].b], scale=col(C_CP, ci))
                tt("dve", TMd[d].t[:, 0:LX], RAd[d].t[:, 0:LX], RAd[d].t[:, 0:LX], ALU.mult, [RAd[d].b], [TMd[d].b])
            for d in range(2):
                act(SQd[d].t[:, 0:LX], TMd[d].t[:, 0:LX], AF.Sqrt, [TMd[d].b, bCOLS], [SQd[d].b], scale=-1.0, bias=col(C_ONE))
            for d in range(2):
                tt(MULENG, IBd[d].t[:, 0:LX], IBd[d].t[:, 0:LX], SQd[d].t[:, 0:LX], ALU.mult, [IBd[d].b, SQd[d].b], [IBd[d].b])
                tt(MULENG, IBd[d].t[:, 0:LX], IBd[d].t[:, 0:LX], XCB.t[:, 0:LX], ALU.mult, [IBd[d].b, XCB.b], [IBd[d].b])
            HF, HB = TMd[0], TMd[1]
            S.strict = True
            S.add("dve", lambda e: e.tensor_tensor_scan(out=HF.t[:, C0:C0 + TC], data0=RAd[0].t[:, C0:C0 + TC],
                  data1=IBd[0].t[:, C0:C0 + TC], initial=0.0, op0=ALU.mult, op1=ALU.add), [RAd[0].b, IBd[0].b], [HF.b])
            S.add("dve", lambda e: e.tensor_tensor_scan(out=HB.t[:, C0:C0 + TC][:, ::-1],
                  data0=RAd[1].t[:, C0:C0 + TC][:, ::-1], data1=IBd[1].t[:, C0:C0 + TC][:, ::-1], initial=0.0,
                  op0=ALU.mult, op1=ALU.add), [RAd[1].b, IBd[1].b], [HB.b])
            S.add("dve", lambda e: e.tensor_tensor_scan(out=HF.t[:, L0:L0 + T], data0=RAd[0].t[:, L0:L0 + T],
                  data1=IBd[0].t[:, L0:L0 + T], initial=HF.t[:, C0 + TC - 1:C0 + TC], op0=ALU.mult, op1=ALU.add),
                  [RAd[0].b, IBd[0].b, HF.b], [HF.b])
            S.add("dve", lambda e: e.tensor_tensor_scan(out=HB.t[:, L0:L0 + T][:, ::-1],
                  data0=RAd[1].t[:, L0:L0 + T][:, ::-1], data1=IBd[1].t[:, L0:L0 + T][:, ::-1],
                  initial=HB.t[:, C0:C0 + 1], op0=ALU.mult, op1=ALU.add), [RAd[1].b, IBd[1].b, HB.b], [HB.b])
            tt("dve", HF.t[:, L0:L0 + T], HF.t[:, L0:L0 + T], HB.t[:, L0:L0 + T], ALU.add, [HF.b, HB.b], [HF.b])
            S.strict = False
            tt(MULENG, OUTc.t[:], HF.t[:, L0:L0 + T], GGc.t[:], ALU.mult, [HF.b, GGc.b], [OUTc.b])
            if cb + 1 < 4:
                dma("sp", GGc.t[:], GG[cb + 1], writes=[GGc.b])
            dma("sp", CAT[4 + cb], OUTc.t[:], reads=[OUTc.b])
        S.barrier()
        if stop_after == "B":
            finish_debug(None)
            return nc

        wout_v = wout_d.rearrange("(kh kl) c -> kl kh c", kl=128)
        for g in range(T // NT):
            t0 = g * NT
            n = NT
            hv = halves(n)
            for kc in range(8):
                for (h, h0, hw) in hv:
                    dma("sp", HT[:, kc, h0:h0 + hw], CAT[kc, :, t0 + h0:t0 + h0 + hw], writes=[bHT[kc][h]])
            for dc in range(8):
                dma("sp", XT[:, dc, :], X1[dc * 128:(dc + 1) * 128, t0:t0 + n], writes=bXT[dc])
            wbs = []
            for s in range(4):
                wb = ws_rot.next()
                dma("pool", wb.t[:], wout_v[:, :, s * 256:(s + 1) * 256], writes=[wb.b])
                wbs.append(wb)
            def wout_unit(h, h0, hw, dc, wbs=wbs):
                wb, jj = wbs[dc // 2], dc % 2
                p = banks.next()
                for kc in range(8):
                    mm(p.t[:, :hw], wb.t[:, kc, jj * 128:(jj + 1) * 128], HT[:, kc, h0:h0 + hw],
                       kc == 0, kc == 7, [wb.b, bHT[kc][h]], [p.b])
                xb = xtb(dc, h0, hw)
                stt(XT[:, dc, h0:h0 + hw], p.t[:, :hw], col(C_G2, dc), XT[:, dc, h0:h0 + hw],
                    ALU.mult, ALU.add, [p.b, bCOLS] + xb, xb)

            early = {}

            def filler2(n=n, early=early):
                early.update(mm1_early(1, n, slabs=(0, 1)))

            residual_ln(n, wout_unit, "ln2", filler=filler2)
            ffn_mm1(1, n, early=early)

            def out_tiles(tixs, t0=t0):
                for tix in tixs:
                    if tix < 2:
                        ob_t, ob_b = XIN[tix].t[:], [XIN[tix].b]
                    else:
                        ob_t, ob_b = HSTG[tix - 2]
                    for hb in range(2):
                        pb = banks.next()
                        for q in range(4):
                            dc = hb * 4 + q
                            tr(pb.t[:, q * 128:(q + 1) * 128], XT[:, dc, tix * 128:(tix + 1) * 128], IDF,
                               [bXT[dc][tix], bCST], [pb.b])
                        cp("act" if hb == 0 else "dve", ob_t[:, hb * 512:(hb + 1) * 512], pb.t[:], [pb.b], ob_b)
                    dma("sp", out_d[t0 + tix * 128:t0 + (tix + 1) * 128, :], ob_t, reads=ob_b, is_output=True)

            residual_ln(n, lambda h, h0, hw, dc: mm2_unit(h, h0, hw, dc, C_G3), "ln3",
                        filler=lambda: out_tiles(range(0, 4)))
            out_tiles(range(4, 8))
        S.emit()
    return nc


def _consts():
    p = np.arange(128, dtype=np.float32)
    ident = np.eye(128, dtype=np.float32)
    diff = p[None, :] - p[:, None]
    diffpos = np.maximum(diff, 0.0)
    diffneg = np.maximum(-diff, 0.0)
    irow = np.concatenate([np.tile((p + 1.0)[None, :], (128, 1)), np.tile((128.0 - p)[None, :], (128, 1))], 1)
    pcolx = np.concatenate([np.tile((127.0 - p)[:, None], (1, 4)), np.tile(p[:, None], (1, 4))], 1)
    cst = np.concatenate([ident, diffpos, diffneg, irow, pcolx], 1).astype(np.float32)
    n = 32
    inv = (np.float32(10000.0) ** (-np.arange(n, dtype=np.float32) / np.float32(n))).astype(np.float32)
    t = np.arange(T)
    rows = (t // 64).astype(np.float32)
    cols = (t % 64).astype(np.float32)
    d = np.arange(128)
    f = d % 32
    pos = np.where((d // 64)[:, None] == 0, rows[None, :], cols[None, :]).astype(np.float32)
    ang = (pos * inv[f][:, None]).astype(np.float32)
    c = np.cos(ang).astype(np.float32)
    s = np.sin(ang).astype(np.float32)
    half = (d % 64) // 32
    ss = np.where(half[:, None] == 0, -s, s).astype(np.float32)
    return cst, c.astype(ml_dtypes.bfloat16), ss.astype(ml_dtypes.bfloat16)


def _small(b, c, c_ctx, b_ada, ln_g, ln_b, ret_norm_g, ret_norm_b, lru_conv_w, lru_conv_b, lru_b_a, lru_b_i, lru_lambda):
    sm = np.zeros((256, 128), np.float32)
    sm[R_BADA:R_BADA + 72] = b_ada[0].reshape(72, 128)
    sm[R_C:R_C + 8] = c[b].reshape(8, 128)
    sm[R_CCTX:R_CCTX + 8] = c_ctx.reshape(8, 128)
    sm[R_LNG:R_LNG + 24] = ln_g[0].reshape(24, 128)
    sm[R_LNB:R_LNB + 24] = ln_b[0].reshape(24, 128)
    sm[R_RNG:R_RNG + 4] = ret_norm_g[0].reshape(4, 128)
    sm[R_RNB:R_RNB + 4] = ret_norm_b[0].reshape(4, 128)
    sm[R_CW:R_CW + 16] = lru_conv_w[0].reshape(16, 128)
    sm[R_CB:R_CB + 4] = lru_conv_b[0].reshape(4, 128)
    sm[R_BA:R_BA + 8] = lru_b_a[0].reshape(8, 128)
    sm[R_BI:R_BI + 8] = lru_b_i[0].reshape(8, 128)
    sm[R_LAM:R_LAM + 8] = lru_lambda[0].reshape(8, 128)
    return sm


def make_in_maps(x, c, ctx, c_ctx, w_ada, b_ada, ffn1_w_gate, ffn1_w_up, ffn1_w_down,
                 ffn2_w_gate, ffn2_w_up, ffn2_w_down, w_in, w_out, ret_decay_logit,
                 ret_norm_g, ret_norm_b, lru_conv_w, lru_conv_b, lru_w_a, lru_b_a,
                 lru_w_i, lru_b_i, lru_lambda, ln_g, ln_b):
    f = lambda a: np.ascontiguousarray(np.asarray(a, dtype=np.float32))
    cst, rc, rs = _consts()
    shared = {
        "w_ada": f(w_ada[0]), "wg1": f(ffn1_w_gate[0]), "wu1": f(ffn1_w_up[0]), "wd1": f(ffn1_w_down[0]),
        "wg2": f(ffn2_w_gate[0]), "wu2": f(ffn2_w_up[0]), "wd2": f(ffn2_w_down[0]),
        "w_in": f(w_in[0]), "w_out": f(w_out[0]),
        "lru_wa": f(np.asarray(lru_w_a[0]).reshape(16, 64, 64)), "lru_wi": f(np.asarray(lru_w_i[0]).reshape(16, 64, 64)),
        "logit": f(np.asarray(ret_decay_logit[0]).reshape(1, 8)),
        "cst": cst, "rope_c": rc, "rope_s": rs,
    }
    args = [np.asarray(a) for a in (c, c_ctx, b_ada, ln_g, ln_b, ret_norm_g, ret_norm_b, lru_conv_w, lru_conv_b,
                                    lru_b_a, lru_b_i, lru_lambda)]
    in_maps = []
    for b in range(8):
        m = dict(shared)
        m["x"] = f(x[b])
        m["ctx"] = f(ctx[b])
        m["small"] = _small(b, args[0], args[1], *args[2:])
        in_maps.append(m)
    return in_maps


def kernel(**inputs):
    in_maps = make_in_maps(**inputs)
    nc = build_nc()
    res = run_bass_kernel_spmd(nc, in_maps, core_ids=list(range(8)))
    return np.stack([np.asarray(r["out"], dtype=np.float32) for r in res.results], 0)
```
